# Optimizing a Trainium2 kernel written in Bass

```python
import math
import jax
import jax.numpy as jnp
from jax import lax
import numpy as np

D_MODEL = 1024
BATCH = 4
SEQ = 4096
DEPTH = 4
DEC_BATCH = 32
DEC_SEQ = 1
PAST_LEN = 8192
PAGE_SIZE = 128

N_MIXERS = 3
N_META = 16
ALPHA = (2 * DEPTH) ** 0.25
BETA_INIT = (8 * DEPTH) ** -0.25
D_FF = 2816
LN_EPS = 1e-5
RMS_EPS = 1e-6

GDN_HEADS = 8
GDN_DK = D_MODEL // GDN_HEADS
GDN_DV = D_MODEL // GDN_HEADS
GDN_CONV = 4
GDN_CHUNK = 64
GDN_QKV = GDN_HEADS * (2 * GDN_DK + GDN_DV)
GDN_Z = GDN_HEADS * GDN_DV
GDN_IN = GDN_QKV + GDN_Z + 2 * GDN_HEADS

SB_HEADS = 8
SB_DH = D_MODEL // SB_HEADS
SB_BLOCK = 128
SB_BIAS_INIT = -6.0

CONV_WIDTH = 31

N_A = (DEPTH + 2) // N_MIXERS
N_B = (DEPTH + 1) // N_MIXERS
N_C = DEPTH // N_MIXERS

kernel_name = 'hybrid_gdn_stickbreak_conformer_step'


def layer_norm(x, g, b):
    xf = x.astype(jnp.float32)
    mu = jnp.mean(xf, -1, keepdims=True)
    var = jnp.mean(jnp.square(xf - mu), -1, keepdims=True)
    y = (xf - mu) * lax.rsqrt(var + LN_EPS) * g.astype(jnp.float32) + b.astype(jnp.float32)
    return y.astype(x.dtype)


def residual_norm(h, delta, g, b):
    return layer_norm(ALPHA * h + delta, g, b)


def swiglu(x, w_gate, w_up, w_down):
    return (jax.nn.silu(x @ w_gate) * (x @ w_up)) @ w_down


def causal_dwconv(xp, w):
    return lax.conv_general_dilated(xp, w[:, None, :].astype(xp.dtype), window_strides=(1,), padding='VALID',
                                    dimension_numbers=('NWC', 'WIO', 'NWC'), feature_group_count=xp.shape[-1])


def l2norm(x):
    return x * lax.rsqrt(jnp.sum(x * x, -1, keepdims=True) + RMS_EPS)


def gdn_inputs(x, conv_ctx, w_in, conv_w, a_log, dt_bias):
    n, t, _ = x.shape
    h = x @ w_in
    qkv_pre = h[..., :GDN_QKV]
    z = h[..., GDN_QKV:GDN_QKV + GDN_Z].reshape(n, t, GDN_HEADS, GDN_DV)
    b_logit = h[..., GDN_QKV + GDN_Z:GDN_QKV + GDN_Z + GDN_HEADS].astype(jnp.float32)
    a_logit = h[..., GDN_QKV + GDN_Z + GDN_HEADS:].astype(jnp.float32)
    seq = jnp.concatenate([conv_ctx.astype(h.dtype), qkv_pre], axis=1)
    qkv = jax.nn.silu(causal_dwconv(seq, conv_w)).astype(jnp.float32)
    new_ctx = seq[:, -(GDN_CONV - 1):]
    hk = GDN_HEADS * GDN_DK
    q = l2norm(qkv[..., :hk].reshape(n, t, GDN_HEADS, GDN_DK)) * GDN_DK ** -0.5
    k = l2norm(qkv[..., hk:2 * hk].reshape(n, t, GDN_HEADS, GDN_DK))
    v = qkv[..., 2 * hk:].reshape(n, t, GDN_HEADS, GDN_DV)
    beta = jax.nn.sigmoid(b_logit)
    g = -jnp.exp(a_log.astype(jnp.float32)) * jax.nn.softplus(a_logit + dt_bias.astype(jnp.float32))
    sw = lambda u: jnp.swapaxes(u, 1, 2)
    return sw(q), sw(k), sw(v), sw(g), sw(beta), z, new_ctx


def gdn_chunk(S, q, k, v, g, beta):
    c = q.shape[2]
    gc = jnp.cumsum(g, axis=-1)
    lower = jnp.tril(jnp.ones((c, c), bool))
    decay = jnp.exp(jnp.where(lower, gc[..., :, None] - gc[..., None, :], -jnp.inf))
    kb = k * beta[..., None]
    a_mat = jnp.einsum('nhid,nhjd->nhij', kb, k) * decay
    rhs = jnp.concatenate([v * beta[..., None], kb * jnp.exp(gc)[..., None]], axis=-1)
    sol = lax.linalg.triangular_solve(a_mat, rhs, left_side=True, lower=True, unit_diagonal=True)
    u, w = sol[..., :GDN_DV], sol[..., GDN_DV:]
    v_new = u - jnp.einsum('nhik,nhkv->nhiv', w, S)
    qk = jnp.einsum('nhid,nhjd->nhij', q, k) * decay
    o = jnp.einsum('nhik,nhkv->nhiv', q * jnp.exp(gc)[..., None], S) + jnp.einsum('nhij,nhjv->nhiv', qk, v_new)
    g_last = gc[..., -1]
    S = S * jnp.exp(g_last)[..., None, None] + jnp.einsum(
        'nhik,nhiv->nhkv', k * jnp.exp(g_last[..., None] - gc)[..., None], v_new)
    return S, o


def gdn_prompt_core(q, k, v, g, beta):
    n, hh, t, _ = q.shape
    S = jnp.zeros((n, hh, GDN_DK, GDN_DV), jnp.float32)
    S, o_meta = gdn_chunk(S, q[:, :, :N_META], k[:, :, :N_META], v[:, :, :N_META], g[:, :, :N_META], beta[:, :, :N_META])
    nc = (t - N_META) // GDN_CHUNK

    def split(u):
        u = u[:, :, N_META:]
        return jnp.moveaxis(u.reshape(n, hh, nc, GDN_CHUNK, *u.shape[3:]), 2, 0)

    S, o_rest = lax.scan(lambda s, c: gdn_chunk(s, *c), S, tuple(split(u) for u in (q, k, v, g, beta)))
    o_rest = jnp.moveaxis(o_rest, 0, 2).reshape(n, hh, nc * GDN_CHUNK, GDN_DV)
    return jnp.concatenate([o_meta, o_rest], axis=2), S


def gdn_sample_core(S0, q, k, v, g, beta):
    def step(S, c):
        q_t, k_t, v_t, g_t, b_t = c
        S = S * jnp.exp(g_t)[..., None, None]
        delta = (v_t - jnp.einsum('nhk,nhkv->nhv', k_t, S)) * b_t[..., None]
        S = S + jnp.einsum('nhk,nhv->nhkv', k_t, delta)
        return S, jnp.einsum('nhk,nhkv->nhv', q_t, S)

    xs = tuple(jnp.moveaxis(u, 2, 0) for u in (q, k, v, g, beta))
    S, o = lax.scan(step, S0.astype(jnp.float32), xs)
    return jnp.moveaxis(o, 0, 2), S


def gdn_output(o, z, norm_w, w_out):
    o = jnp.swapaxes(o, 1, 2)
    o = o * lax.rsqrt(jnp.mean(o * o, -1, keepdims=True) + RMS_EPS) * norm_w.astype(jnp.float32)
    o = (o * jax.nn.silu(z.astype(jnp.float32))).astype(z.dtype)
    n, t = o.shape[:2]
    return o.reshape(n, t, GDN_Z) @ w_out


def sb_heads(x, w_qkv):
    n, t, _ = x.shape
    qkv = (x @ w_qkv).reshape(n, t, 3, SB_HEADS, SB_DH)
    return qkv[:, :, 0], qkv[:, :, 1], qkv[:, :, 2]


def sb_block(q, tq, k, v, tk, bias):
    z = jnp.einsum('nhqd,nhkd->nhqk', q, k, preferred_element_type=jnp.float32) * SB_DH ** -0.5
    z = z + bias.astype(jnp.float32)[None, :, None, None]
    vis = tk[None, :] < tq[:, None]
    log_fail = jnp.where(vis, jax.nn.log_sigmoid(-z), 0.0)
    after = lax.cumsum(log_fail, axis=3, reverse=True) - log_fail
    w = jnp.where(vis, jnp.exp(jax.nn.log_sigmoid(z) + after), 0.0)
    return jnp.einsum('nhqk,nhkd->nhqd', w.astype(v.dtype), v)


def sb_prompt(x, w_qkv, w_out, bias):
    n, t, _ = x.shape
    q, k, v = sb_heads(x, w_qkv)
    qh, kh, vh = jnp.swapaxes(q, 1, 2), jnp.swapaxes(k, 1, 2), jnp.swapaxes(v, 1, 2)
    pos = jnp.arange(t)
    o_meta = sb_block(qh[:, :, :N_META], pos[:N_META], kh[:, :, :N_META], vh[:, :, :N_META], pos[:N_META], bias)
    nb = (t - N_META) // SB_BLOCK
    q_blocks = jnp.moveaxis(qh[:, :, N_META:].reshape(n, SB_HEADS, nb, SB_BLOCK, SB_DH), 2, 0)
    p_blocks = pos[N_META:].reshape(nb, SB_BLOCK)
    o_rest = lax.map(lambda qp: sb_block(qp[0], qp[1], kh, vh, pos, bias), (q_blocks, p_blocks))
    o_rest = jnp.moveaxis(o_rest, 0, 2).reshape(n, SB_HEADS, t - N_META, SB_DH)
    o = jnp.swapaxes(jnp.concatenate([o_meta, o_rest], axis=2), 1, 2).reshape(n, t, D_MODEL)
    return o @ w_out, k, v


def sb_sample(x, cache_k, cache_v, page_table, w_qkv, w_out, bias):
    n, t, _ = x.shape
    q, k, v = sb_heads(x, w_qkv)
    past = page_table.shape[1] * PAGE_SIZE
    k_past = cache_k[page_table].reshape(n, past, SB_HEADS, SB_DH).astype(k.dtype)
    v_past = cache_v[page_table].reshape(n, past, SB_HEADS, SB_DH).astype(v.dtype)
    k_all = jnp.swapaxes(jnp.concatenate([k_past, k], axis=1), 1, 2)
    v_all = jnp.swapaxes(jnp.concatenate([v_past, v], axis=1), 1, 2)
    o = sb_block(jnp.swapaxes(q, 1, 2), past + jnp.arange(t), k_all, v_all, jnp.arange(past + t), bias)
    return jnp.swapaxes(o, 1, 2).reshape(n, t, D_MODEL) @ w_out, k, v


def conformer_conv(x, ctx, w_pw1, b_pw1, dw_w, dw_b, ln_g, ln_b, w_pw2, b_pw2):
    h = x @ w_pw1 + b_pw1
    u = h[..., :D_MODEL] * jax.nn.sigmoid(h[..., D_MODEL:])
    seq = jnp.concatenate([ctx.astype(u.dtype), u], axis=1)
    d = causal_dwconv(seq, dw_w) + dw_b
    d = jax.nn.silu(layer_norm(d, ln_g, ln_b))
    return d @ w_pw2 + b_pw2, seq[:, -(CONV_WIDTH - 1):]


def setup_inputs(seed: int = 0) -> dict:
    key = jax.random.key(seed)
    keys = jax.random.split(key, 40)
    it = iter(range(40))

    def nrm(shape, scale):
        return jax.random.normal(keys[next(it)], shape, jnp.float32) * scale

    n_pages = PAST_LEN // PAGE_SIZE
    n_phys = (5 * DEC_BATCH * n_pages) // 4
    dt = jnp.exp(jax.random.uniform(keys[next(it)], (N_A, GDN_HEADS), jnp.float32, math.log(1e-3), math.log(1e-1)))
    a_log = jnp.log(jax.random.uniform(keys[next(it)], (N_A, GDN_HEADS), jnp.float32, 1.0, 16.0))
    page_table = jax.random.permutation(keys[next(it)], n_phys)[:DEC_BATCH * n_pages]
    page_table = page_table.reshape(DEC_BATCH, n_pages).astype(jnp.int32)
    d_in = D_MODEL ** -0.5
    return {
        'x_prompt': nrm((BATCH, SEQ, D_MODEL), 1.0),
        'x_sample': nrm((DEC_BATCH, DEC_SEQ, D_MODEL), 1.0),
        'state_gdn_conv': nrm((N_A, DEC_BATCH, GDN_CONV - 1, GDN_QKV), 1.0),
        'state_gdn_S': nrm((N_A, DEC_BATCH, GDN_HEADS, GDN_DK, GDN_DV), 0.3),
        'cache_sb_k': nrm((N_B, n_phys, PAGE_SIZE, SB_HEADS, SB_DH), 1.0),
        'cache_sb_v': nrm((N_B, n_phys, PAGE_SIZE, SB_HEADS, SB_DH), 1.0),
        'state_conv': nrm((N_C, DEC_BATCH, CONV_WIDTH - 1, D_MODEL), 0.5),
        'page_table': page_table,
        'meta_tokens': nrm((N_META, D_MODEL), 1.0),
        'ln_g': 1.0 + nrm((DEPTH, 3, D_MODEL), 0.01),
        'ln_b': nrm((DEPTH, 3, D_MODEL), 0.01),
        'ffn_w_gate': nrm((DEPTH, 2, D_MODEL, D_FF), d_in),
        'ffn_w_up': nrm((DEPTH, 2, D_MODEL, D_FF), d_in),
        'ffn_w_down': nrm((DEPTH, 2, D_FF, D_MODEL), D_FF ** -0.5 * BETA_INIT),
        'gdn_w_in': nrm((N_A, D_MODEL, GDN_IN), d_in),
        'gdn_conv_w': nrm((N_A, GDN_CONV, GDN_QKV), GDN_CONV ** -0.5),
        'gdn_a_log': a_log,
        'gdn_dt_bias': dt + jnp.log(-jnp.expm1(-dt)),
        'gdn_norm_w': 1.0 + nrm((N_A, GDN_DV), 0.01),
        'gdn_w_out': nrm((N_A, GDN_Z, D_MODEL), GDN_Z ** -0.5 * BETA_INIT),
        'sb_w_qkv': nrm((N_B, D_MODEL, 3 * D_MODEL), d_in),
        'sb_w_out': nrm((N_B, D_MODEL, D_MODEL), d_in * BETA_INIT),
        'sb_logit_bias': SB_BIAS_INIT + nrm((N_B, SB_HEADS), 0.1),
        'cv_w_pw1': nrm((N_C, D_MODEL, 2 * D_MODEL), d_in),
        'cv_b_pw1': nrm((N_C, 2 * D_MODEL), 0.01),
        'cv_dw_w': nrm((N_C, CONV_WIDTH, D_MODEL), CONV_WIDTH ** -0.5),
        'cv_dw_b': nrm((N_C, D_MODEL), 0.01),
        'cv_ln_g': 1.0 + nrm((N_C, D_MODEL), 0.01),
        'cv_ln_b': nrm((N_C, D_MODEL), 0.01),
        'cv_w_pw2': nrm((N_C, D_MODEL, D_MODEL), d_in * BETA_INIT),
        'cv_b_pw2': nrm((N_C, D_MODEL), 0.01),
    }


def reference(x_prompt, x_sample, state_gdn_conv, state_gdn_S, cache_sb_k, cache_sb_v, state_conv, page_table,
              meta_tokens, ln_g, ln_b, ffn_w_gate, ffn_w_up, ffn_w_down,
              gdn_w_in, gdn_conv_w, gdn_a_log, gdn_dt_bias, gdn_norm_w, gdn_w_out,
              sb_w_qkv, sb_w_out, sb_logit_bias,
              cv_w_pw1, cv_b_pw1, cv_dw_w, cv_dw_b, cv_ln_g, cv_ln_b, cv_w_pw2, cv_b_pw2):
    n_p = x_prompt.shape[0]
    meta = jnp.broadcast_to(meta_tokens.astype(x_prompt.dtype)[None], (n_p, N_META, D_MODEL))
    hp = jnp.concatenate([meta, x_prompt], axis=1)
    hs = x_sample
    gdn_s_p, gdn_s_s, gdn_c_p, gdn_c_s = [], [], [], []
    sb_k_p, sb_v_p, sb_k_s, sb_v_s = [], [], [], []
    cv_p, cv_s = [], []

    for i in range(DEPTH):
        j, kind = i // N_MIXERS, i % N_MIXERS
        hp = residual_norm(hp, 0.5 * swiglu(hp, ffn_w_gate[i, 0], ffn_w_up[i, 0], ffn_w_down[i, 0]), ln_g[i, 0], ln_b[i, 0])
        hs = residual_norm(hs, 0.5 * swiglu(hs, ffn_w_gate[i, 0], ffn_w_up[i, 0], ffn_w_down[i, 0]), ln_g[i, 0], ln_b[i, 0])
        if kind == 0:
            pin = (gdn_w_in[j], gdn_conv_w[j], gdn_a_log[j], gdn_dt_bias[j])
            q, k, v, g, beta, z, ctx = gdn_inputs(hp, jnp.zeros((n_p, GDN_CONV - 1, GDN_QKV), hp.dtype), *pin)
            o, s_fin = gdn_prompt_core(q, k, v, g, beta)
            mix_p = gdn_output(o, z, gdn_norm_w[j], gdn_w_out[j])
            gdn_s_p.append(s_fin.astype(hp.dtype))
            gdn_c_p.append(ctx)
            q, k, v, g, beta, z, ctx = gdn_inputs(hs, state_gdn_conv[j], *pin)
            o, s_fin = gdn_sample_core(state_gdn_S[j], q, k, v, g, beta)
            mix_s = gdn_output(o, z, gdn_norm_w[j], gdn_w_out[j])
            gdn_s_s.append(s_fin.astype(state_gdn_S.dtype))
            gdn_c_s.append(ctx.astype(state_gdn_conv.dtype))
        elif kind == 1:
            mix_p, k, v = sb_prompt(hp, sb_w_qkv[j], sb_w_out[j], sb_logit_bias[j])
            sb_k_p.append(k)
            sb_v_p.append(v)
            mix_s, k, v = sb_sample(hs, cache_sb_k[j], cache_sb_v[j], page_table, sb_w_qkv[j], sb_w_out[j],
                                    sb_logit_bias[j])
            sb_k_s.append(k.astype(cache_sb_k.dtype))
            sb_v_s.append(v.astype(cache_sb_v.dtype))
        else:
            pc = (cv_w_pw1[j], cv_b_pw1[j], cv_dw_w[j], cv_dw_b[j], cv_ln_g[j], cv_ln_b[j], cv_w_pw2[j], cv_b_pw2[j])
            mix_p, ctx = conformer_conv(hp, jnp.zeros((n_p, CONV_WIDTH - 1, D_MODEL), hp.dtype), *pc)
            cv_p.append(ctx)
            mix_s, ctx = conformer_conv(hs, state_conv[j], *pc)
            cv_s.append(ctx.astype(state_conv.dtype))
        hp = residual_norm(hp, mix_p, ln_g[i, 1], ln_b[i, 1])
        hs = residual_norm(hs, mix_s, ln_g[i, 1], ln_b[i, 1])
        hp = residual_norm(hp, 0.5 * swiglu(hp, ffn_w_gate[i, 1], ffn_w_up[i, 1], ffn_w_down[i, 1]), ln_g[i, 2], ln_b[i, 2])
        hs = residual_norm(hs, 0.5 * swiglu(hs, ffn_w_gate[i, 1], ffn_w_up[i, 1], ffn_w_down[i, 1]), ln_g[i, 2], ln_b[i, 2])

    y_prompt = hp[:, N_META:]
    y_sample = hs
    return (y_prompt, y_sample,
            jnp.stack(gdn_s_p), jnp.stack(gdn_s_s), jnp.stack(gdn_c_p), jnp.stack(gdn_c_s),
            jnp.stack(sb_k_p), jnp.stack(sb_v_p), jnp.stack(sb_k_s), jnp.stack(sb_v_s),
            jnp.stack(cv_p), jnp.stack(cv_s))
```

```python
import contextlib
import os
import numpy as np
import concourse.bass as bass
import concourse.mybir as mybir
from concourse.bass_utils import run_bass_kernel_spmd

F32 = mybir.dt.float32
BF16 = mybir.dt.bfloat16
I32 = mybir.dt.int32
AF = mybir.ActivationFunctionType
ALU = mybir.AluOpType
AX = mybir.AxisListType

ENGS = ("pe", "act", "dve", "pool", "sp")
N_LANES = 8


class Op:
    __slots__ = ("eng", "fn", "deps", "dma", "idx", "sig", "lane", "lane_val", "need_sig")

    def __init__(self, eng, fn, dma):
        self.eng = eng
        self.fn = fn
        self.dma = dma
        self.deps = set()
        self.idx = -1
        self.sig = 0
        self.lane = None
        self.lane_val = 0
        self.need_sig = False


class Sched:
    def __init__(self, same_engine_sync=True):
        self.ops = []
        self.lastw = {}
        self.readers = {}
        self.same_engine_sync = same_engine_sync
        self.last_by_eng = {e: None for e in ENGS}
        self.lane_last = {}
        self.lane_rr = {e: 0 for e in ENGS}
        self.lane_cnt = {}
        self.bar_deps = set()
        self.bar_seen = {e: True for e in ENGS}

    def add(self, eng, fn, r=(), w=(), dma=False):
        op = Op(eng, fn, dma)
        deps = op.deps
        for k in r:
            lw = self.lastw.get(k)
            if lw is not None:
                deps.add(lw)
            if isinstance(k, tuple) and k[0] in ("ps", "psacc"):
                for rd in self.readers.get(k, ()):
                    if rd.eng != eng:
                        deps.add(rd)
        for k in w:
            lw = self.lastw.get(k)
            if lw is not None:
                deps.add(lw)
            for rd in self.readers.get(k, ()):
                deps.add(rd)
        for k in r:
            self.readers.setdefault(k, []).append(op)
        for k in w:
            self.lastw[k] = op
            self.readers[k] = []
        if not self.bar_seen[eng]:
            deps.update(self.bar_deps)
            self.bar_seen[eng] = True
        if dma:
            lane = (eng, self.lane_rr[eng] % N_LANES)
            self.lane_rr[eng] += 1
            prev = self.lane_last.get(lane)
            if prev is not None:
                deps.add(prev)
            self.lane_last[lane] = op
            c = self.lane_cnt.get(lane, 0) + 1
            self.lane_cnt[lane] = c
            op.lane = lane
            op.lane_val = 16 * c
        else:
            self.last_by_eng[eng] = op
        deps.discard(op)
        self.ops.append(op)
        return op

    def barrier(self):
        d = set()
        for e in ENGS:
            if self.last_by_eng[e] is not None:
                d.add(self.last_by_eng[e])
        for lane, op in self.lane_last.items():
            d.add(op)
        self.bar_deps = d
        self.bar_seen = {e: False for e in ENGS}
        self.lastw = {}
        self.readers = {}

    def pe(self, fn, r=(), w=()):
        return self.add("pe", fn, r, w)

    def act(self, fn, r=(), w=()):
        return self.add("act", fn, r, w)

    def dve(self, fn, r=(), w=()):
        return self.add("dve", fn, r, w)

    def pool(self, fn, r=(), w=()):
        return self.add("pool", fn, r, w)

    def dma(self, fn, r=(), w=(), q="sp"):
        return self.add(q, fn, r, w, dma=True)

    def emit(self, nc, stack):
        ops = self.ops
        per_eng = {e: [] for e in ENGS}
        for op in ops:
            op.idx = len(per_eng[op.eng])
            per_eng[op.eng].append(op)
        ses = self.same_engine_sync
        for op in ops:
            for d in op.deps:
                if d.dma:
                    continue
                if d.eng != op.eng or op.dma or (ses and d.eng != "pe"):
                    d.need_sig = True
        for e in ENGS:
            c = 0
            for op in per_eng[e]:
                if not op.dma and op.need_sig:
                    c += 1
                    op.sig = c
        eng_sem = {e: stack.enter_context(nc.semaphore("es_" + e)) for e in ENGS}
        lane_sem = {}
        for lane in self.lane_cnt:
            lane_sem[lane] = stack.enter_context(nc.semaphore("ls_%s_%d" % lane))
        block = stack.enter_context(nc.Block())

        def run(e, eng):
            waited = {}
            for op in per_eng[e]:
                need = {}
                for d in op.deps:
                    if d.dma:
                        s = lane_sem[d.lane]
                        v = d.lane_val
                    else:
                        if not d.need_sig:
                            continue
                        if d.eng == e and not op.dma and not (ses and e != "pe"):
                            continue
                        s = eng_sem[d.eng]
                        v = d.sig
                    key = id(s)
                    if waited.get(key, 0) >= v:
                        continue
                    if key not in need or need[key][1] < v:
                        need[key] = (s, v)
                for key, (s, v) in need.items():
                    eng.wait_ge(s, v)
                    waited[key] = v
                ins = op.fn(eng)
                if op.dma:
                    ins.then_inc(lane_sem[op.lane], 16)
                elif op.need_sig:
                    ins.then_inc(eng_sem[e], 1)
            if e == "sp":
                for lane, c in self.lane_cnt.items():
                    eng.wait_ge(lane_sem[lane], 16 * c)

        @block.tensor
        def _(eng):
            run("pe", eng)

        @block.scalar
        def _(eng):
            run("act", eng)

        @block.vector
        def _(eng):
            run("dve", eng)

        @block.gpsimd
        def _(eng):
            run("pool", eng)

        @block.sync
        def _(eng):
            run("sp", eng)

        return {e: len(per_eng[e]) for e in ENGS}


D = 1024
DFF = 2816
NFC = DFF // 128
DEPTH = 4
ALPHA = (2 * DEPTH) ** 0.25
LN_EPS = 1e-5
RMS_EPS = 1e-6
NH = 8
GQKV = 3072
GIN = 4112
NMETA = 16
CW = 31
PAGE = 128
NS = 4
SB_BASE = 16640
SB_END = 229000

PR_LNG = 0
PR_LNB = 12
PR_DWW = 24
PR_DWB = 55
PR_CLG = 56
PR_CLB = 57
PR_BP2 = 58
PR_BP1 = 59
PR_N = 61


class Cfg:
    def __init__(self, seq, npages, nphys, n_cores):
        self.SEQ = seq
        self.T = seq + NMETA
        self.TT = self.T + NS
        self.NPAGES = npages
        self.NPHYS = nphys
        self.n_cores = n_cores
        self.ntile = (self.TT + 511) // 512
        self.nchunk = (self.T + 127) // 128
        self.TP = self.nchunk * 128


class Builder:
    def __init__(self, cfg, stop_after=99, only=None):
        self.cfg = cfg
        self.stop_after = stop_after
        self.only = only
        self.nc = bass.Bass("TRN2", target_bir_lowering=False)
        self.S = Sched(same_engine_sync=(os.environ.get('SES', '1') == '1'))
        self.uid = 0
        self.persist_end = SB_BASE
        self.off = SB_BASE
        self.bank_rr = 0

    def sb(self, name, shape, dt, persist=False):
        nbytes = int(np.prod(shape[1:])) * (2 if dt == BF16 else 4)
        nbytes = (nbytes + 63) // 64 * 64
        self.uid += 1
        t = self.nc.alloc_sbuf_tensor_at("%s_%d" % (name, self.uid), list(shape), dt, offset=self.off)
        self.off += nbytes
        assert self.off <= SB_END, ("SBUF overflow", name, self.off)
        if persist:
            self.persist_end = self.off
        return t

    def sb_alias(self, name, shape, dt, other_off):
        self.uid += 1
        return self.nc.alloc_sbuf_tensor_at("%s_%d" % (name, self.uid), list(shape), dt, offset=other_off)

    def phase(self):
        self.S.barrier()
        self.off = self.persist_end

    def bank(self):
        i = self.bank_rr % 6
        self.bank_rr += 1
        return self.ps[i], ("ps", i)

    def dram_in(self, name, shape, dt=F32):
        return self.nc.dram_tensor(name, list(shape), dt, kind="ExternalInput").ap()

    def dram_out(self, name, shape, dt=F32):
        return self.nc.dram_tensor(name, list(shape), dt, kind="ExternalOutput").ap()

    def dram_tmp(self, name, shape, dt=F32):
        return self.nc.dram_tensor(name, list(shape), dt, kind="Internal").ap()

    def build(self):
        cfg, nc, S = self.cfg, self.nc, self.S
        T, TT = cfg.T, cfg.TT
        I = {}
        I["x_prompt"] = self.dram_in("x_prompt", [cfg.SEQ, D])
        I["x_sample"] = self.dram_in("x_sample", [NS, D])
        I["state_gdn_conv"] = self.dram_in("state_gdn_conv", [2, NS * 3, GQKV])
        I["state_gdn_S"] = self.dram_in("state_gdn_S", [2, NS, NH, 128, 128])
        I["cache_k"] = self.dram_in("cache_k", [cfg.NPHYS, PAGE, D])
        I["cache_v"] = self.dram_in("cache_v", [cfg.NPHYS, PAGE, D])
        I["state_conv"] = self.dram_in("state_conv", [NS, CW - 1, D])
        I["page_table"] = self.dram_in("page_table", [1, NS * cfg.NPAGES], I32)
        I["meta_tokens"] = self.dram_in("meta_tokens", [NMETA, D])
        I["pvec"] = self.dram_in("pvec", [PR_N, D])
        I["ffn_w_gate"] = self.dram_in("ffn_w_gate", [DEPTH, 2, D, DFF])
        I["ffn_w_up"] = self.dram_in("ffn_w_up", [DEPTH, 2, D, DFF])
        I["ffn_w_down"] = self.dram_in("ffn_w_down", [DEPTH, 2, DFF, D])
        I["gdn_w_in"] = self.dram_in("gdn_w_in", [2, D, GIN])
        I["gdn_conv_w"] = self.dram_in("gdn_conv_w", [8, GQKV])
        I["gdn_a_log"] = self.dram_in("gdn_a_log", [2, NH])
        I["gdn_dt_bias"] = self.dram_in("gdn_dt_bias", [2, NH])
        I["gdn_norm_w"] = self.dram_in("gdn_norm_w", [2, 128])
        I["gdn_w_out"] = self.dram_in("gdn_w_out", [2, D, D])
        I["sb_w_qkv"] = self.dram_in("sb_w_qkv", [D, 3 * D])
        I["sb_w_out"] = self.dram_in("sb_w_out", [D, D])
        I["sb_logit_bias"] = self.dram_in("sb_logit_bias", [1, NH])
        I["cv_w_pw1"] = self.dram_in("cv_w_pw1", [D, 2 * D])
        I["cv_w_pw2"] = self.dram_in("cv_w_pw2", [D, D])
        self.I = I
        O = {}
        O["y_prompt"] = self.dram_out("y_prompt", [cfg.SEQ, D])
        O["y_sample"] = self.dram_out("y_sample", [NS, D])
        O["gdn_S_prompt"] = self.dram_out("gdn_S_prompt", [2, NH, 128, 128])
        O["gdn_S_sample"] = self.dram_out("gdn_S_sample", [2, NS, NH, 128, 128])
        O["gdn_conv_prompt"] = self.dram_out("gdn_conv_prompt", [2, 3, GQKV])
        O["gdn_conv_sample"] = self.dram_out("gdn_conv_sample", [2, NS, 3, GQKV])
        O["sb_k_prompt"] = self.dram_out("sb_k_prompt", [T, D])
        O["sb_v_prompt"] = self.dram_out("sb_v_prompt", [T, D])
        O["sb_k_sample"] = self.dram_out("sb_k_sample", [NS, D])
        O["sb_v_sample"] = self.dram_out("sb_v_sample", [NS, D])
        O["conv_prompt"] = self.dram_out("conv_prompt", [CW - 1, D])
        O["conv_sample"] = self.dram_out("conv_sample", [NS, CW - 1, D])
        self.O = O
        self.hT = self.dram_tmp("hT", [8, 128, TT])
        self.qkvT = self.dram_tmp("qkvT", [24, 128, TT], BF16)
        self.szT = self.dram_tmp("szT", [8, 128, TT])
        self.qkvF = self.dram_tmp("qkvF", [24, 128, TT])
        self.gb = self.dram_tmp("gb", [cfg.TP, 16])
        self.oT = self.dram_tmp("oT", [8, 128, TT])
        self.vtok = self.dram_tmp("vtok", [cfg.TP, D], BF16)
        self.qs = self.dram_tmp("qs", [NS, D])
        self.uT = self.dram_tmp("uT", [8, 128, TT])

        with contextlib.ExitStack() as st:
            self.ps = [st.enter_context(nc.psum_tensor("ps%d" % i, [128, 512], F32)) for i in range(8)]
            self.setup_consts()
            self.embed()
            stages = []
            for li in range(DEPTH):
                stages.append(lambda li=li: self.ffn(li, 0))
                kind = li % 3
                if kind == 0:
                    stages.append(lambda li=li: self.gdn(li, li // 3))
                elif kind == 1:
                    stages.append(lambda li=li: self.sbattn(li))
                else:
                    stages.append(lambda li=li: self.conformer(li))
                stages.append(lambda li=li: self.ffn(li, 1))
            for i, s in enumerate(stages):
                if i >= self.stop_after:
                    break
                if self.only is not None and i not in self.only:
                    continue
                s()
            self.final_out()
            self.counts = S.emit(nc, st)
        return nc

    def setup_consts(self):
        nc, S = self.nc, self.S
        self.ones_f = self.sb("ones_f", [128, 512], F32, True)
        self.ident = self.sb("ident", [128, 128], F32, True)
        self.ident_bf = self.sb("ident_bf", [128, 128], BF16, True)
        self.ones_bf = self.sb("ones_bf", [128, 128], BF16, True)
        self.MU = self.sb("MU", [128, 128], F32, True)
        self.MU_bf = self.sb("MU_bf", [128, 128], BF16, True)
        self.ML_bf = self.sb("ML_bf", [128, 128], BF16, True)
        self.SLn = self.sb("SLn", [128, 128], F32, True)
        self.PT = self.sb("PT", [128, 8, PR_N], F32, True)
        self.GCW = self.sb("GCW", [128, 24, 8], F32, True)
        ones_f, ident = self.ones_f, self.ident
        S.pool(lambda e: e.memset(ones_f[:], 1.0), w=["ones_f"])
        S.pool(lambda e: e.memset(self.ones_bf[:], 1.0), w=["ones_bf"])
        o128 = ones_f[:, 0:128]

        def sel(out, op, base, fill=0.0, cm=1, pat=-1, src=None):
            src = o128 if src is None else src
            return lambda e: e.affine_select(out=out, in_=src, pattern=[[pat, src.shape[-1]]], compare_op=op,
                                             fill=fill, base=base, channel_multiplier=cm)
        S.pool(sel(ident[:], ALU.is_equal, 0), r=["ones_f"], w=["ident"])
        S.pool(sel(self.ident_bf[:], ALU.is_equal, 0), r=["ones_f"], w=["ident_bf"])
        S.pool(sel(self.MU[:], ALU.is_ge, 0, cm=-1, pat=1), r=["ones_f"], w=["MU"])
        S.pool(sel(self.MU_bf[:], ALU.is_ge, 0, cm=-1, pat=1), r=["ones_f"], w=["MU_bf"])
        S.pool(sel(self.ML_bf[:], ALU.is_ge, 0, cm=1, pat=-1), r=["ones_f"], w=["ML_bf"])
        S.pool(lambda e: e.memset(self.SLn[:], -1.0), w=["SLn"])
        S.pool(sel(self.SLn[:], ALU.is_gt, 0, cm=1, pat=-1, src=self.SLn[:]), r=["SLn"], w=["SLn"])
        stg = self.sb("stg", [PR_N, D], F32)
        S.dma(lambda e: e.dma_start(out=stg[:], in_=self.I["pvec"]), w=["stg"])
        for c in range(8):
            pb, pk = self.bank()
            S.pe(lambda e, c=c, pb=pb: e.transpose(pb[:, 0:PR_N], stg[:, c * 128:(c + 1) * 128], ident[0:PR_N, 0:PR_N]),
                 r=["stg", "ident"], w=[pk])
            S.dve(lambda e, c=c, pb=pb: e.tensor_copy(out=self.PT[:, c, :], in_=pb[:, 0:PR_N]), r=[pk], w=[("PT", c)])
        stg2 = self.sb("stg2", [8, GQKV], F32)
        S.dma(lambda e: e.dma_start(out=stg2[:], in_=self.I["gdn_conv_w"]), w=["stg2"])
        for c in range(24):
            pb, pk = self.bank()
            S.pe(lambda e, c=c, pb=pb: e.transpose(pb[:, 0:8], stg2[:, c * 128:(c + 1) * 128], ident[0:8, 0:8]),
                 r=["stg2", "ident"], w=[pk])
            S.dve(lambda e, c=c, pb=pb: e.tensor_copy(out=self.GCW[:, c, :], in_=pb[:, 0:8]), r=[pk], w=[("GCW", c)])

    def pcol(self, row, c):
        return self.PT[:, c, row:row + 1]

    def embed(self):
        cfg, S = self.cfg, self.S
        self.phase()
        hT = self.hT
        srcs = [(self.I["meta_tokens"], 0, NMETA, 0)]
        for j in range(cfg.SEQ // 128):
            srcs.append((self.I["x_prompt"], j * 128, 128, NMETA + j * 128))
        srcs.append((self.I["x_sample"], 0, NS, cfg.T))
        xin = [self.sb("xin%d" % i, [128, D], F32) for i in range(2)]
        xo = [self.sb("xo%d" % i, [128, 8, 128], F32) for i in range(2)]
        for n, (src, r0, nr, c0) in enumerate(srcs):
            b = n % 2
            xi, xt = xin[b], xo[b]
            S.dma(lambda e, xi=xi, src=src, r0=r0, nr=nr: e.dma_start(out=xi[0:nr, :], in_=src[r0:r0 + nr, :]),
                  w=[("xin", b)])
            for c in range(8):
                pb, pk = self.bank()
                S.pe(lambda e, xi=xi, c=c, nr=nr, pb=pb: e.transpose(pb[:, 0:nr], xi[0:nr, c * 128:(c + 1) * 128],
                                                                      self.ident[0:nr, 0:nr]),
                     r=[("xin", b), "ident"], w=[pk])
                eng = S.act if c % 2 else S.dve
                if c % 2:
                    S.act(lambda e, xt=xt, c=c, nr=nr, pb=pb: e.copy(out=xt[:, c, 0:nr], in_=pb[:, 0:nr]),
                          r=[pk], w=[("xo", b, c)])
                else:
                    S.dve(lambda e, xt=xt, c=c, nr=nr, pb=pb: e.tensor_copy(out=xt[:, c, 0:nr], in_=pb[:, 0:nr]),
                          r=[pk], w=[("xo", b, c)])
            S.dma(lambda e, xt=xt, nr=nr, c0=c0: e.dma_start(
                out=hT[:, :, c0:c0 + nr].rearrange("c p t -> p c t"), in_=xt[:, :, 0:nr]),
                r=[("xo", b, c) for c in range(8)], w=[("hT", n)])

    def tiles(self):
        cfg = self.cfg
        return [(t * 512, min(512, cfg.TT - t * 512)) for t in range(cfg.ntile)]

    def load_x(self, x, xb, c0, N, xa=None, key="x"):
        S = self.S
        S.dma(lambda e: e.dma_start(out=x[:, :, 0:N], in_=self.hT[:, :, c0:c0 + N].rearrange("c p t -> p c t")),
              w=[key])
        S.pool(lambda e: e.tensor_copy(out=xb[:, :, 0:N], in_=x[:, :, 0:N]), r=[key], w=[key + "b"])
        if xa is not None:
            S.act(lambda e: e.activation(out=xa[:, :, 0:N], in_=x[:, :, 0:N], func=AF.Copy, scale=float(ALPHA)),
                  r=[key], w=[key + "a"])

    def load_w(self, dst, src, nk, key):
        for kc in range(nk):
            self.S.dma(lambda e, kc=kc: e.dma_start(out=dst[:, kc, :], in_=src[kc * 128:(kc + 1) * 128, :]),
                       w=[(key, kc)], q="pool")

    def store_h(self, y, c0, N, key):
        self.S.dma(lambda e: e.dma_start(out=self.hT[:, :, c0:c0 + N].rearrange("c p t -> p c t"), in_=y[:, :, 0:N]),
                   r=[(key, c) for c in range(8)], w=[("hT", c0)])

    def ffn(self, li, fi):
        S, I = self.S, self.I
        self.phase()
        Wg = self.sb("Wg", [128, 8, DFF], BF16)
        Wu = self.sb("Wu", [128, 8, DFF], BF16)
        Wd = self.sb("Wd", [128, NFC, D], BF16)
        def wblk(dst, src, nk, key, cb, ncols):
            c1 = min(ncols, (cb + 1) * 512)
            S.dma(lambda e: e.dma_start(out=dst[:, :, cb * 512:c1], in_=src[:, cb * 512:c1].rearrange("(k p) n -> p k n", p=128)),
                  w=[(key, cb)], q="pool")
        for cb in range((DFF + 511) // 512):
            wblk(Wg, I["ffn_w_gate"][li, fi], 8, "Wg", cb, DFF)
            wblk(Wu, I["ffn_w_up"][li, fi], 8, "Wu", cb, DFF)
        for cb in range(2):
            wblk(Wd, I["ffn_w_down"][li, fi], NFC, "Wd", cb, D)
        x = self.sb("x", [128, 8, 512], F32)
        xbs = [self.sb("xb%d" % i, [128, 8, 512], BF16) for i in range(2)]
        actb = self.sb("actb", [128, NFC, 512], BF16)
        sg = [self.sb("sg%d" % i, [128, 512], F32) for i in range(2)]
        rb = actb
        mean = self.sb("ln_mean", [128, 512], F32)
        msq = self.sb("ln_msq", [128, 512], F32)
        rstd = self.sb("ln_rstd", [128, 512], F32)
        lrow = li * 3 + (0 if fi == 0 else 2)
        tl = self.tiles()

        def load_xb(ti):
            c0_, N_ = tl[ti]
            S.dma(lambda e: e.dma_start(out=xbs[ti % 2][:, :, 0:N_], in_=self.hT[:, :, c0_:c0_ + N_].rearrange("c p t -> p c t")),
                  r=[("hTt", ti)], w=[("xb", ti % 2)], q="pool")
        load_xb(0)
        for ti, (c0, N) in enumerate(tl):
            xb = xbs[ti % 2]
            xbk = ("xb", ti % 2)
            if ti + 1 < len(tl):
                load_xb(ti + 1)
            S.dma(lambda e, c0=c0, N=N: e.dma_start(out=x[:, :, 0:N], in_=self.hT[:, :, c0:c0 + N].rearrange("c p t -> p c t")),
                  r=[("hTt", ti)], w=["x"])
            for fc in range(NFC):
                pg, kg = self.bank()
                pu, ku = self.bank()
                for kc in range(8):
                    S.pe(lambda e, fc=fc, kc=kc, pg=pg, N=N, xb=xb: e.matmul(pg[:, 0:N], lhsT=Wg[:, kc, fc * 128:(fc + 1) * 128],
                                                                       rhs=xb[:, kc, 0:N], start=(kc == 0), stop=(kc == 7)),
                         r=[xbk, ("Wg", fc // 4)], w=[kg])
                for kc in range(8):
                    S.pe(lambda e, fc=fc, kc=kc, pu=pu, N=N, xb=xb: e.matmul(pu[:, 0:N], lhsT=Wu[:, kc, fc * 128:(fc + 1) * 128],
                                                                       rhs=xb[:, kc, 0:N], start=(kc == 0), stop=(kc == 7)),
                         r=[xbk, ("Wu", fc // 4)], w=[ku])
                sgi = sg[fc % 2]
                S.act(lambda e, pg=pg, sgi=sgi, N=N: e.activation(out=sgi[:, 0:N], in_=pg[:, 0:N], func=AF.Silu),
                      r=[kg], w=[("sg", fc % 2)])
                S.dve(lambda e, fc=fc, pu=pu, sgi=sgi, N=N: e.tensor_tensor(out=actb[:, fc, 0:N], in0=sgi[:, 0:N],
                                                                            in1=pu[:, 0:N], op=ALU.mult),
                      r=[("sg", fc % 2), ku], w=[("actb", fc)])
            akeys = [("actb", fc) for fc in range(NFC)]
            for oc in range(8):
                pd, kd = self.bank()
                for fc in range(NFC):
                    S.pe(lambda e, oc=oc, fc=fc, pd=pd, N=N: e.matmul(pd[:, 0:N], lhsT=Wd[:, fc, oc * 128:(oc + 1) * 128],
                                                                       rhs=actb[:, fc, 0:N], start=(fc == 0), stop=(fc == NFC - 1)),
                         r=akeys + [("Wd", oc // 4)], w=[kd])
                S.act(lambda e, oc=oc, N=N: e.activation(out=x[:, oc, 0:N], in_=x[:, oc, 0:N], func=AF.Copy, scale=float(ALPHA)),
                      r=["x"], w=[("r", oc)])
                S.dve(lambda e, oc=oc, pd=pd, N=N: e.scalar_tensor_tensor(out=x[:, oc, 0:N], in0=pd[:, 0:N], scalar=0.5,
                                                                          in1=x[:, oc, 0:N], op0=ALU.mult, op1=ALU.add),
                      r=[kd, ("r", oc)], w=[("r", oc)])
            self.ln_inplace(x, N, lrow, rb, mean, msq, rstd, rbkeys=[("actb", fc) for fc in range(8)])
            S.dma(lambda e, c0=c0, N=N: e.dma_start(out=self.hT[:, :, c0:c0 + N].rearrange("c p t -> p c t"), in_=x[:, :, 0:N]),
                  r=[("r", c) for c in range(8)], w=["x", ("hTt", ti)])

    def ln_inplace(self, x, N, lrow, rb, mean, msq, rstd, rkey="r", growfn=None, browfn=None, rbkeys=("ln_rb",)):
        S = self.S
        rbkeys = list(rbkeys)
        rk = [(rkey, c) for c in range(8)]
        p1, k1 = self.bank()
        p2, k2 = self.bank()
        S.pool(lambda e: e.tensor_copy(out=rb[:, 0:8, 0:N], in_=x[:, :, 0:N]), r=rk, w=rbkeys)
        for c in range(8):
            S.pe(lambda e, c=c: e.matmul(p1[:, 0:N], lhsT=self.ones_bf[:], rhs=rb[:, c, 0:N], start=(c == 0), stop=(c == 7)),
                 r=rbkeys + ["ones_bf"], w=[k1])
        S.act(lambda e: e.activation(out=mean[:, 0:N], in_=p1[:, 0:N], func=AF.Copy, scale=1.0 / D), r=[k1], w=["ln_mean"])
        S.pool(lambda e: e.tensor_tensor(out=rb[:, 0:8, 0:N], in0=x[:, :, 0:N], in1=x[:, :, 0:N], op=ALU.mult),
               r=rk + rbkeys, w=rbkeys)
        for c in range(8):
            S.pe(lambda e, c=c: e.matmul(p2[:, 0:N], lhsT=self.ones_bf[:], rhs=rb[:, c, 0:N], start=(c == 0), stop=(c == 7)),
                 r=rbkeys + ["ones_bf"], w=[k2])
        S.dve(lambda e: e.tensor_tensor(out=msq[:, 0:N], in0=mean[:, 0:N], in1=mean[:, 0:N], op=ALU.mult),
              r=["ln_mean"], w=["ln_msq"])
        S.dve(lambda e: e.scalar_tensor_tensor(out=msq[:, 0:N], in0=p2[:, 0:N], scalar=1.0 / D, in1=msq[:, 0:N],
                                               op0=ALU.mult, op1=ALU.subtract), r=[k2, "ln_msq"], w=["ln_msq"])
        S.dve(lambda e: e.tensor_scalar(out=msq[:, 0:N], in0=msq[:, 0:N], scalar1=0.0, scalar2=float(LN_EPS),
                                        op0=ALU.max, op1=ALU.add), r=["ln_msq"], w=["ln_msq"])
        S.act(lambda e: e.activation(out=rstd[:, 0:N], in_=msq[:, 0:N], func=AF.Ln), r=["ln_msq"], w=["ln_rstd"])
        S.act(lambda e: e.activation(out=rstd[:, 0:N], in_=rstd[:, 0:N], func=AF.Exp, scale=-0.5), r=["ln_rstd"], w=["ln_rstd"])
        for c in range(8):
            S.dve(lambda e, c=c: e.tensor_tensor(out=x[:, c, 0:N], in0=x[:, c, 0:N], in1=mean[:, 0:N], op=ALU.subtract),
                  r=[(rkey, c), "ln_mean"] + rbkeys, w=[(rkey, c)])
            S.pool(lambda e, c=c: e.tensor_tensor(out=x[:, c, 0:N], in0=x[:, c, 0:N], in1=rstd[:, 0:N], op=ALU.mult),
                   r=[(rkey, c), "ln_rstd"], w=[(rkey, c)])
            g = self.pcol(PR_LNG + lrow, c) if growfn is None else growfn(c)
            b = self.pcol(PR_LNB + lrow, c) if browfn is None else browfn(c)
            S.act(lambda e, c=c, g=g, b=b: e.activation(out=x[:, c, 0:N], in_=x[:, c, 0:N], func=AF.Identity, scale=g, bias=b),
                  r=[(rkey, c)], w=[(rkey, c)])

    def gdn(self, li, j):
        cfg, S, I, O = self.cfg, self.S, self.I, self.O
        T, TT = cfg.T, cfg.TT
        self.phase()
        Win = self.sb("Win", [128, 8, GIN], BF16)
        self.load_w(Win, I["gdn_w_in"][j], 8, "Win")
        wk = [("Win", k) for k in range(8)]
        x = self.sb("x", [128, 8, 512], F32)
        xb = self.sb("xb", [128, 8, 512], BF16)
        prc = [self.sb("prc%d" % i, [128, 3 + 512], F32) for i in range(3)]
        cvc = [self.sb("cvc%d" % i, [128, 512], F32) for i in range(3)]
        sqb = [self.sb("sqb%d" % i, [128, 512], BF16) for i in range(3)]
        rinv = [self.sb("rinv%d" % i, [128, 512], F32) for i in range(3)]
        off_q = self.off
        qkvb = self.sb("qkvb", [128, 24, 512], F32)
        off_z = self.off
        szt = self.sb("szt", [128, 8, 512], F32)
        hal = self.sb("hal", [128, 24, 3], F32)
        ctxo = self.sb("ctxo", [128, 24, 3], F32)
        prs = self.sb("prs", [128, 24, NS], F32)
        cst = self.sb_alias("gcst", [NS * 3, GQKV], F32, off_q)
        ctxT = self.sb("gctxT", [128, 24, NS * 3], F32)
        tm3 = self.sb_alias("tm3", [3, GQKV], F32, off_q)
        tm4 = self.sb_alias("tm4", [NS, GQKV], F32, off_z)
        dtb = self.sb("dtb", [128, NH], F32)
        nea = self.sb("nea", [128, NH], F32)
        gbt = [self.sb("gbt%d" % i, [128, 16], F32) for i in range(2)]
        S.pool(lambda e: e.memset(hal[:], 0.0), w=["hal"])
        S.dma(lambda e: e.dma_start(out=dtb[:], in_=I["gdn_dt_bias"][j:j + 1, :].to_broadcast([128, NH])), w=["dtb"])
        S.dma(lambda e: e.dma_start(out=nea[:], in_=I["gdn_a_log"][j:j + 1, :].to_broadcast([128, NH])), w=["nea"])
        S.act(lambda e: e.activation(out=nea[:], in_=nea[:], func=AF.Exp), r=["nea"], w=["nea"])
        S.dve(lambda e: e.tensor_scalar(out=nea[:], in0=nea[:], scalar1=-1.0, scalar2=None, op0=ALU.mult), r=["nea"], w=["nea"])
        S.dma(lambda e: e.dma_start(out=cst[:], in_=I["state_gdn_conv"][j]), w=["gcst"])
        ckeys = self.tm_to_fm(cst, NS * 3, lambda c: ctxT[:, c, :], 24, ["gcst"], "gctxT")
        S.barrier()
        gw = lambda c, i: self.GCW[:, c, j * 4 + i:j * 4 + i + 1]
        it = 0
        for (c0, N) in self.tiles():
            self.load_x(x, xb, c0, N)
            npc = min(N, T - c0)
            has_s = c0 + N > T
            for oc in range(24):
                b = it % 3
                it += 1
                pc, cv = prc[b], cvc[b]
                pb, pk = self.bank()
                for kc in range(8):
                    S.pe(lambda e, oc=oc, kc=kc, pb=pb, N=N: e.matmul(pb[:, 0:N], lhsT=Win[:, kc, oc * 128:(oc + 1) * 128], rhs=xb[:, kc, 0:N],
                                                                       start=(kc == 0), stop=(kc == 7)), r=["xb"] + wk, w=[pk])
                S.act(lambda e, pc=pc, pb=pb, N=N: e.copy(out=pc[:, 3:3 + N], in_=pb[:, 0:N]), r=[pk], w=[("prc", b)])
                S.pool(lambda e, pc=pc, oc=oc: e.tensor_copy(out=pc[:, 0:3], in_=hal[:, oc, :]), r=[("hal", oc), "hal"], w=[("prch", b)])
                pkeys = [("prc", b), ("prch", b)]
                if not has_s:
                    S.pool(lambda e, pc=pc, oc=oc, N=N: e.tensor_copy(out=hal[:, oc, :], in_=pc[:, N:N + 3]), r=pkeys, w=[("hal", oc)])
                else:
                    S.pool(lambda e, pc=pc, oc=oc, npc=npc: e.tensor_copy(out=ctxo[:, oc, :], in_=pc[:, npc:npc + 3]), r=pkeys, w=[("ctxo", oc)])
                    S.pool(lambda e, pc=pc, oc=oc, npc=npc: e.tensor_copy(out=prs[:, oc, :], in_=pc[:, 3 + npc:3 + npc + NS]), r=pkeys, w=[("prs", oc)])
                S.dve(lambda e, pc=pc, cv=cv, oc=oc, N=N: e.tensor_scalar(out=cv[:, 0:N], in0=pc[:, 0:N], scalar1=gw(oc, 0), scalar2=None, op0=ALU.mult),
                      r=pkeys, w=[("cvc", b)])
                for i in range(1, 4):
                    S.dve(lambda e, pc=pc, cv=cv, oc=oc, i=i, N=N: e.scalar_tensor_tensor(out=cv[:, 0:N], in0=pc[:, i:i + N], scalar=gw(oc, i), in1=cv[:, 0:N],
                                                                                       op0=ALU.mult, op1=ALU.add), r=pkeys + [("cvc", b)], w=[("cvc", b)])
                if has_s:
                    cx = ctxT[:, oc, :].rearrange("p (s i) -> p s i", i=3)
                    S.dve(lambda e, pc=pc, cv=cv, oc=oc, npc=npc: e.tensor_scalar(out=cv[:, npc:npc + NS], in0=pc[:, 3 + npc:3 + npc + NS], scalar1=gw(oc, 3),
                                                                                scalar2=None, op0=ALU.mult), r=pkeys + [("cvc", b)], w=[("cvc", b)])
                    for i in range(3):
                        S.dve(lambda e, cv=cv, oc=oc, i=i, npc=npc, cx=cx: e.scalar_tensor_tensor(out=cv[:, npc:npc + NS], in0=cx[:, :, i], scalar=gw(oc, i),
                                                                                              in1=cv[:, npc:npc + NS], op0=ALU.mult, op1=ALU.add),
                              r=ckeys + [("cvc", b)], w=[("cvc", b)])
                if oc >= 16:
                    S.act(lambda e, cv=cv, oc=oc, N=N: e.activation(out=qkvb[:, oc, 0:N], in_=cv[:, 0:N], func=AF.Silu), r=[("cvc", b)], w=[("qkvb", oc)])
                else:
                    S.act(lambda e, cv=cv, N=N: e.activation(out=cv[:, 0:N], in_=cv[:, 0:N], func=AF.Silu), r=[("cvc", b)], w=[("cvc", b)])
                    S.pool(lambda e, cv=cv, b=b, N=N: e.tensor_tensor(out=sqb[b][:, 0:N], in0=cv[:, 0:N], in1=cv[:, 0:N], op=ALU.mult), r=[("cvc", b)], w=[("sqb", b)])
                    p2, k2 = self.bank()
                    S.pe(lambda e, b=b, p2=p2, N=N: e.matmul(p2[:, 0:N], lhsT=self.ones_bf[:], rhs=sqb[b][:, 0:N], start=True, stop=True), r=[("sqb", b), "ones_bf"], w=[k2])
                    S.act(lambda e, b=b, p2=p2, N=N: e.activation(out=rinv[b][:, 0:N], in_=p2[:, 0:N], func=AF.Ln, bias=float(RMS_EPS), scale=1.0), r=[k2], w=[("rinv", b)])
                    S.act(lambda e, b=b, N=N: e.activation(out=rinv[b][:, 0:N], in_=rinv[b][:, 0:N], func=AF.Exp, scale=-0.5), r=[("rinv", b)], w=[("rinv", b)])
                    cc = (128.0 ** -0.5) if oc < 8 else 1.0
                    S.dve(lambda e, cv=cv, b=b, oc=oc, N=N, cc=cc: e.scalar_tensor_tensor(out=qkvb[:, oc, 0:N], in0=cv[:, 0:N], scalar=cc, in1=rinv[b][:, 0:N],
                                                                                       op0=ALU.mult, op1=ALU.mult), r=[("cvc", b), ("rinv", b)], w=[("qkvb", oc)])
            S.dma(lambda e, c0=c0, N=N: e.dma_start(out=self.qkvF[:, :, c0:c0 + N].rearrange("c p t -> p c t"), in_=qkvb[:, :, 0:N]),
                  r=[("qkvb", oc) for oc in range(24)], w=[("qkvT", c0)])
            for oc in range(8 if 'z' in os.environ.get('G1P', 'zgo') else 0):
                pb, pk = self.bank()
                for kc in range(8):
                    S.pe(lambda e, oc=oc, kc=kc, pb=pb, N=N: e.matmul(pb[:, 0:N], lhsT=Win[:, kc, GQKV + oc * 128:GQKV + (oc + 1) * 128], rhs=xb[:, kc, 0:N],
                                                                       start=(kc == 0), stop=(kc == 7)), r=["xb"] + wk, w=[pk])
                S.act(lambda e, oc=oc, pb=pb, N=N: e.activation(out=szt[:, oc, 0:N], in_=pb[:, 0:N], func=AF.Silu), r=[pk], w=[("szt", oc)])
            if 'z' in os.environ.get('G1P', 'zgo'):
              S.dma(lambda e, c0=c0, N=N: e.dma_start(out=self.szT[:, :, c0:c0 + N].rearrange("c p t -> p c t"), in_=szt[:, :, 0:N]),
                  r=[("szt", oc) for oc in range(8)], w=[("szT", c0)])
            for j0 in range(0, N if 'g' in os.environ.get('G1P', 'zgo') else 0, 128):
                n = min(128, N - j0)
                b = it % 2
                it += 1
                gt = gbt[b]
                pb, pk = self.bank()
                for kc in range(8):
                    S.pe(lambda e, kc=kc, pb=pb, j0=j0, n=n: e.matmul(pb[0:n, 0:16], lhsT=xb[:, kc, j0:j0 + n], rhs=Win[:, kc, 4096:4112],
                                                                       start=(kc == 0), stop=(kc == 7)), r=["xb"] + wk, w=[pk])
                S.act(lambda e, gt=gt, pb=pb, n=n: e.activation(out=gt[0:n, 0:8], in_=pb[0:n, 0:8], func=AF.Sigmoid), r=[pk], w=[("gbt", b)])
                S.dve(lambda e, gt=gt, pb=pb, n=n: e.tensor_tensor(out=gt[0:n, 8:16], in0=pb[0:n, 8:16], in1=dtb[0:n, :], op=ALU.add), r=[pk, "dtb"], w=[("gbt2", b)])
                S.act(lambda e, gt=gt, n=n: e.activation(out=gt[0:n, 8:16], in_=gt[0:n, 8:16], func=AF.Exp), r=[("gbt2", b)], w=[("gbt2", b)])
                S.act(lambda e, gt=gt, n=n: e.activation(out=gt[0:n, 8:16], in_=gt[0:n, 8:16], func=AF.Ln, bias=1.0, scale=1.0), r=[("gbt2", b)], w=[("gbt2", b)])
                S.dve(lambda e, gt=gt, n=n: e.tensor_tensor(out=gt[0:n, 8:16], in0=gt[0:n, 8:16], in1=nea[0:n, :], op=ALU.mult), r=[("gbt2", b), "nea"], w=[("gbt2", b)])
                g0 = c0 + j0
                S.dma(lambda e, gt=gt, g0=g0, n=n: e.dma_start(out=self.gb[g0:g0 + n, :], in_=gt[0:n, :]), r=[("gbt", b), ("gbt2", b)], w=[("gb", g0)])
        if 'o' not in os.environ.get('G1P', 'zgo'):
            return
        S.barrier()
        k3 = self.fm_to_tm(lambda c: ctxo[:, c, :], 3, tm3, 24, [("ctxo", c) for c in range(24)], "tm3")
        S.dma(lambda e: e.dma_start(out=O["gdn_conv_prompt"][j], in_=tm3[:]), r=k3, w=["o_gcp"])
        k4 = self.fm_to_tm(lambda c: prs[:, c, :], NS, tm4, 24, [("prs", c) for c in range(24)], "tm4")
        S.dma(lambda e: e.dma_start(out=O["gdn_conv_sample"][j, :, 2, :], in_=tm4[:]), r=k4, w=["o_gcs"])
        for s_ in range(NS):
            S.dma(lambda e, s_=s_: e.dma_start(out=O["gdn_conv_sample"][j, s_, 0:2, :], in_=I["state_gdn_conv"][j, s_ * 3 + 1:s_ * 3 + 3, :]), w=[("o_gcs2", s_)])

        if int(os.environ.get('GDN_STOP', '9')) <= 1:
            return
        self.phase()
        HB = lambda name, dt: self.sb(name, [128, NH, 128], dt)
        qkv = [self.sb("cq%d" % i, [128, 24, 128], F32) for i in range(2)]
        gbc = [self.sb("gbc%d" % i, [128, 16], F32) for i in range(2)]
        gcs = self.sb("gcs", [128, NH], F32)
        gls = self.sb("gls", [128, NH], F32)
        ngc = self.sb("ngc", [128, NH], F32)
        bgc = self.sb("bgc", [128, NH], F32)
        edc = self.sb("edc", [128, NH], F32)
        egl = self.sb("egl", [128, NH], F32)
        T1 = HB("T1", F32); M1 = HB("M1", F32); M2 = HB("M2", F32); EG = HB("EG", F32)
        Xa = [HB("Xa%d" % i, F32) for i in range(2)]
        Ya = [HB("Ya%d" % i, F32) for i in range(2)]
        Pa = [HB("Pa%d" % i, F32) for i in range(2)]
        X0f = Xa[0]
        KBG = HB("KBG", F32); KD = HB("KD", F32); VB = HB("VB", F32); WTN = HB("WTN", F32)
        VN = HB("VN", F32); QG = HB("QG", F32); PM = HB("PM", F32)
        Sf = HB("Sf", F32); Sb = Sf
        och = [self.sb("och%d" % i, [128, NH, 128], F32) for i in range(2)]
        S.pool(lambda e: e.memset(Sf[:], 0.0), w=[("Sf", h) for h in range(NH)])
        nchunk = cfg.nchunk
        for c in range(nchunk):
            cb = c % 2
            t0 = c * 128
            nv = min(128, T - t0)
            Q, G_ = qkv[cb], gbc[cb]
            if nv < 128:
                S.pool(lambda e, Q=Q: e.memset(Q[:], 0.0), w=[("cq", cb)])
                S.pool(lambda e, G_=G_: e.memset(G_[:], 0.0), w=[("gbc", cb)])
            S.dma(lambda e, Q=Q, t0=t0, nv=nv: e.dma_start(out=Q[:, :, 0:nv], in_=self.qkvF[:, :, t0:t0 + nv].rearrange("c p t -> p c t")), r=[("cq", cb)], w=[("cq", cb)])
            S.dma(lambda e, G_=G_, t0=t0, nv=nv: e.dma_start(out=G_[0:nv, :], in_=self.gb[t0:t0 + nv, :]), r=[("gbc", cb)], w=[("gbc", cb)])
            qk_, gk_ = ("cq", cb), ("gbc", cb)
            p1, k1 = self.bank()
            p2, k2 = self.bank()
            S.pe(lambda e, G_=G_, p1=p1: e.matmul(p1[:, 0:NH], lhsT=self.MU[:], rhs=G_[:, 8:16], start=True, stop=True), r=[gk_, "MU"], w=[k1])
            S.pe(lambda e, G_=G_, p2=p2: e.matmul(p2[:, 0:NH], lhsT=self.ones_f[:, 0:128], rhs=G_[:, 8:16], start=True, stop=True), r=[gk_, "ones_f"], w=[k2])
            S.dve(lambda e, p1=p1: e.tensor_copy(out=gcs[:], in_=p1[:, 0:NH]), r=[k1], w=["gcs"])
            S.dve(lambda e, p2=p2: e.tensor_copy(out=gls[:], in_=p2[:, 0:NH]), r=[k2], w=["gls"])
            S.act(lambda e: e.activation(out=bgc[:], in_=gcs[:], func=AF.Exp), r=["gcs"], w=["bgc"])
            S.dve(lambda e, G_=G_: e.tensor_tensor(out=bgc[:], in0=bgc[:], in1=G_[:, 0:8], op=ALU.mult), r=["bgc", gk_], w=["bgc"])
            S.dve(lambda e: e.tensor_tensor(out=edc[:], in0=gls[:], in1=gcs[:], op=ALU.subtract), r=["gls", "gcs"], w=["edc"])
            S.act(lambda e: e.activation(out=edc[:], in_=edc[:], func=AF.Exp), r=["edc"], w=["edc"])
            S.act(lambda e: e.activation(out=egl[:], in_=gls[:], func=AF.Exp), r=["gls"], w=["egl"])
            if os.environ.get('G2ST', 'E') == '0':
                continue
            for h in range(NH):
                pr, kr = self.bank()
                S.pe(lambda e, G_=G_, h=h, pr=pr: e.matmul(pr[:, 0:128], lhsT=G_[:, 8 + h:9 + h].to_broadcast([128, 128]), rhs=self.MU[:], start=True, stop=True),
                     r=[gk_, "MU"], w=[kr])
                gci = gcs[:, h:h + 1]
                S.dve(lambda e, h=h, pr=pr, gci=gci: e.tensor_scalar(out=T1[:, h, :], in0=pr[:, 0:128], scalar1=gci, scalar2=0.0, op0=ALU.subtract, op1=ALU.max),
                      r=[kr, "gcs"], w=[("T1", h)])
                S.act(lambda e, h=h: e.activation(out=T1[:, h, :], in_=T1[:, h, :], func=AF.Exp, scale=-1.0), r=[("T1", h)], w=[("T1", h)])
                S.dve(lambda e, h=h, G_=G_: e.scalar_tensor_tensor(out=M1[:, h, :], in0=T1[:, h, :], scalar=G_[:, h:h + 1], in1=self.SLn[:], op0=ALU.mult, op1=ALU.mult),
                      r=[("T1", h), gk_, "SLn"], w=[("M1", h)])
                S.dve(lambda e, h=h, pr=pr, gci=gci: e.tensor_scalar(out=M2[:, h, :], in0=pr[:, 0:128], scalar1=gci, scalar2=0.0, op0=ALU.subtract, op1=ALU.min),
                      r=[kr, "gcs"], w=[("M2", h)])
                S.act(lambda e, h=h: e.activation(out=M2[:, h, :], in_=M2[:, h, :], func=AF.Exp), r=[("M2", h)], w=[("M2", h)])
                S.pool(lambda e, h=h: e.tensor_tensor(out=M2[:, h, :], in0=M2[:, h, :], in1=self.MU[:], op=ALU.mult), r=[("M2", h), "MU"], w=[("M2", h)])
                S.act(lambda e, h=h, pr=pr: e.activation(out=EG[:, h, :], in_=pr[:, 0:128], func=AF.Exp), r=[kr], w=[("EG", h)])
            if os.environ.get('G2ST', 'E') == 'A':
                continue
            for h in range(NH):
                kT = Q[:, 8 + h, :]
                pg, kg = self.bank()
                S.pe(lambda e, kT=kT, pg=pg: e.matmul(pg[:, 0:128], lhsT=kT, rhs=kT, start=True, stop=True), r=[qk_], w=[kg])
                S.dve(lambda e, h=h, pg=pg: e.tensor_tensor(out=X0f[:, h, :], in0=pg[:, 0:128], in1=M1[:, h, :], op=ALU.mult), r=[kg, ("M1", h)], w=[("Xa", 0, h)])
                pt, kt = self.bank()
                S.pe(lambda e, h=h, pt=pt: e.transpose(pt[:, 0:128], X0f[:, h, :], self.ident[:]), r=[("Xa", 0, h), "ident"], w=[kt])
                S.act(lambda e, h=h, pt=pt: e.copy(out=Ya[0][:, h, :], in_=pt[:, 0:128]), r=[kt], w=[("Ya", 0, h)])
                S.dve(lambda e, h=h, pt=pt: e.tensor_tensor(out=Pa[0][:, h, :], in0=pt[:, 0:128], in1=self.ident[:], op=ALU.add), r=[kt, "ident"], w=[("Pa", 0, h)])
            if os.environ.get('G2ST', 'E') == 'B':
                continue
            for lv in range(6):
                a, bq = lv % 2, (lv + 1) % 2
                for h in range(NH):
                    px, kx = self.bank()
                    S.pe(lambda e, h=h, a=a, px=px: e.matmul(px[:, 0:128], lhsT=Ya[a][:, h, :], rhs=Xa[a][:, h, :], start=True, stop=True),
                         r=[("Xa", a, h), ("Ya", a, h)], w=[kx])
                    S.act(lambda e, h=h, bq=bq, px=px: e.copy(out=Xa[bq][:, h, :], in_=px[:, 0:128]), r=[kx], w=[("Xa", bq, h)])
                    if lv < 5:
                        py, ky = self.bank()
                        S.pe(lambda e, h=h, a=a, py=py: e.matmul(py[:, 0:128], lhsT=Xa[a][:, h, :], rhs=Ya[a][:, h, :], start=True, stop=True),
                             r=[("Xa", a, h), ("Ya", a, h)], w=[ky])
                        S.dve(lambda e, h=h, bq=bq, py=py: e.tensor_copy(out=Ya[bq][:, h, :], in_=py[:, 0:128]), r=[ky], w=[("Ya", bq, h)])
                    pp, kp_ = self.bank()
                    S.pe(lambda e, h=h, a=a, bq=bq, pp=pp: e.matmul(pp[:, 0:128], lhsT=Xa[bq][:, h, :], rhs=Pa[a][:, h, :], start=True, stop=True),
                         r=[("Xa", bq, h), ("Pa", a, h)], w=[kp_])
                    S.dve(lambda e, h=h, a=a, bq=bq, pp=pp: e.tensor_tensor(out=Pa[bq][:, h, :], in0=pp[:, 0:128], in1=Pa[a][:, h, :], op=ALU.add),
                          r=[kp_, ("Pa", a, h)], w=[("Pa", bq, h)])
            if os.environ.get('G2ST', 'E') == 'C':
                continue
            PTf = Pa[0]
            ptk = lambda h: ("Pa", 0, h)
            for h in range(NH):
                kT, vT, qT = Q[:, 8 + h, :], Q[:, 16 + h, :], Q[:, h, :]
                pk_, kk = self.bank()
                S.pe(lambda e, kT=kT, pk_=pk_: e.matmul(pk_[:, 0:128], lhsT=kT, rhs=self.ident[:], start=True, stop=True), r=[qk_, "ident"], w=[kk])
                S.act(lambda e, h=h, pk_=pk_: e.activation(out=KBG[:, h, :], in_=pk_[:, 0:128], func=AF.Copy, scale=bgc[:, h:h + 1]), r=[kk, "bgc"], w=[("KBG", h)])
                S.dve(lambda e, h=h, pk_=pk_: e.tensor_scalar(out=KD[:, h, :], in0=pk_[:, 0:128], scalar1=edc[:, h:h + 1], scalar2=None, op0=ALU.mult), r=[kk, "edc"], w=[("KD", h)])
                pv, kv = self.bank()
                S.pe(lambda e, vT=vT, pv=pv: e.matmul(pv[:, 0:128], lhsT=vT, rhs=self.ident[:], start=True, stop=True), r=[qk_, "ident"], w=[kv])
                S.act(lambda e, h=h, pv=pv, G_=G_: e.activation(out=VB[:, h, :], in_=pv[:, 0:128], func=AF.Copy, scale=G_[:, h:h + 1]), r=[kv, gk_], w=[("VB", h)])
                pw, kw = self.bank()
                S.pe(lambda e, h=h, pw=pw: e.matmul(pw[:, 0:128], lhsT=KBG[:, h, :], rhs=PTf[:, h, :], start=True, stop=True), r=[("KBG", h), ptk(h)], w=[kw])
                S.act(lambda e, h=h, pw=pw: e.activation(out=WTN[:, h, :], in_=pw[:, 0:128], func=AF.Copy, scale=-1.0), r=[kw], w=[("WTN", h)])
                S.dve(lambda e, h=h, qT=qT: e.tensor_tensor(out=QG[:, h, :], in0=qT, in1=EG[:, h, :], op=ALU.mult), r=[qk_, ("EG", h)], w=[("QG", h)])
                pq, kq_ = self.bank()
                S.pe(lambda e, kT=kT, qT=qT, pq=pq: e.matmul(pq[:, 0:128], lhsT=kT, rhs=qT, start=True, stop=True), r=[qk_], w=[kq_])
                S.dve(lambda e, h=h, pq=pq: e.tensor_tensor(out=PM[:, h, :], in0=pq[:, 0:128], in1=M2[:, h, :], op=ALU.mult), r=[kq_, ("M2", h)], w=[("PM", h)])
            if os.environ.get('G2ST', 'E') == 'D':
                continue
            oc_ = och[cb]
            for h in range(NH):
                pn, kn = self.bank()
                S.pe(lambda e, h=h, pn=pn: e.matmul(pn[:, 0:128], lhsT=PTf[:, h, :], rhs=VB[:, h, :], start=True, stop=False), r=[ptk(h), ("VB", h)], w=[kn])
                S.pe(lambda e, h=h, pn=pn: e.matmul(pn[:, 0:128], lhsT=WTN[:, h, :], rhs=Sb[:, h, :], start=False, stop=True), r=[("WTN", h), ("Sf", h)], w=[kn])
                S.act(lambda e, h=h, pn=pn: e.copy(out=VN[:, h, :], in_=pn[:, 0:128]), r=[kn], w=[("VN", h)])
                po, ko = self.bank()
                S.pe(lambda e, h=h, po=po: e.matmul(po[:, 0:128], lhsT=Sb[:, h, :], rhs=QG[:, h, :], start=True, stop=False), r=[("Sf", h), ("QG", h)], w=[ko])
                S.pe(lambda e, h=h, po=po: e.matmul(po[:, 0:128], lhsT=VN[:, h, :], rhs=PM[:, h, :], start=False, stop=True), r=[("VN", h), ("PM", h)], w=[ko])
                S.act(lambda e, h=h, po=po, oc_=oc_: e.copy(out=oc_[:, h, :], in_=po[:, 0:128]), r=[ko], w=[("och", cb, h)])
                ps_, ks_ = self.bank()
                S.pe(lambda e, h=h, ps_=ps_: e.matmul(ps_[:, 0:128], lhsT=KD[:, h, :], rhs=VN[:, h, :], start=True, stop=True), r=[("KD", h), ("VN", h)], w=[ks_])
                S.dve(lambda e, h=h, ps_=ps_: e.scalar_tensor_tensor(out=Sf[:, h, :], in0=Sf[:, h, :], scalar=egl[:, h:h + 1], in1=ps_[:, 0:128], op0=ALU.mult, op1=ALU.add),
                      r=[ks_, ("Sf", h), "egl"], w=[("Sf", h)])
            S.dma(lambda e, oc_=oc_, t0=t0, nv=nv: e.dma_start(out=self.oT[:, :, t0:t0 + nv].rearrange("c p t -> p c t"), in_=oc_[:, :, 0:nv]),
                  r=[("och", cb, h) for h in range(NH)], w=[("oT", t0)])
        S.dma(lambda e: e.dma_start(out=O["gdn_S_prompt"][j].rearrange("h k v -> k h v"), in_=Sf[:]), r=[("Sf", h) for h in range(NH)], w=["o_gsp"])

        if int(os.environ.get('GDN_STOP', '9')) <= 2:
            return
        self.phase()
        qsf = self.sb("qsf", [128, 24, NS], F32)
        gbs = self.sb("gbs", [128, NS * 16], F32)
        egs = self.sb("egs", [128, NS * 16], F32)
        S0 = [self.sb("S0%d" % i, [128, 128], F32) for i in range(2)]
        S1 = [self.sb("S1%d" % i, [128, 128], F32) for i in range(2)]
        dcol = [self.sb("dcol%d" % i, [128, 1], F32) for i in range(2)]
        krow = [self.sb("krow%d" % i, [1, 128], F32) for i in range(2)]
        drow = [self.sb("drow%d" % i, [1, 128], F32) for i in range(2)]
        osm = self.sb("osm", [128, NH, NS], F32)
        S.dma(lambda e: e.dma_start(out=qsf[:], in_=self.qkvF[:, :, T:T + NS].rearrange("c p t -> p c t")), w=["qsf"])
        S.dma(lambda e: e.dma_start(out=gbs[:], in_=self.gb[T:T + NS, :].rearrange("(o s) c -> o (s c)", o=1).to_broadcast([128, NS * 16])), w=["gbs"])
        S.act(lambda e: e.activation(out=egs[:], in_=gbs[:], func=AF.Exp), r=["gbs"], w=["egs"])
        it = 0
        for s_ in range(NS):
            for h in range(NH):
                b = it % 2
                it += 1
                s0, s1 = S0[b], S1[b]
                S.dma(lambda e, s0=s0, s_=s_, h=h: e.dma_start(out=s0[:], in_=I["state_gdn_S"][j, s_, h]), w=[("S0", b)])
                egc_ = egs[:, s_ * 16 + 8 + h:s_ * 16 + 9 + h]
                bec_ = gbs[:, s_ * 16 + h:s_ * 16 + h + 1]
                S.dve(lambda e, s0=s0, egc_=egc_: e.tensor_scalar(out=s0[:], in0=s0[:], scalar1=egc_, scalar2=None, op0=ALU.mult), r=[("S0", b), "egs"], w=[("S0", b)])
                kcol = qsf[:, 8 + h, s_:s_ + 1]
                vcol = qsf[:, 16 + h, s_:s_ + 1]
                qcol = qsf[:, h, s_:s_ + 1]
                pa, ka = self.bank()
                S.pe(lambda e, s0=s0, kcol=kcol, pa=pa: e.matmul(pa[:, 0:1], lhsT=s0[:], rhs=kcol, start=True, stop=True), r=[("S0", b), "qsf"], w=[ka])
                S.dve(lambda e, b=b, vcol=vcol, pa=pa: e.tensor_tensor(out=dcol[b][:], in0=vcol, in1=pa[:, 0:1], op=ALU.subtract), r=[ka, "qsf"], w=[("dcol", b)])
                S.dve(lambda e, b=b, bec_=bec_: e.tensor_tensor(out=dcol[b][:], in0=dcol[b][:], in1=bec_, op=ALU.mult), r=[("dcol", b), "gbs"], w=[("dcol", b)])
                pr1, kr1 = self.bank()
                pr2, kr2 = self.bank()
                S.pe(lambda e, kcol=kcol, pr1=pr1: e.matmul(pr1[0:1, 0:128], lhsT=kcol, rhs=self.ident[:], start=True, stop=True), r=["qsf", "ident"], w=[kr1])
                S.pe(lambda e, b=b, pr2=pr2: e.matmul(pr2[0:1, 0:128], lhsT=dcol[b][:], rhs=self.ident[:], start=True, stop=True), r=[("dcol", b), "ident"], w=[kr2])
                S.act(lambda e, b=b, pr1=pr1: e.copy(out=krow[b][:], in_=pr1[0:1, 0:128]), r=[kr1], w=[("krow", b)])
                S.act(lambda e, b=b, pr2=pr2: e.copy(out=drow[b][:], in_=pr2[0:1, 0:128]), r=[kr2], w=[("drow", b)])
                pu, ku = self.bank()
                S.pe(lambda e, b=b, pu=pu: e.matmul(pu[:, 0:128], lhsT=krow[b][:], rhs=drow[b][:], start=True, stop=True), r=[("krow", b), ("drow", b)], w=[ku])
                S.dve(lambda e, s0=s0, s1=s1, pu=pu: e.tensor_tensor(out=s1[:], in0=s0[:], in1=pu[:, 0:128], op=ALU.add), r=[ku, ("S0", b)], w=[("S1", b)])
                S.dma(lambda e, s1=s1, s_=s_, h=h: e.dma_start(out=O["gdn_S_sample"][j, s_, h], in_=s1[:]), r=[("S1", b)], w=[("o_gss", s_, h)])
                po, ko = self.bank()
                S.pe(lambda e, s1=s1, qcol=qcol, po=po: e.matmul(po[:, 0:1], lhsT=s1[:], rhs=qcol, start=True, stop=True), r=[("S1", b), "qsf"], w=[ko])
                S.act(lambda e, h=h, s_=s_, po=po: e.copy(out=osm[:, h, s_:s_ + 1], in_=po[:, 0:1]), r=[ko], w=[("osm", h, s_)])
        S.dma(lambda e: e.dma_start(out=self.oT[:, :, T:T + NS].rearrange("c p t -> p c t"), in_=osm[:]),
              r=[("osm", h, s_) for h in range(NH) for s_ in range(NS)], w=["oTs"])

        if int(os.environ.get('GDN_STOP', '9')) <= 3:
            return
        self.phase()
        Wo = self.sb("Wo", [128, 8, D], BF16)
        self.load_w(Wo, I["gdn_w_out"][j], 8, "Wo")
        wok = [("Wo", k) for k in range(8)]
        nw = self.sb("nw", [128, 1], F32)
        S.dma(lambda e: e.dma_start(out=nw[:], in_=I["gdn_norm_w"][j].rearrange("(p o) -> p o", o=1)), w=["nw"])
        szl = self.sb("szl", [128, 8, 512], F32)
        sq = self.sb("gsq", [128, 8, 512], BF16)
        rv = [self.sb("grv%d" % i, [128, 512], F32) for i in range(2)]

        def gate(o, okeys, c0, N):
            S.dma(lambda e: e.dma_start(out=szl[:, :, 0:N], in_=self.szT[:, :, c0:c0 + N].rearrange("c p t -> p c t")), w=["szl"])
            S.pool(lambda e: e.tensor_tensor(out=sq[:, :, 0:N], in0=o[:, :, 0:N], in1=o[:, :, 0:N], op=ALU.mult), r=okeys, w=["gsq"])
            for c in range(8):
                pm, km = self.bank()
                rr = rv[c % 2]
                S.pe(lambda e, c=c, pm=pm: e.matmul(pm[:, 0:N], lhsT=self.ones_bf[:], rhs=sq[:, c, 0:N], start=True, stop=True), r=["gsq", "ones_bf"], w=[km])
                S.act(lambda e, pm=pm, rr=rr: e.activation(out=rr[:, 0:N], in_=pm[:, 0:N], func=AF.Ln, scale=1.0 / 128, bias=float(RMS_EPS)), r=[km], w=[("grv", c % 2)])
                S.act(lambda e, rr=rr: e.activation(out=rr[:, 0:N], in_=rr[:, 0:N], func=AF.Exp, scale=-0.5), r=[("grv", c % 2)], w=[("grv", c % 2)])
                S.dve(lambda e, c=c, rr=rr: e.scalar_tensor_tensor(out=o[:, c, 0:N], in0=o[:, c, 0:N], scalar=nw[:, 0:1], in1=rr[:, 0:N], op0=ALU.mult, op1=ALU.mult),
                      r=list(okeys) + ["gsq", ("grv", c % 2), "nw"], w=[("ogc", c)])
                S.pool(lambda e, c=c: e.tensor_tensor(out=o[:, c, 0:N], in0=o[:, c, 0:N], in1=szl[:, c, 0:N], op=ALU.mult), r=[("ogc", c), "szl"], w=[("ogc", c)])
            S.pool(lambda e: e.engine_nop() if False else e.memset(rv[0][:, 0:1], 0.0), r=[("ogc", c) for c in range(8)] + [("grv", 0)], w=["og", ("grv", 0)])
        self.out_proj_phase(Wo, wok, li * 3 + 1, gate=gate, samples_tm=False)

    def gather_page(self, dst, cache, idxv, col, key):
        rows = cache.rearrange("n p d -> (n p) d")
        self.S.dma(lambda e: e.indirect_dma_start(out=dst[:], out_offset=None, in_=rows,
                                                  in_offset=bass.IndirectOffsetOnAxis(ap=idxv[:, col:col + 1], axis=0)),
                   r=["idxv"], w=[key], q="pool")

    def sbattn(self, li):
        cfg, S, I, O = self.cfg, self.S, self.I, self.O
        T, TT = cfg.T, cfg.TT
        scale = 128.0 ** -0.5
        self.regs = {}
        self.phase()
        Wq = self.sb("Wq", [128, 8, 3 * D], BF16)
        self.load_w(Wq, I["sb_w_qkv"], 8, "Wq")
        wk = [("Wq", k) for k in range(8)]
        x = self.sb("x", [128, 8, 512], F32)
        xb = self.sb("xb", [128, 8, 512], BF16)
        qk = self.sb("qk", [128, 16, 512], BF16)
        tok = [self.sb("tok%d" % i, [128, D], F32) for i in range(3)]
        vbf = self.sb("vbf", [128, D], BF16)
        for (c0, N) in self.tiles():
            self.load_x(x, xb, c0, N)
            for oc in range(16):
                pb, pk = self.bank()
                for kc in range(8):
                    S.pe(lambda e, oc=oc, kc=kc, pb=pb, N=N: e.matmul(pb[:, 0:N], lhsT=Wq[:, kc, oc * 128:(oc + 1) * 128], rhs=xb[:, kc, 0:N],
                                                                       start=(kc == 0), stop=(kc == 7)), r=["xb"] + wk, w=[pk])
                if oc % 2:
                    S.act(lambda e, oc=oc, pb=pb, N=N: e.copy(out=qk[:, oc, 0:N], in_=pb[:, 0:N]), r=[pk], w=[("qk", oc)])
                else:
                    S.dve(lambda e, oc=oc, pb=pb, N=N: e.tensor_copy(out=qk[:, oc, 0:N], in_=pb[:, 0:N]), r=[pk], w=[("qk", oc)])
            S.dma(lambda e, c0=c0, N=N: e.dma_start(out=self.qkvT[0:16, :, c0:c0 + N].rearrange("c p t -> p c t"), in_=qk[:, :, 0:N]),
                  r=[("qk", oc) for oc in range(16)], w=[("qkvT", c0)])
            for j0 in range(0, N, 128):
                n = min(128, N - j0)
                g0 = c0 + j0
                npr = max(0, min(n, T - g0))
                has_s = g0 + n > T
                for which in ((1, 2, 0) if has_s else (1, 2)):
                    tk = tok[which]
                    for half in range(2):
                        pb, pk = self.bank()
                        for kc in range(8):
                            S.pe(lambda e, which=which, half=half, kc=kc, pb=pb, j0=j0, n=n: e.matmul(
                                pb[0:n, 0:512], lhsT=xb[:, kc, j0:j0 + n], rhs=Wq[:, kc, which * D + half * 512:which * D + (half + 1) * 512],
                                start=(kc == 0), stop=(kc == 7)), r=["xb"] + wk, w=[pk])
                        if half:
                            S.act(lambda e, tk=tk, pb=pb, n=n: e.copy(out=tk[0:n, 512:1024], in_=pb[0:n, 0:512]), r=[pk], w=[("tok", which, 1)])
                        else:
                            S.dve(lambda e, tk=tk, pb=pb, n=n: e.tensor_copy(out=tk[0:n, 0:512], in_=pb[0:n, 0:512]), r=[pk], w=[("tok", which, 0)])
                tkk = lambda w_: [("tok", w_, 0), ("tok", w_, 1)]
                if npr > 0:
                    S.dma(lambda e, g0=g0, npr=npr: e.dma_start(out=O["sb_k_prompt"][g0:g0 + npr, :], in_=tok[1][0:npr, :]), r=tkk(1), w=[("okp", g0)])
                    S.dma(lambda e, g0=g0, npr=npr: e.dma_start(out=O["sb_v_prompt"][g0:g0 + npr, :], in_=tok[2][0:npr, :]), r=tkk(2), w=[("ovp", g0)])
                    S.pool(lambda e, npr=npr: e.tensor_copy(out=vbf[0:npr, :], in_=tok[2][0:npr, :]), r=tkk(2), w=["vbf"])
                    S.dma(lambda e, g0=g0, npr=npr: e.dma_start(out=self.vtok[g0:g0 + npr, :], in_=vbf[0:npr, :]), r=["vbf"], w=[("vtok", g0)])
                if has_s:
                    S.dma(lambda e, npr=npr: e.dma_start(out=O["sb_k_sample"], in_=tok[1][npr:npr + NS, :]), r=tkk(1), w=["oks"])
                    S.dma(lambda e, npr=npr: e.dma_start(out=O["sb_v_sample"], in_=tok[2][npr:npr + NS, :]), r=tkk(2), w=["ovs"])
                    S.dma(lambda e, npr=npr: e.dma_start(out=self.qs, in_=tok[0][npr:npr + NS, :]), r=tkk(0), w=["qs"])
        self.phase()
        nch = cfg.nchunk
        biasb = self.sb("biasb", [128, NH], F32)
        S.dma(lambda e: e.dma_start(out=biasb[:], in_=I["sb_logit_bias"].to_broadcast([128, NH])), w=["biasb"])
        sbmask = self.sb("sbmask", [128, 4, 512], F32)
        for d in range(4):
            S.pool(lambda e, d=d: e.affine_select(out=sbmask[:, d, :], in_=self.ones_f[:], pattern=[[1, 512]], compare_op=ALU.is_gt,
                                                  fill=0.0, base=-128 * d, channel_multiplier=-1), r=[], w=[("sbmask", d)])
        mkeys = [("sbmask", d) for d in range(4)]
        qh = [self.sb("qh%d" % i, [128, cfg.TP], BF16) for i in range(2)]
        kh = [self.sb("kh%d" % i, [128, cfg.TP], BF16) for i in range(2)]
        vh = [self.sb("vh%d" % i, [128, nch, 128], BF16) for i in range(2)]
        ez = [self.sb("ez%d" % i, [128, 512], F32) for i in range(2)]
        spf = self.sb("spf", [128, 512], F32)
        spall = [self.sb("spall%d" % i, [128, nch, 512], BF16) for i in range(2)]
        tmp = [self.sb("tmp%d" % i, [128, 512], F32) for i in range(2)]
        wf = self.sb("wf", [128, 512], F32)
        wb = [self.sb("wb%d" % i, [128, 512], BF16) for i in range(2)]
        racc = self.sb("racc", [128, 512], F32)
        ot = [self.sb("ot%d" % i, [128, 512], F32) for i in range(2)]
        nvt = T // 128
        rem = T - nvt * 128
        it = 0
        qn = 0
        for h in range(NH):
            hb = h % 2
            S.dma(lambda e, h=h, hb=hb: e.dma_start(out=qh[hb][:, 0:T], in_=self.qkvT[h, :, 0:T]), w=[("qh", hb)])
            S.dma(lambda e, h=h, hb=hb: e.dma_start(out=kh[hb][:, 0:T], in_=self.qkvT[8 + h, :, 0:T]), w=[("kh", hb)])
            if nvt:
                S.dma(lambda e, h=h, hb=hb: e.dma_start(out=vh[hb][:, 0:nvt, :], in_=self.vtok[0:nvt * 128, h * 128:(h + 1) * 128].rearrange("(t p) d -> p t d", p=128)),
                      w=[("vh", hb, 0)])
            if rem:
                S.dma(lambda e, h=h, hb=hb: e.dma_start(out=vh[hb][0:rem, nvt, :], in_=self.vtok[nvt * 128:T, h * 128:(h + 1) * 128]), w=[("vh", hb, 1)])
            vkeys = [("vh", hb, 0), ("vh", hb, 1)]
            bcol = biasb[:, h:h + 1]
            for c0 in range(0, T, 512):
                Nq = min(512, T - c0)
                Q = c0 // 512
                jmax = min(nch - 1, (c0 + Nq - 2) // 128) if (c0 + Nq - 2) >= 0 else -1
                po, ko = self.ps[6 + qn % 2], ("psacc", qn % 2)
                spA = spall[qn % 2]
                for jt in range(jmax, -1, -1):
                    nk = min(128, T - jt * 128)
                    diag = (jt * 128 + nk - 1) >= c0
                    d = jt - 4 * Q
                    b = it % 2
                    it += 1
                    pz, kz = self.bank()
                    S.pe(lambda e, hb=hb, jt=jt, nk=nk, c0=c0, Nq=Nq, pz=pz: e.matmul(pz[0:nk, 0:Nq], lhsT=kh[hb][:, jt * 128:jt * 128 + nk],
                                                                                   rhs=qh[hb][:, c0:c0 + Nq], start=True, stop=True),
                         r=[("qh", hb), ("kh", hb)], w=[kz])
                    S.act(lambda e, b=b, nk=nk, Nq=Nq, pz=pz, bcol=bcol: e.activation(out=ez[b][0:nk, 0:Nq], in_=pz[0:nk, 0:Nq], func=AF.Exp,
                                                                                   scale=scale, bias=bcol[0:nk, :]), r=[kz, "biasb"], w=[("ez", b)])
                    if diag:
                        S.act(lambda e, b=b, nk=nk, Nq=Nq: e.activation(out=spf[0:nk, 0:Nq], in_=ez[b][0:nk, 0:Nq], func=AF.Ln, bias=1.0, scale=1.0),
                              r=[("ez", b)], w=["spf"])
                        S.pool(lambda e, spA=spA, jt=jt, nk=nk, Nq=Nq, d=d: e.tensor_tensor(out=spA[0:nk, jt, 0:Nq], in0=spf[0:nk, 0:Nq], in1=sbmask[0:nk, d, 0:Nq], op=ALU.mult),
                               r=["spf"] + mkeys, w=[("spA", qn % 2, jt)])
                    else:
                        S.act(lambda e, spA=spA, b=b, jt=jt, nk=nk, Nq=Nq: e.activation(out=spA[0:nk, jt, 0:Nq], in_=ez[b][0:nk, 0:Nq], func=AF.Ln, bias=1.0, scale=1.0),
                              r=[("ez", b)], w=[("spA", qn % 2, jt)])
                S.pool(lambda e, Nq=Nq: e.memset(racc[:, 0:Nq], 0.0), w=["racc"])
                for jt in range(jmax, -1, -1):
                    nk = min(128, T - jt * 128)
                    diag = (jt * 128 + nk - 1) >= c0
                    d = jt - 4 * Q
                    b = it % 2
                    it += 1
                    sk = ("spA", qn % 2, jt)
                    psuf, ks = self.bank()
                    ptot, kt = self.bank()
                    pz, kz = self.bank()
                    S.pe(lambda e, spA=spA, jt=jt, nk=nk, Nq=Nq, psuf=psuf: e.matmul(psuf[0:nk, 0:Nq], lhsT=self.ML_bf[0:nk, 0:nk], rhs=spA[0:nk, jt, 0:Nq], start=True, stop=True),
                         r=[sk, "ML_bf"], w=[ks])
                    S.pe(lambda e, spA=spA, jt=jt, nk=nk, Nq=Nq, ptot=ptot: e.matmul(ptot[:, 0:Nq], lhsT=self.ones_bf[0:nk, :], rhs=spA[0:nk, jt, 0:Nq], start=True, stop=True),
                         r=[sk, "ones_bf"], w=[kt])
                    S.pe(lambda e, hb=hb, jt=jt, nk=nk, c0=c0, Nq=Nq, pz=pz: e.matmul(pz[0:nk, 0:Nq], lhsT=kh[hb][:, jt * 128:jt * 128 + nk],
                                                                                   rhs=qh[hb][:, c0:c0 + Nq], start=True, stop=True),
                         r=[("qh", hb), ("kh", hb)], w=[kz])
                    S.dve(lambda e, b=b, nk=nk, Nq=Nq, psuf=psuf: e.tensor_tensor(out=tmp[b][0:nk, 0:Nq], in0=psuf[0:nk, 0:Nq], in1=racc[0:nk, 0:Nq], op=ALU.add),
                          r=[ks, "racc"], w=[("tmp", b)])
                    S.dve(lambda e, Nq=Nq, ptot=ptot: e.tensor_tensor(out=racc[:, 0:Nq], in0=racc[:, 0:Nq], in1=ptot[:, 0:Nq], op=ALU.add),
                          r=[kt, "racc"], w=["racc"])
                    S.dve(lambda e, b=b, nk=nk, Nq=Nq, pz=pz: e.scalar_tensor_tensor(out=tmp[b][0:nk, 0:Nq], in0=pz[0:nk, 0:Nq], scalar=scale, in1=tmp[b][0:nk, 0:Nq],
                                                                                  op0=ALU.mult, op1=ALU.subtract), r=[kz, ("tmp", b)], w=[("tmp", b)])
                    if diag:
                        S.act(lambda e, b=b, nk=nk, Nq=Nq, bcol=bcol: e.activation(out=wf[0:nk, 0:Nq], in_=tmp[b][0:nk, 0:Nq], func=AF.Exp, bias=bcol[0:nk, :], scale=1.0),
                              r=[("tmp", b), "biasb"], w=["wf"])
                        S.pool(lambda e, b=b, nk=nk, Nq=Nq, d=d: e.tensor_tensor(out=wb[b][0:nk, 0:Nq], in0=wf[0:nk, 0:Nq], in1=sbmask[0:nk, d, 0:Nq], op=ALU.mult),
                               r=["wf"] + mkeys, w=[("wb", b)])
                    else:
                        S.act(lambda e, b=b, nk=nk, Nq=Nq, bcol=bcol: e.activation(out=wb[b][0:nk, 0:Nq], in_=tmp[b][0:nk, 0:Nq], func=AF.Exp, bias=bcol[0:nk, :], scale=1.0),
                              r=[("tmp", b), "biasb"], w=[("wb", b)])
                    S.pe(lambda e, hb=hb, b=b, jt=jt, nk=nk, Nq=Nq, po=po, jmax=jmax: e.matmul(po[:, 0:Nq], lhsT=vh[hb][0:nk, jt, :], rhs=wb[b][0:nk, 0:Nq],
                                                                                           start=(jt == jmax), stop=(jt == 0)),
                         r=[("wb", b)] + vkeys, w=[ko])
                ob = ot[qn % 2]
                if jmax >= 0:
                    S.act(lambda e, ob=ob, po=po, Nq=Nq: e.copy(out=ob[:, 0:Nq], in_=po[:, 0:Nq]), r=[ko], w=[("ot", qn % 2)])
                else:
                    S.pool(lambda e, ob=ob, Nq=Nq: e.memset(ob[:, 0:Nq], 0.0), w=[("ot", qn % 2)])
                S.dma(lambda e, h=h, ob=ob, c0=c0, Nq=Nq: e.dma_start(out=self.oT[h, :, c0:c0 + Nq], in_=ob[:, 0:Nq]), r=[("ot", qn % 2)], w=[("oT", h, c0)])
                qn += 1
        self.phase()
        NP = cfg.NPAGES
        pt_sb = self.sb("pt_sb", [128, NS * NP], I32)
        ptf = self.sb("ptf", [128, NS * NP], F32)
        pidx = self.sb("pidx", [128, 1], I32)
        pidf = self.sb("pidf", [128, 1], F32)
        idxv = self.sb("idxv", [128, NS * NP], I32)
        S.dma(lambda e: e.dma_start(out=pt_sb[:], in_=I["page_table"].to_broadcast([128, NS * NP])), w=["pt_sb"])
        S.pool(lambda e: e.iota(out=pidx[:], pattern=[[0, 1]], base=0, channel_multiplier=1), w=["pidx"])
        S.dve(lambda e: e.tensor_copy(out=pidf[:], in_=pidx[:]), r=["pidx"], w=["pidf"])
        S.dve(lambda e: e.tensor_copy(out=ptf[:], in_=pt_sb[:]), r=["pt_sb"], w=["ptf"])
        S.dve(lambda e: e.tensor_scalar(out=ptf[:], in0=ptf[:], scalar1=float(PAGE), scalar2=pidf[:, 0:1], op0=ALU.mult, op1=ALU.add),
              r=["ptf", "pidf"], w=["ptf"])
        S.dve(lambda e: e.tensor_copy(out=idxv[:], in_=ptf[:]), r=["ptf"], w=["idxv"])
        biasr = self.sb("biasr", [128, NH], F32)
        S.dma(lambda e: e.dma_start(out=biasr[:], in_=I["sb_logit_bias"].to_broadcast([128, NH])), w=["biasr"])
        qb = self.sb("qb", [128, D], F32)
        kp = [self.sb("kp%d" % i, [128, D], F32) for i in range(2)]
        vp = [self.sb("vp%d" % i, [128, D], F32) for i in range(2)]
        kq = self.sb("kq", [128, D], F32)
        zt = [self.sb("zt%d" % i, [128, NH], F32) for i in range(2)]
        et = [self.sb("et%d" % i, [128, NH], F32) for i in range(2)]
        st_ = [self.sb("st%d" % i, [128, NH], F32) for i in range(2)]
        at = [self.sb("at%d" % i, [128, NH], F32) for i in range(2)]
        wt = [self.sb("wt%d" % i, [128, NH], F32) for i in range(2)]
        rs = self.sb("rs", [128, NH], F32)
        osb = self.sb("osb", [NH, D], F32)
        ML_f = self.sb("ML_f", [128, 128], F32)
        S.pool(lambda e: e.affine_select(out=ML_f[:], in_=self.ones_f[:, 0:128], pattern=[[-1, 128]], compare_op=ALU.is_ge, fill=0.0,
                                         base=0, channel_multiplier=1), w=["ML_f"])
        it = 0
        for s_ in range(NS):
            S.dma(lambda e, s_=s_: e.dma_start(out=qb[:], in_=self.qs[s_:s_ + 1, :].to_broadcast([128, D])), w=["qb"])
            S.pool(lambda e: e.memset(rs[:], 0.0), w=["rs"])
            poa, koa = self.ps[6], ("psacc", 0)
            pob, kob = self.ps[7], ("psacc", 1)
            for pg in range(NP - 1, -1, -1):
                b = it % 2
                it += 1
                self.gather_page(kp[b], I["cache_k"], idxv, s_ * NP + pg, ("kp", b))
                self.gather_page(vp[b], I["cache_v"], idxv, s_ * NP + pg, ("vp", b))
                S.dve(lambda e, b=b: e.tensor_tensor(out=kq[:], in0=kp[b][:], in1=qb[:], op=ALU.mult), r=[("kp", b), "qb"], w=["kq"])
                S.dve(lambda e, b=b: e.tensor_reduce(out=zt[b][:], in_=kq[:].rearrange("p (h d) -> p h d", h=NH), axis=AX.X, op=ALU.add),
                      r=["kq"], w=[("zt", b)])
                S.dve(lambda e, b=b: e.scalar_tensor_tensor(out=zt[b][:], in0=zt[b][:], scalar=scale, in1=biasr[:], op0=ALU.mult, op1=ALU.add),
                      r=[("zt", b), "biasr"], w=[("zt", b)])
                S.act(lambda e, b=b: e.activation(out=et[b][:], in_=zt[b][:], func=AF.Exp), r=[("zt", b)], w=[("et", b)])
                S.act(lambda e, b=b: e.activation(out=st_[b][:], in_=et[b][:], func=AF.Ln, bias=1.0, scale=1.0), r=[("et", b)], w=[("st", b)])
                psuf, ks = self.bank()
                ptot, kt = self.bank()
                S.pe(lambda e, b=b, psuf=psuf: e.matmul(psuf[:, 0:NH], lhsT=ML_f[:], rhs=st_[b][:], start=True, stop=True), r=[("st", b), "ML_f"], w=[ks])
                S.pe(lambda e, b=b, ptot=ptot: e.matmul(ptot[:, 0:NH], lhsT=self.ones_f[:, 0:128], rhs=st_[b][:], start=True, stop=True), r=[("st", b), "ones_f"], w=[kt])
                S.dve(lambda e, b=b, psuf=psuf: e.tensor_tensor(out=at[b][:], in0=psuf[:, 0:NH], in1=rs[:], op=ALU.add), r=[ks, "rs"], w=[("at", b)])
                S.dve(lambda e, b=b: e.tensor_tensor(out=at[b][:], in0=zt[b][:], in1=at[b][:], op=ALU.subtract), r=[("zt", b), ("at", b)], w=[("at", b)])
                S.act(lambda e, b=b: e.activation(out=wt[b][:], in_=at[b][:], func=AF.Exp), r=[("at", b)], w=[("wt", b)])
                S.dve(lambda e, ptot=ptot: e.tensor_tensor(out=rs[:], in0=rs[:], in1=ptot[:, 0:NH], op=ALU.add), r=[kt, "rs"], w=["rs"])
                S.pe(lambda e, b=b, pg=pg: e.matmul(poa[0:NH, 0:512], lhsT=wt[b][:], rhs=vp[b][:, 0:512], start=(pg == NP - 1), stop=(pg == 0)),
                     r=[("wt", b), ("vp", b)], w=[koa])
                S.pe(lambda e, b=b, pg=pg: e.matmul(pob[0:NH, 0:512], lhsT=wt[b][:], rhs=vp[b][:, 512:1024], start=(pg == NP - 1), stop=(pg == 0)),
                     r=[("wt", b), ("vp", b)], w=[kob])
            S.act(lambda e: e.copy(out=osb[:, 0:512], in_=poa[0:NH, 0:512]), r=[koa], w=[("osb", 0)])
            S.act(lambda e: e.copy(out=osb[:, 512:1024], in_=pob[0:NH, 0:512]), r=[kob], w=[("osb", 1)])
            for h in range(NH):
                S.dma(lambda e, s_=s_, h=h: e.dma_start(out=self.qs[s_:s_ + 1, h * 128:(h + 1) * 128], in_=osb[h:h + 1, h * 128:(h + 1) * 128]),
                      r=[("osb", 0), ("osb", 1), "qb"], w=[("qs", s_, h)])
        self.phase()
        Wo = self.sb("Wo", [128, 8, D], BF16)
        self.load_w(Wo, I["sb_w_out"], 8, "Wo")
        wok = [("Wo", k) for k in range(8)]
        self.out_proj_phase(Wo, wok, li * 3 + 1)

    def out_proj_phase(self, Wo, wok, lrow, gate=None, samples_tm=True):
        cfg, S = self.cfg, self.S
        T = cfg.T
        x = self.sb("x", [128, 8, 512], F32)
        o = self.sb("o", [128, 8, 512], F32)
        ob = self.sb("ob", [128, 8, 512], BF16)
        t2 = [self.sb("t2%d" % i, [128, 512], F32) for i in range(2)]
        rb, mean, msq, rstd = self.ln_bufs()
        ost = self.sb("ost", [NS, D], F32)
        for (c0, N) in self.tiles():
            S.dma(lambda e, c0=c0, N=N: e.dma_start(out=x[:, :, 0:N], in_=self.hT[:, :, c0:c0 + N].rearrange("c p t -> p c t")), w=["x"])
            npc = min(N, T - c0) if samples_tm else N
            S.dma(lambda e, c0=c0, npc=npc: e.dma_start(out=o[:, :, 0:npc], in_=self.oT[:, :, c0:c0 + npc].rearrange("c p t -> p c t")), w=["o"])
            okeys = ["o"]
            if samples_tm and c0 + N > T:
                S.dma(lambda e: e.dma_start(out=ost[:], in_=self.qs), w=["ost"])
                okeys += self.tm_to_fm(ost, NS, lambda c, npc=npc: o[:, c, npc:npc + NS], 8, ["ost", "o"], "osmp")
            if gate is not None:
                gate(o, okeys, c0, N)
                okeys = ["og"]
            S.pool(lambda e, N=N: e.tensor_copy(out=ob[:, :, 0:N], in_=o[:, :, 0:N]), r=okeys, w=["ob"])
            self.resid_ln_store(x, t2, N, c0, lrow, Wo, wok, ob, ["ob"], rb, mean, msq, rstd)

    def fm_to_tm(self, srcfn, n, dst, nchunks, skeys, dkey):
        S = self.S
        for c in range(nchunks):
            pb, pk = self.bank()
            S.pe(lambda e, c=c, pb=pb: e.transpose(pb[0:n, 0:128], srcfn(c), self.ident[:]), r=list(skeys) + ["ident"], w=[pk])
            if c % 2:
                S.act(lambda e, c=c, pb=pb: e.copy(out=dst[0:n, c * 128:(c + 1) * 128], in_=pb[0:n, 0:128]), r=[pk], w=[(dkey, c)])
            else:
                S.dve(lambda e, c=c, pb=pb: e.tensor_copy(out=dst[0:n, c * 128:(c + 1) * 128], in_=pb[0:n, 0:128]), r=[pk], w=[(dkey, c)])
        return [(dkey, c) for c in range(nchunks)]

    def tm_to_fm(self, src, n, dstfn, nchunks, skeys, dkey):
        S = self.S
        for c in range(nchunks):
            pb, pk = self.bank()
            S.pe(lambda e, c=c, pb=pb: e.transpose(pb[:, 0:n], src[0:n, c * 128:(c + 1) * 128], self.ident[0:n, 0:n]),
                 r=list(skeys) + ["ident"], w=[pk])
            if c % 2:
                S.act(lambda e, c=c, pb=pb: e.copy(out=dstfn(c), in_=pb[:, 0:n]), r=[pk], w=[(dkey, c)])
            else:
                S.dve(lambda e, c=c, pb=pb: e.tensor_copy(out=dstfn(c), in_=pb[:, 0:n]), r=[pk], w=[(dkey, c)])
        return [(dkey, c) for c in range(nchunks)]

    def resid_ln_store(self, x, t2, N, c0, lrow, W, wkeys, rhs, rhskeys, rb, mean, msq, rstd, biasrow=None):
        S = self.S
        for oc in range(8):
            pd, kd = self.bank()
            for kc in range(8):
                S.pe(lambda e, oc=oc, kc=kc, pd=pd: e.matmul(pd[:, 0:N], lhsT=W[:, kc, oc * 128:(oc + 1) * 128], rhs=rhs[:, kc, 0:N],
                                                            start=(kc == 0), stop=(kc == 7)), r=list(rhskeys) + list(wkeys), w=[kd])
            tt = t2[oc % 2]
            if biasrow is None:
                S.act(lambda e, pd=pd, tt=tt: e.copy(out=tt[:, 0:N], in_=pd[:, 0:N]), r=[kd], w=[("t2", oc % 2)])
            else:
                S.act(lambda e, pd=pd, tt=tt, oc=oc: e.activation(out=tt[:, 0:N], in_=pd[:, 0:N], func=AF.Identity,
                                                                  bias=self.pcol(biasrow, oc), scale=1.0), r=[kd], w=[("t2", oc % 2)])
            S.dve(lambda e, oc=oc, tt=tt: e.scalar_tensor_tensor(out=x[:, oc, 0:N], in0=x[:, oc, 0:N], scalar=float(ALPHA),
                                                                 in1=tt[:, 0:N], op0=ALU.mult, op1=ALU.add),
                  r=["x", ("t2", oc % 2)], w=[("r", oc)])
        self.ln_inplace(x, N, lrow, rb, mean, msq, rstd)
        S.dma(lambda e: e.dma_start(out=self.hT[:, :, c0:c0 + N].rearrange("c p t -> p c t"), in_=x[:, :, 0:N]),
              r=[("r", c) for c in range(8)], w=["x"])

    def ln_bufs(self):
        return (self.sb("ln_rb", [128, 8, 512], BF16), self.sb("ln_mean", [128, 512], F32),
                self.sb("ln_msq", [128, 512], F32), self.sb("ln_rstd", [128, 512], F32))

    def conformer(self, li):
        cfg, S, I, O = self.cfg, self.S, self.I, self.O
        T, TT = cfg.T, cfg.TT
        uT = self.uT
        self.phase()
        W1 = self.sb("W1", [128, 8, 2 * D], BF16)
        self.load_w(W1, I["cv_w_pw1"], 8, "W1")
        w1k = [("W1", k) for k in range(8)]
        x = self.sb("x", [128, 8, 512], F32)
        xb = self.sb("xb", [128, 8, 512], BF16)
        ga = self.sb("ga", [128, 8, 512], F32)
        gs = [self.sb("gs%d" % i, [128, 512], F32) for i in range(2)]
        for (c0, N) in self.tiles():
            self.load_x(x, xb, c0, N)
            for oc in range(8):
                pa, ka = self.bank()
                pg, kg = self.bank()
                for kc in range(8):
                    S.pe(lambda e, oc=oc, kc=kc, pa=pa, N=N: e.matmul(pa[:, 0:N], lhsT=W1[:, kc, oc * 128:(oc + 1) * 128], rhs=xb[:, kc, 0:N],
                                                                       start=(kc == 0), stop=(kc == 7)), r=["xb"] + w1k, w=[ka])
                for kc in range(8):
                    S.pe(lambda e, oc=oc, kc=kc, pg=pg, N=N: e.matmul(pg[:, 0:N], lhsT=W1[:, kc, D + oc * 128:D + (oc + 1) * 128], rhs=xb[:, kc, 0:N],
                                                                       start=(kc == 0), stop=(kc == 7)), r=["xb"] + w1k, w=[kg])
                g = gs[oc % 2]
                S.act(lambda e, oc=oc, pg=pg, g=g, N=N: e.activation(out=g[:, 0:N], in_=pg[:, 0:N], func=AF.Sigmoid,
                                                                      bias=self.pcol(PR_BP1 + 1, oc), scale=1.0), r=[kg], w=[("gs", oc % 2)])
                S.dve(lambda e, oc=oc, pa=pa, g=g, N=N: e.scalar_tensor_tensor(out=ga[:, oc, 0:N], in0=pa[:, 0:N], scalar=self.pcol(PR_BP1, oc),
                                                                               in1=g[:, 0:N], op0=ALU.add, op1=ALU.mult),
                      r=[ka, ("gs", oc % 2)], w=[("ga", oc)])
            S.dma(lambda e, c0=c0, N=N: e.dma_start(out=uT[:, :, c0:c0 + N].rearrange("c p t -> p c t"), in_=ga[:, :, 0:N]),
                  r=[("ga", oc) for oc in range(8)], w=[("uT", c0)])
        self.phase()
        W2 = self.sb("W2", [128, 8, D], BF16)
        self.load_w(W2, I["cv_w_pw2"], 8, "W2")
        w2k = [("W2", k) for k in range(8)]
        H = CW - 1
        ub = self.sb("ub", [128, 8, H + 512], F32)
        ubb = self.sb("ubb", [128, 8, H + 512], BF16)
        dg = self.sb("dg", [128, 8 * CW, 128], BF16)
        for c in range(8):
            for i in range(CW):
                if (c * CW + i) % 2:
                    S.act(lambda e, c=c, i=i: e.activation(out=dg[:, c * CW + i, :], in_=self.ident[:], func=AF.Copy, scale=self.PT[:, c, PR_DWW + i:PR_DWW + i + 1]),
                          r=["ident"], w=[("dg", c)])
                else:
                    S.dve(lambda e, c=c, i=i: e.tensor_scalar(out=dg[:, c * CW + i, :], in0=self.ident[:], scalar1=self.PT[:, c, PR_DWW + i:PR_DWW + i + 1], scalar2=None, op0=ALU.mult),
                          r=["ident"], w=[("dg", c)])
        acc = self.sb("acc", [128, 8, 512], F32)
        db = self.sb("db", [128, 8, 512], BF16)
        x = self.sb("x", [128, 8, 512], F32)
        t2 = [self.sb("t2%d" % i, [128, 512], F32) for i in range(2)]
        rb, mean, msq, rstd = self.ln_bufs()
        cst = self.sb("cst", [NS * H, D], F32)
        ctxT = self.sb("ctxT", [128, 8, NS * H], F32)
        prod = self.sb("prod", [128, H], F32)
        red = self.sb("red", [128, 8, NS], F32)
        tmo = self.sb("tmo", [H, D], F32)
        tms = self.sb("tms", [NS, D], F32)
        S.dma(lambda e: e.dma_start(out=cst[:], in_=I["state_conv"].rearrange("s r d -> (s r) d")), w=["cst"])
        ckeys = self.tm_to_fm(cst, NS * H, lambda c: ctxT[:, c, :], 8, ["cst"], "ctxT")
        wcol = lambda c, i: self.PT[:, c, PR_DWW + i:PR_DWW + i + 1]
        for (c0, N) in self.tiles():
            if c0 == 0:
                S.pool(lambda e: e.memset(ub[:, :, 0:H], 0.0), w=["ub"])
                S.dma(lambda e, N=N: e.dma_start(out=ub[:, :, H:H + N], in_=uT[:, :, 0:N].rearrange("c p t -> p c t")), r=["ub"], w=["ub"])
            else:
                S.dma(lambda e, c0=c0, N=N: e.dma_start(out=ub[:, :, 0:H + N], in_=uT[:, :, c0 - H:c0 + N].rearrange("c p t -> p c t")), w=["ub"])
            S.dma(lambda e, c0=c0, N=N: e.dma_start(out=x[:, :, 0:N], in_=self.hT[:, :, c0:c0 + N].rearrange("c p t -> p c t")), w=["x"])
            S.pool(lambda e, N=N: e.tensor_copy(out=ubb[:, :, 0:H + N], in_=ub[:, :, 0:H + N]), r=["ub"], w=["ubb"])
            for c in range(8):
                pdw, kdw = self.bank()
                for i in range(CW):
                    S.pe(lambda e, c=c, i=i, N=N, pdw=pdw: e.matmul(pdw[:, 0:N], lhsT=dg[:, c * CW + i, :], rhs=ubb[:, c, i:i + N],
                                                                     start=(i == 0), stop=(i == CW - 1)), r=["ubb", ("dg", c)], w=[kdw])
                if c % 2:
                    S.act(lambda e, c=c, N=N, pdw=pdw: e.activation(out=acc[:, c, 0:N], in_=pdw[:, 0:N], func=AF.Identity, bias=self.pcol(PR_DWB, c), scale=1.0),
                          r=[kdw], w=[("r", c)])
                else:
                    S.dve(lambda e, c=c, N=N, pdw=pdw: e.tensor_scalar(out=acc[:, c, 0:N], in0=pdw[:, 0:N], scalar1=self.pcol(PR_DWB, c), scalar2=None, op0=ALU.add),
                          r=[kdw], w=[("r", c)])
            if c0 + N > T:
                sc = T - c0
                for c in range(8):
                    for s_ in range(NS):
                        S.dve(lambda e, c=c, s_=s_: e.tensor_tensor(out=prod[:], in0=ctxT[:, c, s_ * H:(s_ + 1) * H],
                                                                    in1=self.PT[:, c, PR_DWW:PR_DWW + H], op=ALU.mult),
                              r=ckeys, w=["prod"])
                        S.dve(lambda e, c=c, s_=s_: e.tensor_reduce(out=red[:, c, s_:s_ + 1], in_=prod[:], axis=AX.X, op=ALU.add),
                              r=["prod"], w=[("red", c)])
                    S.dve(lambda e, c=c: e.scalar_tensor_tensor(out=acc[:, c, sc:sc + NS], in0=ub[:, c, H + sc:H + sc + NS], scalar=wcol(c, H),
                                                                in1=red[:, c, :], op0=ALU.mult, op1=ALU.add),
                          r=["ub", ("red", c), ("r", c)], w=[("r", c)])
                    S.dve(lambda e, c=c: e.tensor_scalar(out=acc[:, c, sc:sc + NS], in0=acc[:, c, sc:sc + NS], scalar1=self.pcol(PR_DWB, c),
                                                         scalar2=None, op0=ALU.add), r=[("r", c)], w=[("r", c)])
            self.ln_inplace(acc, N, 0, rb, mean, msq, rstd, growfn=lambda c: self.pcol(PR_CLG, c), browfn=lambda c: self.pcol(PR_CLB, c))
            S.act(lambda e, N=N: e.activation(out=db[:, :, 0:N], in_=acc[:, :, 0:N], func=AF.Silu), r=[("r", c) for c in range(8)], w=["db"])
            self.resid_ln_store(x, t2, N, c0, li * 3 + 1, W2, w2k, db, ["db"], rb, mean, msq, rstd, biasrow=PR_BP2)
        ul = self.sb("ul", [128, 8, H + NS], F32)
        S.dma(lambda e: e.dma_start(out=ul[:, :, :], in_=uT[:, :, T - H:T + NS].rearrange("c p t -> p c t")), w=["ul"])
        k1 = self.fm_to_tm(lambda c: ul[:, c, 0:H], H, tmo, 8, ["ul"], "tmo")
        S.dma(lambda e: e.dma_start(out=O["conv_prompt"], in_=tmo[:]), r=k1, w=["o_cp"])
        k2 = self.fm_to_tm(lambda c: ul[:, c, H:H + NS], NS, tms, 8, ["ul"], "tms")
        S.dma(lambda e: e.dma_start(out=O["conv_sample"][:, H - 1, :], in_=tms[:]), r=k2, w=["o_cs"])
        for s_ in range(NS):
            S.dma(lambda e, s_=s_: e.dma_start(out=O["conv_sample"][s_, 0:H - 1, :], in_=I["state_conv"][s_, 1:H, :]), w=[("o_cs2", s_)])

    def final_out(self):
        cfg, S = self.cfg, self.S
        self.phase()
        dsts = []
        for j in range(cfg.SEQ // 128):
            dsts.append((self.O["y_prompt"], j * 128, 128, NMETA + j * 128))
        dsts.append((self.O["y_sample"], 0, NS, cfg.T))
        xi = [self.sb("fx%d" % i, [128, 8, 128], F32) for i in range(2)]
        yo = [self.sb("fy%d" % i, [128, D], F32) for i in range(2)]
        for n, (dst, r0, nr, c0) in enumerate(dsts):
            b = n % 2
            S.dma(lambda e, b=b, c0=c0, nr=nr: e.dma_start(out=xi[b][:, :, 0:nr],
                                                           in_=self.hT[:, :, c0:c0 + nr].rearrange("c p t -> p c t")),
                  w=[("fx", b)])
            for c in range(8):
                pb, pk = self.bank()
                S.pe(lambda e, b=b, c=c, nr=nr, pb=pb: e.transpose(pb[0:nr, 0:128], xi[b][:, c, 0:nr], self.ident[:]),
                     r=[("fx", b), "ident"], w=[pk])
                if c % 2:
                    S.act(lambda e, b=b, c=c, nr=nr, pb=pb: e.copy(out=yo[b][0:nr, c * 128:(c + 1) * 128], in_=pb[0:nr, 0:128]),
                          r=[pk], w=[("fy", b, c)])
                else:
                    S.dve(lambda e, b=b, c=c, nr=nr, pb=pb: e.tensor_copy(out=yo[b][0:nr, c * 128:(c + 1) * 128], in_=pb[0:nr, 0:128]),
                          r=[pk], w=[("fy", b, c)])
            S.dma(lambda e, b=b, dst=dst, r0=r0, nr=nr: e.dma_start(out=dst[r0:r0 + nr, :], in_=yo[b][0:nr, :]),
                  r=[("fy", b, c) for c in range(8)], w=[("yout", n)])


def make_pvec(inp):
    rows = [np.asarray(inp["ln_g"], np.float32).reshape(12, D), np.asarray(inp["ln_b"], np.float32).reshape(12, D),
            np.asarray(inp["cv_dw_w"], np.float32).reshape(CW, D), np.asarray(inp["cv_dw_b"], np.float32).reshape(1, D),
            np.asarray(inp["cv_ln_g"], np.float32).reshape(1, D), np.asarray(inp["cv_ln_b"], np.float32).reshape(1, D),
            np.asarray(inp["cv_b_pw2"], np.float32).reshape(1, D), np.asarray(inp["cv_b_pw1"], np.float32).reshape(2, D)]
    return np.ascontiguousarray(np.concatenate(rows, 0))


def make_in_maps(inp, cfg):
    f = lambda a: np.ascontiguousarray(np.asarray(a))
    pvec = make_pvec(inp)
    shared = {
        "meta_tokens": f(inp["meta_tokens"]), "pvec": pvec,
        "ffn_w_gate": f(inp["ffn_w_gate"]), "ffn_w_up": f(inp["ffn_w_up"]), "ffn_w_down": f(inp["ffn_w_down"]),
        "gdn_w_in": f(inp["gdn_w_in"]), "gdn_conv_w": f(inp["gdn_conv_w"]).reshape(8, GQKV),
        "gdn_a_log": f(inp["gdn_a_log"]), "gdn_dt_bias": f(inp["gdn_dt_bias"]), "gdn_norm_w": f(inp["gdn_norm_w"]),
        "gdn_w_out": f(inp["gdn_w_out"]), "sb_w_qkv": f(inp["sb_w_qkv"])[0], "sb_w_out": f(inp["sb_w_out"])[0],
        "sb_logit_bias": f(inp["sb_logit_bias"]), "cv_w_pw1": f(inp["cv_w_pw1"])[0], "cv_w_pw2": f(inp["cv_w_pw2"])[0],
        "cache_k": f(inp["cache_sb_k"])[0].reshape(cfg.NPHYS, PAGE, D),
        "cache_v": f(inp["cache_sb_v"])[0].reshape(cfg.NPHYS, PAGE, D),
    }
    maps = []
    for c in range(cfg.n_cores):
        b = c // 2
        s0 = c * NS
        m = dict(shared)
        m["x_prompt"] = f(inp["x_prompt"][b])
        m["x_sample"] = f(inp["x_sample"][s0:s0 + NS, 0])
        m["state_gdn_conv"] = f(inp["state_gdn_conv"][:, s0:s0 + NS]).reshape(2, NS * 3, GQKV)
        m["state_gdn_S"] = f(inp["state_gdn_S"][:, s0:s0 + NS])
        m["state_conv"] = f(inp["state_conv"][0, s0:s0 + NS])
        m["page_table"] = f(inp["page_table"][s0:s0 + NS]).reshape(1, NS * cfg.NPAGES).astype(np.int32)
        maps.append(m)
    return maps


def assemble(results, cfg):
    nb = cfg.n_cores // 2
    even = [results[2 * b] for b in range(nb)]
    allc = results
    y_prompt = np.stack([r["y_prompt"] for r in even])
    y_sample = np.concatenate([r["y_sample"] for r in allc])[:, None, :]
    gdn_S_prompt = np.stack([r["gdn_S_prompt"] for r in even], 1)
    gdn_S_sample = np.concatenate([r["gdn_S_sample"] for r in allc], 1)
    gdn_conv_prompt = np.stack([r["gdn_conv_prompt"] for r in even], 1)
    gdn_conv_sample = np.concatenate([r["gdn_conv_sample"] for r in allc], 1)
    sb_k_prompt = np.stack([r["sb_k_prompt"].reshape(cfg.T, NH, 128) for r in even])[None]
    sb_v_prompt = np.stack([r["sb_v_prompt"].reshape(cfg.T, NH, 128) for r in even])[None]
    sb_k_sample = np.concatenate([r["sb_k_sample"].reshape(NS, 1, NH, 128) for r in allc])[None]
    sb_v_sample = np.concatenate([r["sb_v_sample"].reshape(NS, 1, NH, 128) for r in allc])[None]
    conv_prompt = np.stack([r["conv_prompt"] for r in even])[None]
    conv_sample = np.concatenate([r["conv_sample"] for r in allc])[None]
    outs = (y_prompt, y_sample, gdn_S_prompt, gdn_S_sample, gdn_conv_prompt, gdn_conv_sample,
            sb_k_prompt, sb_v_prompt, sb_k_sample, sb_v_sample, conv_prompt, conv_sample)
    return tuple(np.ascontiguousarray(o, dtype=np.float32) for o in outs)


def run(inp, n_cores=8, stop_after=99, only=None):
    seq = inp["x_prompt"].shape[1]
    npages = inp["page_table"].shape[1]
    nphys = inp["cache_sb_k"].shape[1]
    cfg = Cfg(seq, npages, nphys, n_cores)
    bld = Builder(cfg, stop_after, only)
    nc = bld.build()
    maps = make_in_maps(inp, cfg)
    res = run_bass_kernel_spmd(nc, maps, core_ids=list(range(n_cores)))
    return assemble(res.results, cfg)


def kernel(**inputs):
    return run(inputs, 8)
```

```python
import contextlib
import os
import numpy as np
import concourse.bass as bass
import concourse.mybir as mybir
from concourse.bass_utils import run_bass_kernel_spmd

F32 = mybir.dt.float32
BF16 = mybir.dt.bfloat16
I32 = mybir.dt.int32
AF = mybir.ActivationFunctionType
ALU = mybir.AluOpType
AX = mybir.AxisListType

ENGS = ("pe", "act", "dve", "pool", "sp")
N_LANES = 8


class Op:
    __slots__ = ("eng", "fn", "deps", "dma", "idx", "sig", "lane", "lane_val", "need_sig")

    def __init__(self, eng, fn, dma):
        self.eng = eng
        self.fn = fn
        self.dma = dma
        self.deps = set()
        self.idx = -1
        self.sig = 0
        self.lane = None
        self.lane_val = 0
        self.need_sig = False


class Sched:
    def __init__(self, same_engine_sync=True):
        self.ops = []
        self.lastw = {}
        self.readers = {}
        self.same_engine_sync = same_engine_sync
        self.last_by_eng = {e: None for e in ENGS}
        self.lane_last = {}
        self.lane_rr = {e: 0 for e in ENGS}
        self.lane_cnt = {}
        self.bar_deps = set()
        self.bar_seen = {e: True for e in ENGS}

    def add(self, eng, fn, r=(), w=(), dma=False):
        op = Op(eng, fn, dma)
        deps = op.deps
        for k in r:
            lw = self.lastw.get(k)
            if lw is not None:
                deps.add(lw)
            if isinstance(k, tuple) and k[0] in ("ps", "psacc"):
                for rd in self.readers.get(k, ()):
                    if rd.eng != eng:
                        deps.add(rd)
        for k in w:
            lw = self.lastw.get(k)
            if lw is not None:
                deps.add(lw)
            for rd in self.readers.get(k, ()):
                deps.add(rd)
        for k in r:
            self.readers.setdefault(k, []).append(op)
        for k in w:
            self.lastw[k] = op
            self.readers[k] = []
        if not self.bar_seen[eng]:
            deps.update(self.bar_deps)
            self.bar_seen[eng] = True
        if dma:
            lane = (eng, self.lane_rr[eng] % N_LANES)
            self.lane_rr[eng] += 1
            prev = self.lane_last.get(lane)
            if prev is not None:
                deps.add(prev)
            self.lane_last[lane] = op
            c = self.lane_cnt.get(lane, 0) + 1
            self.lane_cnt[lane] = c
            op.lane = lane
            op.lane_val = 16 * c
        else:
            self.last_by_eng[eng] = op
        deps.discard(op)
        self.ops.append(op)
        return op

    def barrier(self):
        d = set()
        for e in ENGS:
            if self.last_by_eng[e] is not None:
                d.add(self.last_by_eng[e])
        for lane, op in self.lane_last.items():
            d.add(op)
        self.bar_deps = d
        self.bar_seen = {e: False for e in ENGS}
        self.lastw = {}
        self.readers = {}

    def pe(self, fn, r=(), w=()):
        return self.add("pe", fn, r, w)

    def act(self, fn, r=(), w=()):
        return self.add("act", fn, r, w)

    def dve(self, fn, r=(), w=()):
        return self.add("dve", fn, r, w)

    def pool(self, fn, r=(), w=()):
        return self.add("pool", fn, r, w)

    def dma(self, fn, r=(), w=(), q="sp"):
        return self.add(q, fn, r, w, dma=True)

    def emit(self, nc, stack):
        ops = self.ops
        per_eng = {e: [] for e in ENGS}
        for op in ops:
            op.idx = len(per_eng[op.eng])
            per_eng[op.eng].append(op)
        ses = self.same_engine_sync
        for op in ops:
            for d in op.deps:
                if d.dma:
                    continue
                if d.eng != op.eng or op.dma or (ses and d.eng != "pe"):
                    d.need_sig = True
        for e in ENGS:
            c = 0
            for op in per_eng[e]:
                if not op.dma and op.need_sig:
                    c += 1
                    op.sig = c
        eng_sem = {e: stack.enter_context(nc.semaphore("es_" + e)) for e in ENGS}
        lane_sem = {}
        for lane in self.lane_cnt:
            lane_sem[lane] = stack.enter_context(nc.semaphore("ls_%s_%d" % lane))
        block = stack.enter_context(nc.Block())

        def run(e, eng):
            waited = {}
            for op in per_eng[e]:
                need = {}
                for d in op.deps:
                    if d.dma:
                        s = lane_sem[d.lane]
                        v = d.lane_val
                    else:
                        if not d.need_sig:
                            continue
                        if d.eng == e and not op.dma and not (ses and e != "pe"):
                            continue
                        s = eng_sem[d.eng]
                        v = d.sig
                    key = id(s)
                    if waited.get(key, 0) >= v:
                        continue
                    if key not in need or need[key][1] < v:
                        need[key] = (s, v)
                for key, (s, v) in need.items():
                    eng.wait_ge(s, v)
                    waited[key] = v
                ins = op.fn(eng)
                if op.dma:
                    ins.then_inc(lane_sem[op.lane], 16)
                elif op.need_sig:
                    ins.then_inc(eng_sem[e], 1)
            if e == "sp":
                for lane, c in self.lane_cnt.items():
                    eng.wait_ge(lane_sem[lane], 16 * c)

        @block.tensor
        def _(eng):
            run("pe", eng)

        @block.scalar
        def _(eng):
            run("act", eng)

        @block.vector
        def _(eng):
            run("dve", eng)

        @block.gpsimd
        def _(eng):
            run("pool", eng)

        @block.sync
        def _(eng):
            run("sp", eng)

        return {e: len(per_eng[e]) for e in ENGS}


D = 1024
DFF = 2816
NFC = DFF // 128
DEPTH = 4
ALPHA = (2 * DEPTH) ** 0.25
LN_EPS = 1e-5
RMS_EPS = 1e-6
NH = 8
GQKV = 3072
GIN = 4112
NMETA = 16
CW = 31
PAGE = 128
NS = 4
SB_BASE = 16640
SB_END = 229000

PR_LNG = 0
PR_LNB = 12
PR_DWW = 24
PR_DWB = 55
PR_CLG = 56
PR_CLB = 57
PR_BP2 = 58
PR_BP1 = 59
PR_N = 61


class Cfg:
    def __init__(self, seq, npages, nphys, n_cores):
        self.SEQ = seq
        self.T = seq + NMETA
        self.TT = self.T + NS
        self.NPAGES = npages
        self.NPHYS = nphys
        self.n_cores = n_cores
        self.ntile = (self.TT + 511) // 512
        self.nchunk = (self.T + 127) // 128
        self.TP = self.nchunk * 128


class Builder:
    def __init__(self, cfg, stop_after=99, only=None):
        self.cfg = cfg
        self.stop_after = stop_after
        self.only = only
        self.nc = bass.Bass("TRN2", target_bir_lowering=False)
        self.S = Sched(same_engine_sync=(os.environ.get('SES', '1') == '1'))
        self.uid = 0
        self.persist_end = SB_BASE
        self.off = SB_BASE
        self.bank_rr = 0

    def sb(self, name, shape, dt, persist=False):
        nbytes = int(np.prod(shape[1:])) * (2 if dt == BF16 else 4)
        nbytes = (nbytes + 63) // 64 * 64
        self.uid += 1
        t = self.nc.alloc_sbuf_tensor_at("%s_%d" % (name, self.uid), list(shape), dt, offset=self.off)
        self.off += nbytes
        assert self.off <= SB_END, ("SBUF overflow", name, self.off)
        if persist:
            self.persist_end = self.off
        return t

    def sb_alias(self, name, shape, dt, other_off):
        self.uid += 1
        return self.nc.alloc_sbuf_tensor_at("%s_%d" % (name, self.uid), list(shape), dt, offset=other_off)

    def phase(self):
        self.S.barrier()
        self.off = self.persist_end

    def bank(self):
        i = self.bank_rr % 6
        self.bank_rr += 1
        return self.ps[i], ("ps", i)

    def dram_in(self, name, shape, dt=F32):
        return self.nc.dram_tensor(name, list(shape), dt, kind="ExternalInput").ap()

    def dram_out(self, name, shape, dt=F32):
        return self.nc.dram_tensor(name, list(shape), dt, kind="ExternalOutput").ap()

    def dram_tmp(self, name, shape, dt=F32):
        return self.nc.dram_tensor(name, list(shape), dt, kind="Internal").ap()

    def build(self):
        cfg, nc, S = self.cfg, self.nc, self.S
        T, TT = cfg.T, cfg.TT
        I = {}
        I["x_prompt"] = self.dram_in("x_prompt", [cfg.SEQ, D])
        I["x_sample"] = self.dram_in("x_sample", [NS, D])
        I["state_gdn_conv"] = self.dram_in("state_gdn_conv", [2, NS * 3, GQKV])
        I["state_gdn_S"] = self.dram_in("state_gdn_S", [2, NS, NH, 128, 128])
        I["cache_k"] = self.dram_in("cache_k", [cfg.NPHYS, PAGE, D])
        I["cache_v"] = self.dram_in("cache_v", [cfg.NPHYS, PAGE, D])
        I["state_conv"] = self.dram_in("state_conv", [NS, CW - 1, D])
        I["page_table"] = self.dram_in("page_table", [1, NS * cfg.NPAGES], I32)
        I["meta_tokens"] = self.dram_in("meta_tokens", [NMETA, D])
        I["pvec"] = self.dram_in("pvec", [PR_N, D])
        I["ffn_w_gate"] = self.dram_in("ffn_w_gate", [DEPTH, 2, D, DFF])
        I["ffn_w_up"] = self.dram_in("ffn_w_up", [DEPTH, 2, D, DFF])
        I["ffn_w_down"] = self.dram_in("ffn_w_down", [DEPTH, 2, DFF, D])
        I["gdn_w_in"] = self.dram_in("gdn_w_in", [2, D, GIN])
        I["gdn_conv_w"] = self.dram_in("gdn_conv_w", [8, GQKV])
        I["gdn_a_log"] = self.dram_in("gdn_a_log", [2, NH])
        I["gdn_dt_bias"] = self.dram_in("gdn_dt_bias", [2, NH])
        I["gdn_norm_w"] = self.dram_in("gdn_norm_w", [2, 128])
        I["gdn_w_out"] = self.dram_in("gdn_w_out", [2, D, D])
        I["sb_w_qkv"] = self.dram_in("sb_w_qkv", [D, 3 * D])
        I["sb_w_out"] = self.dram_in("sb_w_out", [D, D])
        I["sb_logit_bias"] = self.dram_in("sb_logit_bias", [1, NH])
        I["cv_w_pw1"] = self.dram_in("cv_w_pw1", [D, 2 * D])
        I["cv_w_pw2"] = self.dram_in("cv_w_pw2", [D, D])
        self.I = I
        O = {}
        O["y_prompt"] = self.dram_out("y_prompt", [cfg.SEQ, D])
        O["y_sample"] = self.dram_out("y_sample", [NS, D])
        O["gdn_S_prompt"] = self.dram_out("gdn_S_prompt", [2, NH, 128, 128])
        O["gdn_S_sample"] = self.dram_out("gdn_S_sample", [2, NS, NH, 128, 128])
        O["gdn_conv_prompt"] = self.dram_out("gdn_conv_prompt", [2, 3, GQKV])
        O["gdn_conv_sample"] = self.dram_out("gdn_conv_sample", [2, NS, 3, GQKV])
        O["sb_k_prompt"] = self.dram_out("sb_k_prompt", [T, D])
        O["sb_v_prompt"] = self.dram_out("sb_v_prompt", [T, D])
        O["sb_k_sample"] = self.dram_out("sb_k_sample", [NS, D])
        O["sb_v_sample"] = self.dram_out("sb_v_sample", [NS, D])
        O["conv_prompt"] = self.dram_out("conv_prompt", [CW - 1, D])
        O["conv_sample"] = self.dram_out("conv_sample", [NS, CW - 1, D])
        self.O = O
        self.hT = self.dram_tmp("hT", [8, 128, TT])
        self.qkvT = self.dram_tmp("qkvT", [24, 128, TT], BF16)
        self.szT = self.dram_tmp("szT", [8, 128, TT])
        self.qkvF = self.dram_tmp("qkvF", [24, 128, TT])
        self.gb = self.dram_tmp("gb", [cfg.TP, 16])
        self.oT = self.dram_tmp("oT", [8, 128, TT])
        self.vtok = self.dram_tmp("vtok", [cfg.TP, D], BF16)
        self.qs = self.dram_tmp("qs", [NS, D])
        self.uT = self.dram_tmp("uT", [8, 128, TT])

        with contextlib.ExitStack() as st:
            self.ps = [st.enter_context(nc.psum_tensor("ps%d" % i, [128, 512], F32)) for i in range(8)]
            self.setup_consts()
            self.embed()
            stages = []
            for li in range(DEPTH):
                stages.append(lambda li=li: self.ffn(li, 0))
                kind = li % 3
                if kind == 0:
                    stages.append(lambda li=li: self.gdn(li, li // 3))
                elif kind == 1:
                    stages.append(lambda li=li: self.sbattn(li))
                else:
                    stages.append(lambda li=li: self.conformer(li))
                stages.append(lambda li=li: self.ffn(li, 1))
            for i, s in enumerate(stages):
                if i >= self.stop_after:
                    break
                if self.only is not None and i not in self.only:
                    continue
                s()
            self.final_out()
            self.counts = S.emit(nc, st)
        return nc

    def setup_consts(self):
        nc, S = self.nc, self.S
        self.ones_f = self.sb("ones_f", [128, 512], F32, True)
        self.ident = self.sb("ident", [128, 128], F32, True)
        self.ident_bf = self.sb("ident_bf", [128, 128], BF16, True)
        self.ones_bf = self.sb("ones_bf", [128, 128], BF16, True)
        self.MU = self.sb("MU", [128, 128], F32, True)
        self.MU_bf = self.sb("MU_bf", [128, 128], BF16, True)
        self.ML_bf = self.sb("ML_bf", [128, 128], BF16, True)
        self.SLn = self.sb("SLn", [128, 128], F32, True)
        self.PT = self.sb("PT", [128, 8, PR_N], F32, True)
        self.GCW = self.sb("GCW", [128, 24, 8], F32, True)
        ones_f, ident = self.ones_f, self.ident
        S.pool(lambda e: e.memset(ones_f[:], 1.0), w=["ones_f"])
        S.pool(lambda e: e.memset(self.ones_bf[:], 1.0), w=["ones_bf"])
        o128 = ones_f[:, 0:128]

        def sel(out, op, base, fill=0.0, cm=1, pat=-1, src=None):
            src = o128 if src is None else src
            return lambda e: e.affine_select(out=out, in_=src, pattern=[[pat, src.shape[-1]]], compare_op=op,
                                             fill=fill, base=base, channel_multiplier=cm)
        S.pool(sel(ident[:], ALU.is_equal, 0), r=["ones_f"], w=["ident"])
        S.pool(sel(self.ident_bf[:], ALU.is_equal, 0), r=["ones_f"], w=["ident_bf"])
        S.pool(sel(self.MU[:], ALU.is_ge, 0, cm=-1, pat=1), r=["ones_f"], w=["MU"])
        S.pool(sel(self.MU_bf[:], ALU.is_ge, 0, cm=-1, pat=1), r=["ones_f"], w=["MU_bf"])
        S.pool(sel(self.ML_bf[:], ALU.is_ge, 0, cm=1, pat=-1), r=["ones_f"], w=["ML_bf"])
        S.pool(lambda e: e.memset(self.SLn[:], -1.0), w=["SLn"])
        S.pool(sel(self.SLn[:], ALU.is_gt, 0, cm=1, pat=-1, src=self.SLn[:]), r=["SLn"], w=["SLn"])
        stg = self.sb("stg", [PR_N, D], F32)
        S.dma(lambda e: e.dma_start(out=stg[:], in_=self.I["pvec"]), w=["stg"])
        for c in range(8):
            pb, pk = self.bank()
            S.pe(lambda e, c=c, pb=pb: e.transpose(pb[:, 0:PR_N], stg[:, c * 128:(c + 1) * 128], ident[0:PR_N, 0:PR_N]),
                 r=["stg", "ident"], w=[pk])
            S.dve(lambda e, c=c, pb=pb: e.tensor_copy(out=self.PT[:, c, :], in_=pb[:, 0:PR_N]), r=[pk], w=[("PT", c)])
        stg2 = self.sb("stg2", [8, GQKV], F32)
        S.dma(lambda e: e.dma_start(out=stg2[:], in_=self.I["gdn_conv_w"]), w=["stg2"])
        for c in range(24):
            pb, pk = self.bank()
            S.pe(lambda e, c=c, pb=pb: e.transpose(pb[:, 0:8], stg2[:, c * 128:(c + 1) * 128], ident[0:8, 0:8]),
                 r=["stg2", "ident"], w=[pk])
            S.dve(lambda e, c=c, pb=pb: e.tensor_copy(out=self.GCW[:, c, :], in_=pb[:, 0:8]), r=[pk], w=[("GCW", c)])

    def pcol(self, row, c):
        return self.PT[:, c, row:row + 1]

    def embed(self):
        cfg, S = self.cfg, self.S
        self.phase()
        hT = self.hT
        srcs = [(self.I["meta_tokens"], 0, NMETA, 0)]
        for j in range(cfg.SEQ // 128):
            srcs.append((self.I["x_prompt"], j * 128, 128, NMETA + j * 128))
        srcs.append((self.I["x_sample"], 0, NS, cfg.T))
        xin = [self.sb("xin%d" % i, [128, D], F32) for i in range(2)]
        xo = [self.sb("xo%d" % i, [128, 8, 128], F32) for i in range(2)]
        for n, (src, r0, nr, c0) in enumerate(srcs):
            b = n % 2
            xi, xt = xin[b], xo[b]
            S.dma(lambda e, xi=xi, src=src, r0=r0, nr=nr: e.dma_start(out=xi[0:nr, :], in_=src[r0:r0 + nr, :]),
                  w=[("xin", b)])
            for c in range(8):
                pb, pk = self.bank()
                S.pe(lambda e, xi=xi, c=c, nr=nr, pb=pb: e.transpose(pb[:, 0:nr], xi[0:nr, c * 128:(c + 1) * 128],
                                                                      self.ident[0:nr, 0:nr]),
                     r=[("xin", b), "ident"], w=[pk])
                eng = S.act if c % 2 else S.dve
                if c % 2:
                    S.act(lambda e, xt=xt, c=c, nr=nr, pb=pb: e.copy(out=xt[:, c, 0:nr], in_=pb[:, 0:nr]),
                          r=[pk], w=[("xo", b, c)])
                else:
                    S.dve(lambda e, xt=xt, c=c, nr=nr, pb=pb: e.tensor_copy(out=xt[:, c, 0:nr], in_=pb[:, 0:nr]),
                          r=[pk], w=[("xo", b, c)])
            S.dma(lambda e, xt=xt, nr=nr, c0=c0: e.dma_start(
                out=hT[:, :, c0:c0 + nr].rearrange("c p t -> p c t"), in_=xt[:, :, 0:nr]),
                r=[("xo", b, c) for c in range(8)], w=[("hT", n)])

    def tiles(self):
        cfg = self.cfg
        return [(t * 512, min(512, cfg.TT - t * 512)) for t in range(cfg.ntile)]

    def load_x(self, x, xb, c0, N, xa=None, key="x"):
        S = self.S
        S.dma(lambda e: e.dma_start(out=x[:, :, 0:N], in_=self.hT[:, :, c0:c0 + N].rearrange("c p t -> p c t")),
              w=[key])
        S.pool(lambda e: e.tensor_copy(out=xb[:, :, 0:N], in_=x[:, :, 0:N]), r=[key], w=[key + "b"])
        if xa is not None:
            S.act(lambda e: e.activation(out=xa[:, :, 0:N], in_=x[:, :, 0:N], func=AF.Copy, scale=float(ALPHA)),
                  r=[key], w=[key + "a"])

    def load_w(self, dst, src, nk, key):
        for kc in range(nk):
            self.S.dma(lambda e, kc=kc: e.dma_start(out=dst[:, kc, :], in_=src[kc * 128:(kc + 1) * 128, :]),
                       w=[(key, kc)], q="pool")

    def store_h(self, y, c0, N, key):
        self.S.dma(lambda e: e.dma_start(out=self.hT[:, :, c0:c0 + N].rearrange("c p t -> p c t"), in_=y[:, :, 0:N]),
                   r=[(key, c) for c in range(8)], w=[("hT", c0)])

    def ffn(self, li, fi):
        S, I = self.S, self.I
        self.phase()
        Wg = self.sb("Wg", [128, 8, DFF], BF16)
        Wu = self.sb("Wu", [128, 8, DFF], BF16)
        Wd = self.sb("Wd", [128, NFC, D], BF16)
        self.load_w(Wg, I["ffn_w_gate"][li, fi], 8, "Wg")
        self.load_w(Wu, I["ffn_w_up"][li, fi], 8, "Wu")
        self.load_w(Wd, I["ffn_w_down"][li, fi], NFC, "Wd")
        wg_keys = [("Wg", k) for k in range(8)]
        wu_keys = [("Wu", k) for k in range(8)]
        wd_keys = [("Wd", k) for k in range(NFC)]
        x = self.sb("x", [128, 8, 512], F32)
        xbs = [self.sb("xb%d" % i, [128, 8, 512], BF16) for i in range(2)]
        actb = self.sb("actb", [128, NFC, 512], BF16)
        sg = [self.sb("sg%d" % i, [128, 512], F32) for i in range(2)]
        rb = actb
        mean = self.sb("ln_mean", [128, 512], F32)
        msq = self.sb("ln_msq", [128, 512], F32)
        rstd = self.sb("ln_rstd", [128, 512], F32)
        lrow = li * 3 + (0 if fi == 0 else 2)
        tl = self.tiles()

        def load_xb(ti):
            c0_, N_ = tl[ti]
            S.dma(lambda e: e.dma_start(out=xbs[ti % 2][:, :, 0:N_], in_=self.hT[:, :, c0_:c0_ + N_].rearrange("c p t -> p c t")),
                  r=[("hTt", ti)], w=[("xb", ti % 2)], q="pool")
        load_xb(0)
        for ti, (c0, N) in enumerate(tl):
            xb = xbs[ti % 2]
            xbk = ("xb", ti % 2)
            if ti + 1 < len(tl):
                load_xb(ti + 1)
            S.dma(lambda e, c0=c0, N=N: e.dma_start(out=x[:, :, 0:N], in_=self.hT[:, :, c0:c0 + N].rearrange("c p t -> p c t")),
                  r=[("hTt", ti)], w=["x"])
            for fc in range(NFC):
                pg, kg = self.bank()
                pu, ku = self.bank()
                for kc in range(8):
                    S.pe(lambda e, fc=fc, kc=kc, pg=pg, N=N, xb=xb: e.matmul(pg[:, 0:N], lhsT=Wg[:, kc, fc * 128:(fc + 1) * 128],
                                                                       rhs=xb[:, kc, 0:N], start=(kc == 0), stop=(kc == 7)),
                         r=[xbk] + wg_keys, w=[kg])
                for kc in range(8):
                    S.pe(lambda e, fc=fc, kc=kc, pu=pu, N=N, xb=xb: e.matmul(pu[:, 0:N], lhsT=Wu[:, kc, fc * 128:(fc + 1) * 128],
                                                                       rhs=xb[:, kc, 0:N], start=(kc == 0), stop=(kc == 7)),
                         r=[xbk] + wu_keys, w=[ku])
                sgi = sg[fc % 2]
                S.act(lambda e, pg=pg, sgi=sgi, N=N: e.activation(out=sgi[:, 0:N], in_=pg[:, 0:N], func=AF.Silu),
                      r=[kg], w=[("sg", fc % 2)])
                S.dve(lambda e, fc=fc, pu=pu, sgi=sgi, N=N: e.tensor_tensor(out=actb[:, fc, 0:N], in0=sgi[:, 0:N],
                                                                            in1=pu[:, 0:N], op=ALU.mult),
                      r=[("sg", fc % 2), ku], w=[("actb", fc)])
            akeys = [("actb", fc) for fc in range(NFC)]
            for oc in range(8):
                pd, kd = self.bank()
                for fc in range(NFC):
                    S.pe(lambda e, oc=oc, fc=fc, pd=pd, N=N: e.matmul(pd[:, 0:N], lhsT=Wd[:, fc, oc * 128:(oc + 1) * 128],
                                                                       rhs=actb[:, fc, 0:N], start=(fc == 0), stop=(fc == NFC - 1)),
                         r=akeys + wd_keys, w=[kd])
                S.act(lambda e, oc=oc, N=N: e.activation(out=x[:, oc, 0:N], in_=x[:, oc, 0:N], func=AF.Copy, scale=float(ALPHA)),
                      r=["x"], w=[("r", oc)])
                S.dve(lambda e, oc=oc, pd=pd, N=N: e.scalar_tensor_tensor(out=x[:, oc, 0:N], in0=pd[:, 0:N], scalar=0.5,
                                                                          in1=x[:, oc, 0:N], op0=ALU.mult, op1=ALU.add),
                      r=[kd, ("r", oc)], w=[("r", oc)])
            self.ln_inplace(x, N, lrow, rb, mean, msq, rstd, rbkeys=[("actb", fc) for fc in range(8)])
            S.dma(lambda e, c0=c0, N=N: e.dma_start(out=self.hT[:, :, c0:c0 + N].rearrange("c p t -> p c t"), in_=x[:, :, 0:N]),
                  r=[("r", c) for c in range(8)], w=["x", ("hTt", ti)])

    def ln_inplace(self, x, N, lrow, rb, mean, msq, rstd, rkey="r", growfn=None, browfn=None, rbkeys=("ln_rb",)):
        S = self.S
        rbkeys = list(rbkeys)
        rk = [(rkey, c) for c in range(8)]
        p1, k1 = self.bank()
        p2, k2 = self.bank()
        S.pool(lambda e: e.tensor_copy(out=rb[:, 0:8, 0:N], in_=x[:, :, 0:N]), r=rk, w=rbkeys)
        for c in range(8):
            S.pe(lambda e, c=c: e.matmul(p1[:, 0:N], lhsT=self.ones_bf[:], rhs=rb[:, c, 0:N], start=(c == 0), stop=(c == 7)),
                 r=rbkeys + ["ones_bf"], w=[k1])
        S.act(lambda e: e.activation(out=mean[:, 0:N], in_=p1[:, 0:N], func=AF.Copy, scale=1.0 / D), r=[k1], w=["ln_mean"])
        S.pool(lambda e: e.tensor_tensor(out=rb[:, 0:8, 0:N], in0=x[:, :, 0:N], in1=x[:, :, 0:N], op=ALU.mult),
               r=rk + rbkeys, w=rbkeys)
        for c in range(8):
            S.pe(lambda e, c=c: e.matmul(p2[:, 0:N], lhsT=self.ones_bf[:], rhs=rb[:, c, 0:N], start=(c == 0), stop=(c == 7)),
                 r=rbkeys + ["ones_bf"], w=[k2])
        S.dve(lambda e: e.tensor_tensor(out=msq[:, 0:N], in0=mean[:, 0:N], in1=mean[:, 0:N], op=ALU.mult),
              r=["ln_mean"], w=["ln_msq"])
        S.dve(lambda e: e.scalar_tensor_tensor(out=msq[:, 0:N], in0=p2[:, 0:N], scalar=1.0 / D, in1=msq[:, 0:N],
                                               op0=ALU.mult, op1=ALU.subtract), r=[k2, "ln_msq"], w=["ln_msq"])
        S.dve(lambda e: e.tensor_scalar(out=msq[:, 0:N], in0=msq[:, 0:N], scalar1=0.0, scalar2=float(LN_EPS),
                                        op0=ALU.max, op1=ALU.add), r=["ln_msq"], w=["ln_msq"])
        S.act(lambda e: e.activation(out=rstd[:, 0:N], in_=msq[:, 0:N], func=AF.Ln), r=["ln_msq"], w=["ln_rstd"])
        S.act(lambda e: e.activation(out=rstd[:, 0:N], in_=rstd[:, 0:N], func=AF.Exp, scale=-0.5), r=["ln_rstd"], w=["ln_rstd"])
        for c in range(8):
            S.dve(lambda e, c=c: e.tensor_tensor(out=x[:, c, 0:N], in0=x[:, c, 0:N], in1=mean[:, 0:N], op=ALU.subtract),
                  r=[(rkey, c), "ln_mean"] + rbkeys, w=[(rkey, c)])
            S.pool(lambda e, c=c: e.tensor_tensor(out=x[:, c, 0:N], in0=x[:, c, 0:N], in1=rstd[:, 0:N], op=ALU.mult),
                   r=[(rkey, c), "ln_rstd"], w=[(rkey, c)])
            g = self.pcol(PR_LNG + lrow, c) if growfn is None else growfn(c)
            b = self.pcol(PR_LNB + lrow, c) if browfn is None else browfn(c)
            S.act(lambda e, c=c, g=g, b=b: e.activation(out=x[:, c, 0:N], in_=x[:, c, 0:N], func=AF.Identity, scale=g, bias=b),
                  r=[(rkey, c)], w=[(rkey, c)])

    def gdn(self, li, j):
        cfg, S, I, O = self.cfg, self.S, self.I, self.O
        T, TT = cfg.T, cfg.TT
        self.phase()
        Win = self.sb("Win", [128, 8, GIN], BF16)
        self.load_w(Win, I["gdn_w_in"][j], 8, "Win")
        wk = [("Win", k) for k in range(8)]
        x = self.sb("x", [128, 8, 512], F32)
        xb = self.sb("xb", [128, 8, 512], BF16)
        prc = [self.sb("prc%d" % i, [128, 3 + 512], F32) for i in range(3)]
        cvc = [self.sb("cvc%d" % i, [128, 512], F32) for i in range(3)]
        sqb = [self.sb("sqb%d" % i, [128, 512], BF16) for i in range(3)]
        rinv = [self.sb("rinv%d" % i, [128, 512], F32) for i in range(3)]
        off_q = self.off
        qkvb = self.sb("qkvb", [128, 24, 512], F32)
        off_z = self.off
        szt = self.sb("szt", [128, 8, 512], F32)
        hal = self.sb("hal", [128, 24, 3], F32)
        ctxo = self.sb("ctxo", [128, 24, 3], F32)
        prs = self.sb("prs", [128, 24, NS], F32)
        cst = self.sb_alias("gcst", [NS * 3, GQKV], F32, off_q)
        ctxT = self.sb("gctxT", [128, 24, NS * 3], F32)
        tm3 = self.sb_alias("tm3", [3, GQKV], F32, off_q)
        tm4 = self.sb_alias("tm4", [NS, GQKV], F32, off_z)
        dtb = self.sb("dtb", [128, NH], F32)
        nea = self.sb("nea", [128, NH], F32)
        gbt = [self.sb("gbt%d" % i, [128, 16], F32) for i in range(2)]
        S.pool(lambda e: e.memset(hal[:], 0.0), w=["hal"])
        S.dma(lambda e: e.dma_start(out=dtb[:], in_=I["gdn_dt_bias"][j:j + 1, :].to_broadcast([128, NH])), w=["dtb"])
        S.dma(lambda e: e.dma_start(out=nea[:], in_=I["gdn_a_log"][j:j + 1, :].to_broadcast([128, NH])), w=["nea"])
        S.act(lambda e: e.activation(out=nea[:], in_=nea[:], func=AF.Exp), r=["nea"], w=["nea"])
        S.dve(lambda e: e.tensor_scalar(out=nea[:], in0=nea[:], scalar1=-1.0, scalar2=None, op0=ALU.mult), r=["nea"], w=["nea"])
        S.dma(lambda e: e.dma_start(out=cst[:], in_=I["state_gdn_conv"][j]), w=["gcst"])
        ckeys = self.tm_to_fm(cst, NS * 3, lambda c: ctxT[:, c, :], 24, ["gcst"], "gctxT")
        S.barrier()
        gw = lambda c, i: self.GCW[:, c, j * 4 + i:j * 4 + i + 1]
        it = 0
        for (c0, N) in self.tiles():
            self.load_x(x, xb, c0, N)
            npc = min(N, T - c0)
            has_s = c0 + N > T
            for oc in range(24):
                b = it % 3
                it += 1
                pc, cv = prc[b], cvc[b]
                pb, pk = self.bank()
                for kc in range(8):
                    S.pe(lambda e, oc=oc, kc=kc, pb=pb, N=N: e.matmul(pb[:, 0:N], lhsT=Win[:, kc, oc * 128:(oc + 1) * 128], rhs=xb[:, kc, 0:N],
                                                                       start=(kc == 0), stop=(kc == 7)), r=["xb"] + wk, w=[pk])
                S.act(lambda e, pc=pc, pb=pb, N=N: e.copy(out=pc[:, 3:3 + N], in_=pb[:, 0:N]), r=[pk], w=[("prc", b)])
                S.pool(lambda e, pc=pc, oc=oc: e.tensor_copy(out=pc[:, 0:3], in_=hal[:, oc, :]), r=[("hal", oc), "hal"], w=[("prch", b)])
                pkeys = [("prc", b), ("prch", b)]
                if not has_s:
                    S.pool(lambda e, pc=pc, oc=oc, N=N: e.tensor_copy(out=hal[:, oc, :], in_=pc[:, N:N + 3]), r=pkeys, w=[("hal", oc)])
                else:
                    S.pool(lambda e, pc=pc, oc=oc, npc=npc: e.tensor_copy(out=ctxo[:, oc, :], in_=pc[:, npc:npc + 3]), r=pkeys, w=[("ctxo", oc)])
                    S.pool(lambda e, pc=pc, oc=oc, npc=npc: e.tensor_copy(out=prs[:, oc, :], in_=pc[:, 3 + npc:3 + npc + NS]), r=pkeys, w=[("prs", oc)])
                S.dve(lambda e, pc=pc, cv=cv, oc=oc, N=N: e.tensor_scalar(out=cv[:, 0:N], in0=pc[:, 0:N], scalar1=gw(oc, 0), scalar2=None, op0=ALU.mult),
                      r=pkeys, w=[("cvc", b)])
                for i in range(1, 4):
                    S.dve(lambda e, pc=pc, cv=cv, oc=oc, i=i, N=N: e.scalar_tensor_tensor(out=cv[:, 0:N], in0=pc[:, i:i + N], scalar=gw(oc, i), in1=cv[:, 0:N],
                                                                                       op0=ALU.mult, op1=ALU.add), r=pkeys + [("cvc", b)], w=[("cvc", b)])
                if has_s:
                    cx = ctxT[:, oc, :].rearrange("p (s i) -> p s i", i=3)
                    S.dve(lambda e, pc=pc, cv=cv, oc=oc, npc=npc: e.tensor_scalar(out=cv[:, npc:npc + NS], in0=pc[:, 3 + npc:3 + npc + NS], scalar1=gw(oc, 3),
                                                                                scalar2=None, op0=ALU.mult), r=pkeys + [("cvc", b)], w=[("cvc", b)])
                    for i in range(3):
                        S.dve(lambda e, cv=cv, oc=oc, i=i, npc=npc, cx=cx: e.scalar_tensor_tensor(out=cv[:, npc:npc + NS], in0=cx[:, :, i], scalar=gw(oc, i),
                                                                                              in1=cv[:, npc:npc + NS], op0=ALU.mult, op1=ALU.add),
                              r=ckeys + [("cvc", b)], w=[("cvc", b)])
                if oc >= 16:
                    S.act(lambda e, cv=cv, oc=oc, N=N: e.activation(out=qkvb[:, oc, 0:N], in_=cv[:, 0:N], func=AF.Silu), r=[("cvc", b)], w=[("qkvb", oc)])
                else:
                    S.act(lambda e, cv=cv, N=N: e.activation(out=cv[:, 0:N], in_=cv[:, 0:N], func=AF.Silu), r=[("cvc", b)], w=[("cvc", b)])
                    S.pool(lambda e, cv=cv, b=b, N=N: e.tensor_tensor(out=sqb[b][:, 0:N], in0=cv[:, 0:N], in1=cv[:, 0:N], op=ALU.mult), r=[("cvc", b)], w=[("sqb", b)])
                    p2, k2 = self.bank()
                    S.pe(lambda e, b=b, p2=p2, N=N: e.matmul(p2[:, 0:N], lhsT=self.ones_bf[:], rhs=sqb[b][:, 0:N], start=True, stop=True), r=[("sqb", b), "ones_bf"], w=[k2])
                    S.act(lambda e, b=b, p2=p2, N=N: e.activation(out=rinv[b][:, 0:N], in_=p2[:, 0:N], func=AF.Ln, bias=float(RMS_EPS), scale=1.0), r=[k2], w=[("rinv", b)])
                    S.act(lambda e, b=b, N=N: e.activation(out=rinv[b][:, 0:N], in_=rinv[b][:, 0:N], func=AF.Exp, scale=-0.5), r=[("rinv", b)], w=[("rinv", b)])
                    cc = (128.0 ** -0.5) if oc < 8 else 1.0
                    S.dve(lambda e, cv=cv, b=b, oc=oc, N=N, cc=cc: e.scalar_tensor_tensor(out=qkvb[:, oc, 0:N], in0=cv[:, 0:N], scalar=cc, in1=rinv[b][:, 0:N],
                                                                                       op0=ALU.mult, op1=ALU.mult), r=[("cvc", b), ("rinv", b)], w=[("qkvb", oc)])
            S.dma(lambda e, c0=c0, N=N: e.dma_start(out=self.qkvF[:, :, c0:c0 + N].rearrange("c p t -> p c t"), in_=qkvb[:, :, 0:N]),
                  r=[("qkvb", oc) for oc in range(24)], w=[("qkvT", c0)])
            for oc in range(8 if 'z' in os.environ.get('G1P', 'zgo') else 0):
                pb, pk = self.bank()
                for kc in range(8):
                    S.pe(lambda e, oc=oc, kc=kc, pb=pb, N=N: e.matmul(pb[:, 0:N], lhsT=Win[:, kc, GQKV + oc * 128:GQKV + (oc + 1) * 128], rhs=xb[:, kc, 0:N],
                                                                       start=(kc == 0), stop=(kc == 7)), r=["xb"] + wk, w=[pk])
                S.act(lambda e, oc=oc, pb=pb, N=N: e.activation(out=szt[:, oc, 0:N], in_=pb[:, 0:N], func=AF.Silu), r=[pk], w=[("szt", oc)])
            if 'z' in os.environ.get('G1P', 'zgo'):
              S.dma(lambda e, c0=c0, N=N: e.dma_start(out=self.szT[:, :, c0:c0 + N].rearrange("c p t -> p c t"), in_=szt[:, :, 0:N]),
                  r=[("szt", oc) for oc in range(8)], w=[("szT", c0)])
            for j0 in range(0, N if 'g' in os.environ.get('G1P', 'zgo') else 0, 128):
                n = min(128, N - j0)
                b = it % 2
                it += 1
                gt = gbt[b]
                pb, pk = self.bank()
                for kc in range(8):
                    S.pe(lambda e, kc=kc, pb=pb, j0=j0, n=n: e.matmul(pb[0:n, 0:16], lhsT=xb[:, kc, j0:j0 + n], rhs=Win[:, kc, 4096:4112],
                                                                       start=(kc == 0), stop=(kc == 7)), r=["xb"] + wk, w=[pk])
                S.act(lambda e, gt=gt, pb=pb, n=n: e.activation(out=gt[0:n, 0:8], in_=pb[0:n, 0:8], func=AF.Sigmoid), r=[pk], w=[("gbt", b)])
                S.dve(lambda e, gt=gt, pb=pb, n=n: e.tensor_tensor(out=gt[0:n, 8:16], in0=pb[0:n, 8:16], in1=dtb[0:n, :], op=ALU.add), r=[pk, "dtb"], w=[("gbt2", b)])
                S.act(lambda e, gt=gt, n=n: e.activation(out=gt[0:n, 8:16], in_=gt[0:n, 8:16], func=AF.Exp), r=[("gbt2", b)], w=[("gbt2", b)])
                S.act(lambda e, gt=gt, n=n: e.activation(out=gt[0:n, 8:16], in_=gt[0:n, 8:16], func=AF.Ln, bias=1.0, scale=1.0), r=[("gbt2", b)], w=[("gbt2", b)])
                S.dve(lambda e, gt=gt, n=n: e.tensor_tensor(out=gt[0:n, 8:16], in0=gt[0:n, 8:16], in1=nea[0:n, :], op=ALU.mult), r=[("gbt2", b), "nea"], w=[("gbt2", b)])
                g0 = c0 + j0
                S.dma(lambda e, gt=gt, g0=g0, n=n: e.dma_start(out=self.gb[g0:g0 + n, :], in_=gt[0:n, :]), r=[("gbt", b), ("gbt2", b)], w=[("gb", g0)])
        if 'o' not in os.environ.get('G1P', 'zgo'):
            return
        S.barrier()
        k3 = self.fm_to_tm(lambda c: ctxo[:, c, :], 3, tm3, 24, [("ctxo", c) for c in range(24)], "tm3")
        S.dma(lambda e: e.dma_start(out=O["gdn_conv_prompt"][j], in_=tm3[:]), r=k3, w=["o_gcp"])
        k4 = self.fm_to_tm(lambda c: prs[:, c, :], NS, tm4, 24, [("prs", c) for c in range(24)], "tm4")
        S.dma(lambda e: e.dma_start(out=O["gdn_conv_sample"][j, :, 2, :], in_=tm4[:]), r=k4, w=["o_gcs"])
        for s_ in range(NS):
            S.dma(lambda e, s_=s_: e.dma_start(out=O["gdn_conv_sample"][j, s_, 0:2, :], in_=I["state_gdn_conv"][j, s_ * 3 + 1:s_ * 3 + 3, :]), w=[("o_gcs2", s_)])

        if int(os.environ.get('GDN_STOP', '9')) <= 1:
            return
        self.phase()
        HB = lambda name, dt: self.sb(name, [128, NH, 128], dt)
        qkv = [self.sb("cq%d" % i, [128, 24, 128], F32) for i in range(2)]
        gbc = [self.sb("gbc%d" % i, [128, 16], F32) for i in range(2)]
        gcs = self.sb("gcs", [128, NH], F32)
        gls = self.sb("gls", [128, NH], F32)
        ngc = self.sb("ngc", [128, NH], F32)
        bgc = self.sb("bgc", [128, NH], F32)
        edc = self.sb("edc", [128, NH], F32)
        egl = self.sb("egl", [128, NH], F32)
        T1 = HB("T1", F32); M1 = HB("M1", F32); M2 = HB("M2", F32); EG = HB("EG", F32)
        Xa = [HB("Xa%d" % i, F32) for i in range(2)]
        Ya = [HB("Ya%d" % i, F32) for i in range(2)]
        Pa = [HB("Pa%d" % i, F32) for i in range(2)]
        X0f = Xa[0]
        KBG = HB("KBG", F32); KD = HB("KD", F32); VB = HB("VB", F32); WTN = HB("WTN", F32)
        VN = HB("VN", F32); QG = HB("QG", F32); PM = HB("PM", F32)
        Sf = HB("Sf", F32); Sb = Sf
        och = [self.sb("och%d" % i, [128, NH, 128], F32) for i in range(2)]
        S.pool(lambda e: e.memset(Sf[:], 0.0), w=[("Sf", h) for h in range(NH)])
        nchunk = cfg.nchunk
        for c in range(nchunk):
            cb = c % 2
            t0 = c * 128
            nv = min(128, T - t0)
            Q, G_ = qkv[cb], gbc[cb]
            if nv < 128:
                S.pool(lambda e, Q=Q: e.memset(Q[:], 0.0), w=[("cq", cb)])
                S.pool(lambda e, G_=G_: e.memset(G_[:], 0.0), w=[("gbc", cb)])
            S.dma(lambda e, Q=Q, t0=t0, nv=nv: e.dma_start(out=Q[:, :, 0:nv], in_=self.qkvF[:, :, t0:t0 + nv].rearrange("c p t -> p c t")), r=[("cq", cb)], w=[("cq", cb)])
            S.dma(lambda e, G_=G_, t0=t0, nv=nv: e.dma_start(out=G_[0:nv, :], in_=self.gb[t0:t0 + nv, :]), r=[("gbc", cb)], w=[("gbc", cb)])
            qk_, gk_ = ("cq", cb), ("gbc", cb)
            p1, k1 = self.bank()
            p2, k2 = self.bank()
            S.pe(lambda e, G_=G_, p1=p1: e.matmul(p1[:, 0:NH], lhsT=self.MU[:], rhs=G_[:, 8:16], start=True, stop=True), r=[gk_, "MU"], w=[k1])
            S.pe(lambda e, G_=G_, p2=p2: e.matmul(p2[:, 0:NH], lhsT=self.ones_f[:, 0:128], rhs=G_[:, 8:16], start=True, stop=True), r=[gk_, "ones_f"], w=[k2])
            S.dve(lambda e, p1=p1: e.tensor_copy(out=gcs[:], in_=p1[:, 0:NH]), r=[k1], w=["gcs"])
            S.dve(lambda e, p2=p2: e.tensor_copy(out=gls[:], in_=p2[:, 0:NH]), r=[k2], w=["gls"])
            S.act(lambda e: e.activation(out=bgc[:], in_=gcs[:], func=AF.Exp), r=["gcs"], w=["bgc"])
            S.dve(lambda e, G_=G_: e.tensor_tensor(out=bgc[:], in0=bgc[:], in1=G_[:, 0:8], op=ALU.mult), r=["bgc", gk_], w=["bgc"])
            S.dve(lambda e: e.tensor_tensor(out=edc[:], in0=gls[:], in1=gcs[:], op=ALU.subtract), r=["gls", "gcs"], w=["edc"])
            S.act(lambda e: e.activation(out=edc[:], in_=edc[:], func=AF.Exp), r=["edc"], w=["edc"])
            S.act(lambda e: e.activation(out=egl[:], in_=gls[:], func=AF.Exp), r=["gls"], w=["egl"])
            if os.environ.get('G2ST', 'E') == '0':
                continue
            for h in range(NH):
                pr, kr = self.bank()
                S.pe(lambda e, G_=G_, h=h, pr=pr: e.matmul(pr[:, 0:128], lhsT=G_[:, 8 + h:9 + h].to_broadcast([128, 128]), rhs=self.MU[:], start=True, stop=True),
                     r=[gk_, "MU"], w=[kr])
                gci = gcs[:, h:h + 1]
                S.dve(lambda e, h=h, pr=pr, gci=gci: e.tensor_scalar(out=T1[:, h, :], in0=pr[:, 0:128], scalar1=gci, scalar2=0.0, op0=ALU.subtract, op1=ALU.max),
                      r=[kr, "gcs"], w=[("T1", h)])
                S.act(lambda e, h=h: e.activation(out=T1[:, h, :], in_=T1[:, h, :], func=AF.Exp, scale=-1.0), r=[("T1", h)], w=[("T1", h)])
                S.dve(lambda e, h=h, G_=G_: e.scalar_tensor_tensor(out=M1[:, h, :], in0=T1[:, h, :], scalar=G_[:, h:h + 1], in1=self.SLn[:], op0=ALU.mult, op1=ALU.mult),
                      r=[("T1", h), gk_, "SLn"], w=[("M1", h)])
                S.dve(lambda e, h=h, pr=pr, gci=gci: e.tensor_scalar(out=M2[:, h, :], in0=pr[:, 0:128], scalar1=gci, scalar2=0.0, op0=ALU.subtract, op1=ALU.min),
                      r=[kr, "gcs"], w=[("M2", h)])
                S.act(lambda e, h=h: e.activation(out=M2[:, h, :], in_=M2[:, h, :], func=AF.Exp), r=[("M2", h)], w=[("M2", h)])
                S.pool(lambda e, h=h: e.tensor_tensor(out=M2[:, h, :], in0=M2[:, h, :], in1=self.MU[:], op=ALU.mult), r=[("M2", h), "MU"], w=[("M2", h)])
                S.act(lambda e, h=h, pr=pr: e.activation(out=EG[:, h, :], in_=pr[:, 0:128], func=AF.Exp), r=[kr], w=[("EG", h)])
            if os.environ.get('G2ST', 'E') == 'A':
                continue
            for h in range(NH):
                kT = Q[:, 8 + h, :]
                pg, kg = self.bank()
                S.pe(lambda e, kT=kT, pg=pg: e.matmul(pg[:, 0:128], lhsT=kT, rhs=kT, start=True, stop=True), r=[qk_], w=[kg])
                S.dve(lambda e, h=h, pg=pg: e.tensor_tensor(out=X0f[:, h, :], in0=pg[:, 0:128], in1=M1[:, h, :], op=ALU.mult), r=[kg, ("M1", h)], w=[("Xa", 0, h)])
                pt, kt = self.bank()
                S.pe(lambda e, h=h, pt=pt: e.transpose(pt[:, 0:128], X0f[:, h, :], self.ident[:]), r=[("Xa", 0, h), "ident"], w=[kt])
                S.act(lambda e, h=h, pt=pt: e.copy(out=Ya[0][:, h, :], in_=pt[:, 0:128]), r=[kt], w=[("Ya", 0, h)])
                S.dve(lambda e, h=h, pt=pt: e.tensor_tensor(out=Pa[0][:, h, :], in0=pt[:, 0:128], in1=self.ident[:], op=ALU.add), r=[kt, "ident"], w=[("Pa", 0, h)])
            if os.environ.get('G2ST', 'E') == 'B':
                continue
            for lv in range(6):
                a, bq = lv % 2, (lv + 1) % 2
                for h in range(NH):
                    px, kx = self.bank()
                    S.pe(lambda e, h=h, a=a, px=px: e.matmul(px[:, 0:128], lhsT=Ya[a][:, h, :], rhs=Xa[a][:, h, :], start=True, stop=True),
                         r=[("Xa", a, h), ("Ya", a, h)], w=[kx])
                    S.act(lambda e, h=h, bq=bq, px=px: e.copy(out=Xa[bq][:, h, :], in_=px[:, 0:128]), r=[kx], w=[("Xa", bq, h)])
                    if lv < 5:
                        py, ky = self.bank()
                        S.pe(lambda e, h=h, a=a, py=py: e.matmul(py[:, 0:128], lhsT=Xa[a][:, h, :], rhs=Ya[a][:, h, :], start=True, stop=True),
                             r=[("Xa", a, h), ("Ya", a, h)], w=[ky])
                        S.dve(lambda e, h=h, bq=bq, py=py: e.tensor_copy(out=Ya[bq][:, h, :], in_=py[:, 0:128]), r=[ky], w=[("Ya", bq, h)])
                    pp, kp_ = self.bank()
                    S.pe(lambda e, h=h, a=a, bq=bq, pp=pp: e.matmul(pp[:, 0:128], lhsT=Xa[bq][:, h, :], rhs=Pa[a][:, h, :], start=True, stop=True),
                         r=[("Xa", bq, h), ("Pa", a, h)], w=[kp_])
                    S.dve(lambda e, h=h, a=a, bq=bq, pp=pp: e.tensor_tensor(out=Pa[bq][:, h, :], in0=pp[:, 0:128], in1=Pa[a][:, h, :], op=ALU.add),
                          r=[kp_, ("Pa", a, h)], w=[("Pa", bq, h)])
            if os.environ.get('G2ST', 'E') == 'C':
                continue
            PTf = Pa[0]
            ptk = lambda h: ("Pa", 0, h)
            for h in range(NH):
                kT, vT, qT = Q[:, 8 + h, :], Q[:, 16 + h, :], Q[:, h, :]
                pk_, kk = self.bank()
                S.pe(lambda e, kT=kT, pk_=pk_: e.matmul(pk_[:, 0:128], lhsT=kT, rhs=self.ident[:], start=True, stop=True), r=[qk_, "ident"], w=[kk])
                S.act(lambda e, h=h, pk_=pk_: e.activation(out=KBG[:, h, :], in_=pk_[:, 0:128], func=AF.Copy, scale=bgc[:, h:h + 1]), r=[kk, "bgc"], w=[("KBG", h)])
                S.dve(lambda e, h=h, pk_=pk_: e.tensor_scalar(out=KD[:, h, :], in0=pk_[:, 0:128], scalar1=edc[:, h:h + 1], scalar2=None, op0=ALU.mult), r=[kk, "edc"], w=[("KD", h)])
                pv, kv = self.bank()
                S.pe(lambda e, vT=vT, pv=pv: e.matmul(pv[:, 0:128], lhsT=vT, rhs=self.ident[:], start=True, stop=True), r=[qk_, "ident"], w=[kv])
                S.act(lambda e, h=h, pv=pv, G_=G_: e.activation(out=VB[:, h, :], in_=pv[:, 0:128], func=AF.Copy, scale=G_[:, h:h + 1]), r=[kv, gk_], w=[("VB", h)])
                pw, kw = self.bank()
                S.pe(lambda e, h=h, pw=pw: e.matmul(pw[:, 0:128], lhsT=KBG[:, h, :], rhs=PTf[:, h, :], start=True, stop=True), r=[("KBG", h), ptk(h)], w=[kw])
                S.act(lambda e, h=h, pw=pw: e.activation(out=WTN[:, h, :], in_=pw[:, 0:128], func=AF.Copy, scale=-1.0), r=[kw], w=[("WTN", h)])
                S.dve(lambda e, h=h, qT=qT: e.tensor_tensor(out=QG[:, h, :], in0=qT, in1=EG[:, h, :], op=ALU.mult), r=[qk_, ("EG", h)], w=[("QG", h)])
                pq, kq_ = self.bank()
                S.pe(lambda e, kT=kT, qT=qT, pq=pq: e.matmul(pq[:, 0:128], lhsT=kT, rhs=qT, start=True, stop=True), r=[qk_], w=[kq_])
                S.dve(lambda e, h=h, pq=pq: e.tensor_tensor(out=PM[:, h, :], in0=pq[:, 0:128], in1=M2[:, h, :], op=ALU.mult), r=[kq_, ("M2", h)], w=[("PM", h)])
            if os.environ.get('G2ST', 'E') == 'D':
                continue
            oc_ = och[cb]
            for h in range(NH):
                pn, kn = self.bank()
                S.pe(lambda e, h=h, pn=pn: e.matmul(pn[:, 0:128], lhsT=PTf[:, h, :], rhs=VB[:, h, :], start=True, stop=False), r=[ptk(h), ("VB", h)], w=[kn])
                S.pe(lambda e, h=h, pn=pn: e.matmul(pn[:, 0:128], lhsT=WTN[:, h, :], rhs=Sb[:, h, :], start=False, stop=True), r=[("WTN", h), ("Sf", h)], w=[kn])
                S.act(lambda e, h=h, pn=pn: e.copy(out=VN[:, h, :], in_=pn[:, 0:128]), r=[kn], w=[("VN", h)])
                po, ko = self.bank()
                S.pe(lambda e, h=h, po=po: e.matmul(po[:, 0:128], lhsT=Sb[:, h, :], rhs=QG[:, h, :], start=True, stop=False), r=[("Sf", h), ("QG", h)], w=[ko])
                S.pe(lambda e, h=h, po=po: e.matmul(po[:, 0:128], lhsT=VN[:, h, :], rhs=PM[:, h, :], start=False, stop=True), r=[("VN", h), ("PM", h)], w=[ko])
                S.act(lambda e, h=h, po=po, oc_=oc_: e.copy(out=oc_[:, h, :], in_=po[:, 0:128]), r=[ko], w=[("och", cb, h)])
                ps_, ks_ = self.bank()
                S.pe(lambda e, h=h, ps_=ps_: e.matmul(ps_[:, 0:128], lhsT=KD[:, h, :], rhs=VN[:, h, :], start=True, stop=True), r=[("KD", h), ("VN", h)], w=[ks_])
                S.dve(lambda e, h=h, ps_=ps_: e.scalar_tensor_tensor(out=Sf[:, h, :], in0=Sf[:, h, :], scalar=egl[:, h:h + 1], in1=ps_[:, 0:128], op0=ALU.mult, op1=ALU.add),
                      r=[ks_, ("Sf", h), "egl"], w=[("Sf", h)])
            S.dma(lambda e, oc_=oc_, t0=t0, nv=nv: e.dma_start(out=self.oT[:, :, t0:t0 + nv].rearrange("c p t -> p c t"), in_=oc_[:, :, 0:nv]),
                  r=[("och", cb, h) for h in range(NH)], w=[("oT", t0)])
        S.dma(lambda e: e.dma_start(out=O["gdn_S_prompt"][j].rearrange("h k v -> k h v"), in_=Sf[:]), r=[("Sf", h) for h in range(NH)], w=["o_gsp"])

        if int(os.environ.get('GDN_STOP', '9')) <= 2:
            return
        self.phase()
        qsf = self.sb("qsf", [128, 24, NS], F32)
        gbs = self.sb("gbs", [128, NS * 16], F32)
        egs = self.sb("egs", [128, NS * 16], F32)
        S0 = [self.sb("S0%d" % i, [128, 128], F32) for i in range(2)]
        S1 = [self.sb("S1%d" % i, [128, 128], F32) for i in range(2)]
        dcol = [self.sb("dcol%d" % i, [128, 1], F32) for i in range(2)]
        krow = [self.sb("krow%d" % i, [1, 128], F32) for i in range(2)]
        drow = [self.sb("drow%d" % i, [1, 128], F32) for i in range(2)]
        osm = self.sb("osm", [128, NH, NS], F32)
        S.dma(lambda e: e.dma_start(out=qsf[:], in_=self.qkvF[:, :, T:T + NS].rearrange("c p t -> p c t")), w=["qsf"])
        S.dma(lambda e: e.dma_start(out=gbs[:], in_=self.gb[T:T + NS, :].rearrange("(o s) c -> o (s c)", o=1).to_broadcast([128, NS * 16])), w=["gbs"])
        S.act(lambda e: e.activation(out=egs[:], in_=gbs[:], func=AF.Exp), r=["gbs"], w=["egs"])
        it = 0
        for s_ in range(NS):
            for h in range(NH):
                b = it % 2
                it += 1
                s0, s1 = S0[b], S1[b]
                S.dma(lambda e, s0=s0, s_=s_, h=h: e.dma_start(out=s0[:], in_=I["state_gdn_S"][j, s_, h]), w=[("S0", b)])
                egc_ = egs[:, s_ * 16 + 8 + h:s_ * 16 + 9 + h]
                bec_ = gbs[:, s_ * 16 + h:s_ * 16 + h + 1]
                S.dve(lambda e, s0=s0, egc_=egc_: e.tensor_scalar(out=s0[:], in0=s0[:], scalar1=egc_, scalar2=None, op0=ALU.mult), r=[("S0", b), "egs"], w=[("S0", b)])
                kcol = qsf[:, 8 + h, s_:s_ + 1]
                vcol = qsf[:, 16 + h, s_:s_ + 1]
                qcol = qsf[:, h, s_:s_ + 1]
                pa, ka = self.bank()
                S.pe(lambda e, s0=s0, kcol=kcol, pa=pa: e.matmul(pa[:, 0:1], lhsT=s0[:], rhs=kcol, start=True, stop=True), r=[("S0", b), "qsf"], w=[ka])
                S.dve(lambda e, b=b, vcol=vcol, pa=pa: e.tensor_tensor(out=dcol[b][:], in0=vcol, in1=pa[:, 0:1], op=ALU.subtract), r=[ka, "qsf"], w=[("dcol", b)])
                S.dve(lambda e, b=b, bec_=bec_: e.tensor_tensor(out=dcol[b][:], in0=dcol[b][:], in1=bec_, op=ALU.mult), r=[("dcol", b), "gbs"], w=[("dcol", b)])
                pr1, kr1 = self.bank()
                pr2, kr2 = self.bank()
                S.pe(lambda e, kcol=kcol, pr1=pr1: e.matmul(pr1[0:1, 0:128], lhsT=kcol, rhs=self.ident[:], start=True, stop=True), r=["qsf", "ident"], w=[kr1])
                S.pe(lambda e, b=b, pr2=pr2: e.matmul(pr2[0:1, 0:128], lhsT=dcol[b][:], rhs=self.ident[:], start=True, stop=True), r=[("dcol", b), "ident"], w=[kr2])
                S.act(lambda e, b=b, pr1=pr1: e.copy(out=krow[b][:], in_=pr1[0:1, 0:128]), r=[kr1], w=[("krow", b)])
                S.act(lambda e, b=b, pr2=pr2: e.copy(out=drow[b][:], in_=pr2[0:1, 0:128]), r=[kr2], w=[("drow", b)])
                pu, ku = self.bank()
                S.pe(lambda e, b=b, pu=pu: e.matmul(pu[:, 0:128], lhsT=krow[b][:], rhs=drow[b][:], start=True, stop=True), r=[("krow", b), ("drow", b)], w=[ku])
                S.dve(lambda e, s0=s0, s1=s1, pu=pu: e.tensor_tensor(out=s1[:], in0=s0[:], in1=pu[:, 0:128], op=ALU.add), r=[ku, ("S0", b)], w=[("S1", b)])
                S.dma(lambda e, s1=s1, s_=s_, h=h: e.dma_start(out=O["gdn_S_sample"][j, s_, h], in_=s1[:]), r=[("S1", b)], w=[("o_gss", s_, h)])
                po, ko = self.bank()
                S.pe(lambda e, s1=s1, qcol=qcol, po=po: e.matmul(po[:, 0:1], lhsT=s1[:], rhs=qcol, start=True, stop=True), r=[("S1", b), "qsf"], w=[ko])
                S.act(lambda e, h=h, s_=s_, po=po: e.copy(out=osm[:, h, s_:s_ + 1], in_=po[:, 0:1]), r=[ko], w=[("osm", h, s_)])
        S.dma(lambda e: e.dma_start(out=self.oT[:, :, T:T + NS].rearrange("c p t -> p c t"), in_=osm[:]),
              r=[("osm", h, s_) for h in range(NH) for s_ in range(NS)], w=["oTs"])

        if int(os.environ.get('GDN_STOP', '9')) <= 3:
            return
        self.phase()
        Wo = self.sb("Wo", [128, 8, D], BF16)
        self.load_w(Wo, I["gdn_w_out"][j], 8, "Wo")
        wok = [("Wo", k) for k in range(8)]
        nw = self.sb("nw", [128, 1], F32)
        S.dma(lambda e: e.dma_start(out=nw[:], in_=I["gdn_norm_w"][j].rearrange("(p o) -> p o", o=1)), w=["nw"])
        szl = self.sb("szl", [128, 8, 512], F32)
        sq = self.sb("gsq", [128, 8, 512], BF16)
        rv = [self.sb("grv%d" % i, [128, 512], F32) for i in range(2)]

        def gate(o, okeys, c0, N):
            S.dma(lambda e: e.dma_start(out=szl[:, :, 0:N], in_=self.szT[:, :, c0:c0 + N].rearrange("c p t -> p c t")), w=["szl"])
            S.pool(lambda e: e.tensor_tensor(out=sq[:, :, 0:N], in0=o[:, :, 0:N], in1=o[:, :, 0:N], op=ALU.mult), r=okeys, w=["gsq"])
            for c in range(8):
                pm, km = self.bank()
                rr = rv[c % 2]
                S.pe(lambda e, c=c, pm=pm: e.matmul(pm[:, 0:N], lhsT=self.ones_bf[:], rhs=sq[:, c, 0:N], start=True, stop=True), r=["gsq", "ones_bf"], w=[km])
                S.act(lambda e, pm=pm, rr=rr: e.activation(out=rr[:, 0:N], in_=pm[:, 0:N], func=AF.Ln, scale=1.0 / 128, bias=float(RMS_EPS)), r=[km], w=[("grv", c % 2)])
                S.act(lambda e, rr=rr: e.activation(out=rr[:, 0:N], in_=rr[:, 0:N], func=AF.Exp, scale=-0.5), r=[("grv", c % 2)], w=[("grv", c % 2)])
                S.dve(lambda e, c=c, rr=rr: e.scalar_tensor_tensor(out=o[:, c, 0:N], in0=o[:, c, 0:N], scalar=nw[:, 0:1], in1=rr[:, 0:N], op0=ALU.mult, op1=ALU.mult),
                      r=list(okeys) + ["gsq", ("grv", c % 2), "nw"], w=[("ogc", c)])
                S.pool(lambda e, c=c: e.tensor_tensor(out=o[:, c, 0:N], in0=o[:, c, 0:N], in1=szl[:, c, 0:N], op=ALU.mult), r=[("ogc", c), "szl"], w=[("ogc", c)])
            S.pool(lambda e: e.engine_nop() if False else e.memset(rv[0][:, 0:1], 0.0), r=[("ogc", c) for c in range(8)] + [("grv", 0)], w=["og", ("grv", 0)])
        self.out_proj_phase(Wo, wok, li * 3 + 1, gate=gate, samples_tm=False)

    def gather_page(self, dst, cache, idxv, col, key):
        rows = cache.rearrange("n p d -> (n p) d")
        self.S.dma(lambda e: e.indirect_dma_start(out=dst[:], out_offset=None, in_=rows,
                                                  in_offset=bass.IndirectOffsetOnAxis(ap=idxv[:, col:col + 1], axis=0)),
                   r=["idxv"], w=[key], q="pool")

    def sbattn(self, li):
        cfg, S, I, O = self.cfg, self.S, self.I, self.O
        T, TT = cfg.T, cfg.TT
        scale = 128.0 ** -0.5
        self.regs = {}
        self.phase()
        Wq = self.sb("Wq", [128, 8, 3 * D], BF16)
        self.load_w(Wq, I["sb_w_qkv"], 8, "Wq")
        wk = [("Wq", k) for k in range(8)]
        x = self.sb("x", [128, 8, 512], F32)
        xb = self.sb("xb", [128, 8, 512], BF16)
        qk = self.sb("qk", [128, 16, 512], BF16)
        tok = [self.sb("tok%d" % i, [128, D], F32) for i in range(3)]
        vbf = self.sb("vbf", [128, D], BF16)
        for (c0, N) in self.tiles():
            self.load_x(x, xb, c0, N)
            for oc in range(16):
                pb, pk = self.bank()
                for kc in range(8):
                    S.pe(lambda e, oc=oc, kc=kc, pb=pb, N=N: e.matmul(pb[:, 0:N], lhsT=Wq[:, kc, oc * 128:(oc + 1) * 128], rhs=xb[:, kc, 0:N],
                                                                       start=(kc == 0), stop=(kc == 7)), r=["xb"] + wk, w=[pk])
                if oc % 2:
                    S.act(lambda e, oc=oc, pb=pb, N=N: e.copy(out=qk[:, oc, 0:N], in_=pb[:, 0:N]), r=[pk], w=[("qk", oc)])
                else:
                    S.dve(lambda e, oc=oc, pb=pb, N=N: e.tensor_copy(out=qk[:, oc, 0:N], in_=pb[:, 0:N]), r=[pk], w=[("qk", oc)])
            S.dma(lambda e, c0=c0, N=N: e.dma_start(out=self.qkvT[0:16, :, c0:c0 + N].rearrange("c p t -> p c t"), in_=qk[:, :, 0:N]),
                  r=[("qk", oc) for oc in range(16)], w=[("qkvT", c0)])
            for j0 in range(0, N, 128):
                n = min(128, N - j0)
                g0 = c0 + j0
                npr = max(0, min(n, T - g0))
                has_s = g0 + n > T
                for which in ((1, 2, 0) if has_s else (1, 2)):
                    tk = tok[which]
                    for half in range(2):
                        pb, pk = self.bank()
                        for kc in range(8):
                            S.pe(lambda e, which=which, half=half, kc=kc, pb=pb, j0=j0, n=n: e.matmul(
                                pb[0:n, 0:512], lhsT=xb[:, kc, j0:j0 + n], rhs=Wq[:, kc, which * D + half * 512:which * D + (half + 1) * 512],
                                start=(kc == 0), stop=(kc == 7)), r=["xb"] + wk, w=[pk])
                        if half:
                            S.act(lambda e, tk=tk, pb=pb, n=n: e.copy(out=tk[0:n, 512:1024], in_=pb[0:n, 0:512]), r=[pk], w=[("tok", which, 1)])
                        else:
                            S.dve(lambda e, tk=tk, pb=pb, n=n: e.tensor_copy(out=tk[0:n, 0:512], in_=pb[0:n, 0:512]), r=[pk], w=[("tok", which, 0)])
                tkk = lambda w_: [("tok", w_, 0), ("tok", w_, 1)]
                if npr > 0:
                    S.dma(lambda e, g0=g0, npr=npr: e.dma_start(out=O["sb_k_prompt"][g0:g0 + npr, :], in_=tok[1][0:npr, :]), r=tkk(1), w=[("okp", g0)])
                    S.dma(lambda e, g0=g0, npr=npr: e.dma_start(out=O["sb_v_prompt"][g0:g0 + npr, :], in_=tok[2][0:npr, :]), r=tkk(2), w=[("ovp", g0)])
                    S.pool(lambda e, npr=npr: e.tensor_copy(out=vbf[0:npr, :], in_=tok[2][0:npr, :]), r=tkk(2), w=["vbf"])
                    S.dma(lambda e, g0=g0, npr=npr: e.dma_start(out=self.vtok[g0:g0 + npr, :], in_=vbf[0:npr, :]), r=["vbf"], w=[("vtok", g0)])
                if has_s:
                    S.dma(lambda e, npr=npr: e.dma_start(out=O["sb_k_sample"], in_=tok[1][npr:npr + NS, :]), r=tkk(1), w=["oks"])
                    S.dma(lambda e, npr=npr: e.dma_start(out=O["sb_v_sample"], in_=tok[2][npr:npr + NS, :]), r=tkk(2), w=["ovs"])
                    S.dma(lambda e, npr=npr: e.dma_start(out=self.qs, in_=tok[0][npr:npr + NS, :]), r=tkk(0), w=["qs"])
        self.phase()
        nch = cfg.nchunk
        biasb = self.sb("biasb", [128, NH], F32)
        S.dma(lambda e: e.dma_start(out=biasb[:], in_=I["sb_logit_bias"].to_broadcast([128, NH])), w=["biasb"])
        sbmask = self.sb("sbmask", [128, 4, 512], F32)
        for d in range(4):
            S.pool(lambda e, d=d: e.affine_select(out=sbmask[:, d, :], in_=self.ones_f[:], pattern=[[1, 512]], compare_op=ALU.is_gt,
                                                  fill=0.0, base=-128 * d, channel_multiplier=-1), r=[], w=[("sbmask", d)])
        mkeys = [("sbmask", d) for d in range(4)]
        qh = [self.sb("qh%d" % i, [128, cfg.TP], BF16) for i in range(2)]
        kh = [self.sb("kh%d" % i, [128, cfg.TP], BF16) for i in range(2)]
        vh = [self.sb("vh%d" % i, [128, nch, 128], BF16) for i in range(2)]
        ez = [self.sb("ez%d" % i, [128, 512], F32) for i in range(2)]
        spf = self.sb("spf", [128, 512], F32)
        spall = [self.sb("spall%d" % i, [128, nch, 512], BF16) for i in range(2)]
        tmp = [self.sb("tmp%d" % i, [128, 512], F32) for i in range(2)]
        wf = self.sb("wf", [128, 512], F32)
        wb = [self.sb("wb%d" % i, [128, 512], BF16) for i in range(2)]
        racc = self.sb("racc", [128, 512], F32)
        ot = [self.sb("ot%d" % i, [128, 512], F32) for i in range(2)]
        nvt = T // 128
        rem = T - nvt * 128
        it = 0
        qn = 0
        for h in range(NH):
            hb = h % 2
            S.dma(lambda e, h=h, hb=hb: e.dma_start(out=qh[hb][:, 0:T], in_=self.qkvT[h, :, 0:T]), w=[("qh", hb)])
            S.dma(lambda e, h=h, hb=hb: e.dma_start(out=kh[hb][:, 0:T], in_=self.qkvT[8 + h, :, 0:T]), w=[("kh", hb)])
            if nvt:
                S.dma(lambda e, h=h, hb=hb: e.dma_start(out=vh[hb][:, 0:nvt, :], in_=self.vtok[0:nvt * 128, h * 128:(h + 1) * 128].rearrange("(t p) d -> p t d", p=128)),
                      w=[("vh", hb, 0)])
            if rem:
                S.dma(lambda e, h=h, hb=hb: e.dma_start(out=vh[hb][0:rem, nvt, :], in_=self.vtok[nvt * 128:T, h * 128:(h + 1) * 128]), w=[("vh", hb, 1)])
            vkeys = [("vh", hb, 0), ("vh", hb, 1)]
            bcol = biasb[:, h:h + 1]
            for c0 in range(0, T, 512):
                Nq = min(512, T - c0)
                Q = c0 // 512
                jmax = min(nch - 1, (c0 + Nq - 2) // 128) if (c0 + Nq - 2) >= 0 else -1
                po, ko = self.ps[6 + qn % 2], ("psacc", qn % 2)
                spA = spall[qn % 2]
                for jt in range(jmax, -1, -1):
                    nk = min(128, T - jt * 128)
                    diag = (jt * 128 + nk - 1) >= c0
                    d = jt - 4 * Q
                    b = it % 2
                    it += 1
                    pz, kz = self.bank()
                    S.pe(lambda e, hb=hb, jt=jt, nk=nk, c0=c0, Nq=Nq, pz=pz: e.matmul(pz[0:nk, 0:Nq], lhsT=kh[hb][:, jt * 128:jt * 128 + nk],
                                                                                   rhs=qh[hb][:, c0:c0 + Nq], start=True, stop=True),
                         r=[("qh", hb), ("kh", hb)], w=[kz])
                    S.act(lambda e, b=b, nk=nk, Nq=Nq, pz=pz, bcol=bcol: e.activation(out=ez[b][0:nk, 0:Nq], in_=pz[0:nk, 0:Nq], func=AF.Exp,
                                                                                   scale=scale, bias=bcol[0:nk, :]), r=[kz, "biasb"], w=[("ez", b)])
                    if diag:
                        S.act(lambda e, b=b, nk=nk, Nq=Nq: e.activation(out=spf[0:nk, 0:Nq], in_=ez[b][0:nk, 0:Nq], func=AF.Ln, bias=1.0, scale=1.0),
                              r=[("ez", b)], w=["spf"])
                        S.pool(lambda e, spA=spA, jt=jt, nk=nk, Nq=Nq, d=d: e.tensor_tensor(out=spA[0:nk, jt, 0:Nq], in0=spf[0:nk, 0:Nq], in1=sbmask[0:nk, d, 0:Nq], op=ALU.mult),
                               r=["spf"] + mkeys, w=[("spA", qn % 2, jt)])
                    else:
                        S.act(lambda e, spA=spA, b=b, jt=jt, nk=nk, Nq=Nq: e.activation(out=spA[0:nk, jt, 0:Nq], in_=ez[b][0:nk, 0:Nq], func=AF.Ln, bias=1.0, scale=1.0),
                              r=[("ez", b)], w=[("spA", qn % 2, jt)])
                S.pool(lambda e, Nq=Nq: e.memset(racc[:, 0:Nq], 0.0), w=["racc"])
                for jt in range(jmax, -1, -1):
                    nk = min(128, T - jt * 128)
                    diag = (jt * 128 + nk - 1) >= c0
                    d = jt - 4 * Q
                    b = it % 2
                    it += 1
                    sk = ("spA", qn % 2, jt)
                    psuf, ks = self.bank()
                    ptot, kt = self.bank()
                    pz, kz = self.bank()
                    S.pe(lambda e, spA=spA, jt=jt, nk=nk, Nq=Nq, psuf=psuf: e.matmul(psuf[0:nk, 0:Nq], lhsT=self.ML_bf[0:nk, 0:nk], rhs=spA[0:nk, jt, 0:Nq], start=True, stop=True),
                         r=[sk, "ML_bf"], w=[ks])
                    S.pe(lambda e, spA=spA, jt=jt, nk=nk, Nq=Nq, ptot=ptot: e.matmul(ptot[:, 0:Nq], lhsT=self.ones_bf[0:nk, :], rhs=spA[0:nk, jt, 0:Nq], start=True, stop=True),
                         r=[sk, "ones_bf"], w=[kt])
                    S.pe(lambda e, hb=hb, jt=jt, nk=nk, c0=c0, Nq=Nq, pz=pz: e.matmul(pz[0:nk, 0:Nq], lhsT=kh[hb][:, jt * 128:jt * 128 + nk],
                                                                                   rhs=qh[hb][:, c0:c0 + Nq], start=True, stop=True),
                         r=[("qh", hb), ("kh", hb)], w=[kz])
                    S.dve(lambda e, b=b, nk=nk, Nq=Nq, psuf=psuf: e.tensor_tensor(out=tmp[b][0:nk, 0:Nq], in0=psuf[0:nk, 0:Nq], in1=racc[0:nk, 0:Nq], op=ALU.add),
                          r=[ks, "racc"], w=[("tmp", b)])
                    S.dve(lambda e, Nq=Nq, ptot=ptot: e.tensor_tensor(out=racc[:, 0:Nq], in0=racc[:, 0:Nq], in1=ptot[:, 0:Nq], op=ALU.add),
                          r=[kt, "racc"], w=["racc"])
                    S.dve(lambda e, b=b, nk=nk, Nq=Nq, pz=pz: e.scalar_tensor_tensor(out=tmp[b][0:nk, 0:Nq], in0=pz[0:nk, 0:Nq], scalar=scale, in1=tmp[b][0:nk, 0:Nq],
                                                                                  op0=ALU.mult, op1=ALU.subtract), r=[kz, ("tmp", b)], w=[("tmp", b)])
                    if diag:
                        S.act(lambda e, b=b, nk=nk, Nq=Nq, bcol=bcol: e.activation(out=wf[0:nk, 0:Nq], in_=tmp[b][0:nk, 0:Nq], func=AF.Exp, bias=bcol[0:nk, :], scale=1.0),
                              r=[("tmp", b), "biasb"], w=["wf"])
                        S.pool(lambda e, b=b, nk=nk, Nq=Nq, d=d: e.tensor_tensor(out=wb[b][0:nk, 0:Nq], in0=wf[0:nk, 0:Nq], in1=sbmask[0:nk, d, 0:Nq], op=ALU.mult),
                               r=["wf"] + mkeys, w=[("wb", b)])
                    else:
                        S.act(lambda e, b=b, nk=nk, Nq=Nq, bcol=bcol: e.activation(out=wb[b][0:nk, 0:Nq], in_=tmp[b][0:nk, 0:Nq], func=AF.Exp, bias=bcol[0:nk, :], scale=1.0),
                              r=[("tmp", b), "biasb"], w=[("wb", b)])
                    S.pe(lambda e, hb=hb, b=b, jt=jt, nk=nk, Nq=Nq, po=po, jmax=jmax: e.matmul(po[:, 0:Nq], lhsT=vh[hb][0:nk, jt, :], rhs=wb[b][0:nk, 0:Nq],
                                                                                           start=(jt == jmax), stop=(jt == 0)),
                         r=[("wb", b)] + vkeys, w=[ko])
                ob = ot[qn % 2]
                if jmax >= 0:
                    S.act(lambda e, ob=ob, po=po, Nq=Nq: e.copy(out=ob[:, 0:Nq], in_=po[:, 0:Nq]), r=[ko], w=[("ot", qn % 2)])
                else:
                    S.pool(lambda e, ob=ob, Nq=Nq: e.memset(ob[:, 0:Nq], 0.0), w=[("ot", qn % 2)])
                S.dma(lambda e, h=h, ob=ob, c0=c0, Nq=Nq: e.dma_start(out=self.oT[h, :, c0:c0 + Nq], in_=ob[:, 0:Nq]), r=[("ot", qn % 2)], w=[("oT", h, c0)])
                qn += 1
        self.phase()
        NP = cfg.NPAGES
        pt_sb = self.sb("pt_sb", [128, NS * NP], I32)
        ptf = self.sb("ptf", [128, NS * NP], F32)
        pidx = self.sb("pidx", [128, 1], I32)
        pidf = self.sb("pidf", [128, 1], F32)
        idxv = self.sb("idxv", [128, NS * NP], I32)
        S.dma(lambda e: e.dma_start(out=pt_sb[:], in_=I["page_table"].to_broadcast([128, NS * NP])), w=["pt_sb"])
        S.pool(lambda e: e.iota(out=pidx[:], pattern=[[0, 1]], base=0, channel_multiplier=1), w=["pidx"])
        S.dve(lambda e: e.tensor_copy(out=pidf[:], in_=pidx[:]), r=["pidx"], w=["pidf"])
        S.dve(lambda e: e.tensor_copy(out=ptf[:], in_=pt_sb[:]), r=["pt_sb"], w=["ptf"])
        S.dve(lambda e: e.tensor_scalar(out=ptf[:], in0=ptf[:], scalar1=float(PAGE), scalar2=pidf[:, 0:1], op0=ALU.mult, op1=ALU.add),
              r=["ptf", "pidf"], w=["ptf"])
        S.dve(lambda e: e.tensor_copy(out=idxv[:], in_=ptf[:]), r=["ptf"], w=["idxv"])
        biasr = self.sb("biasr", [128, NH], F32)
        S.dma(lambda e: e.dma_start(out=biasr[:], in_=I["sb_logit_bias"].to_broadcast([128, NH])), w=["biasr"])
        qb = self.sb("qb", [128, D], F32)
        kp = [self.sb("kp%d" % i, [128, D], F32) for i in range(2)]
        vp = [self.sb("vp%d" % i, [128, D], F32) for i in range(2)]
        kq = self.sb("kq", [128, D], F32)
        zt = [self.sb("zt%d" % i, [128, NH], F32) for i in range(2)]
        et = [self.sb("et%d" % i, [128, NH], F32) for i in range(2)]
        st_ = [self.sb("st%d" % i, [128, NH], F32) for i in range(2)]
        at = [self.sb("at%d" % i, [128, NH], F32) for i in range(2)]
        wt = [self.sb("wt%d" % i, [128, NH], F32) for i in range(2)]
        rs = self.sb("rs", [128, NH], F32)
        osb = self.sb("osb", [NH, D], F32)
        ML_f = self.sb("ML_f", [128, 128], F32)
        S.pool(lambda e: e.affine_select(out=ML_f[:], in_=self.ones_f[:, 0:128], pattern=[[-1, 128]], compare_op=ALU.is_ge, fill=0.0,
                                         base=0, channel_multiplier=1), w=["ML_f"])
        it = 0
        for s_ in range(NS):
            S.dma(lambda e, s_=s_: e.dma_start(out=qb[:], in_=self.qs[s_:s_ + 1, :].to_broadcast([128, D])), w=["qb"])
            S.pool(lambda e: e.memset(rs[:], 0.0), w=["rs"])
            poa, koa = self.ps[6], ("psacc", 0)
            pob, kob = self.ps[7], ("psacc", 1)
            for pg in range(NP - 1, -1, -1):
                b = it % 2
                it += 1
                self.gather_page(kp[b], I["cache_k"], idxv, s_ * NP + pg, ("kp", b))
                self.gather_page(vp[b], I["cache_v"], idxv, s_ * NP + pg, ("vp", b))
                S.dve(lambda e, b=b: e.tensor_tensor(out=kq[:], in0=kp[b][:], in1=qb[:], op=ALU.mult), r=[("kp", b), "qb"], w=["kq"])
                S.dve(lambda e, b=b: e.tensor_reduce(out=zt[b][:], in_=kq[:].rearrange("p (h d) -> p h d", h=NH), axis=AX.X, op=ALU.add),
                      r=["kq"], w=[("zt", b)])
                S.dve(lambda e, b=b: e.scalar_tensor_tensor(out=zt[b][:], in0=zt[b][:], scalar=scale, in1=biasr[:], op0=ALU.mult, op1=ALU.add),
                      r=[("zt", b), "biasr"], w=[("zt", b)])
                S.act(lambda e, b=b: e.activation(out=et[b][:], in_=zt[b][:], func=AF.Exp), r=[("zt", b)], w=[("et", b)])
                S.act(lambda e, b=b: e.activation(out=st_[b][:], in_=et[b][:], func=AF.Ln, bias=1.0, scale=1.0), r=[("et", b)], w=[("st", b)])
                psuf, ks = self.bank()
                ptot, kt = self.bank()
                S.pe(lambda e, b=b, psuf=psuf: e.matmul(psuf[:, 0:NH], lhsT=ML_f[:], rhs=st_[b][:], start=True, stop=True), r=[("st", b), "ML_f"], w=[ks])
                S.pe(lambda e, b=b, ptot=ptot: e.matmul(ptot[:, 0:NH], lhsT=self.ones_f[:, 0:128], rhs=st_[b][:], start=True, stop=True), r=[("st", b), "ones_f"], w=[kt])
                S.dve(lambda e, b=b, psuf=psuf: e.tensor_tensor(out=at[b][:], in0=psuf[:, 0:NH], in1=rs[:], op=ALU.add), r=[ks, "rs"], w=[("at", b)])
                S.dve(lambda e, b=b: e.tensor_tensor(out=at[b][:], in0=zt[b][:], in1=at[b][:], op=ALU.subtract), r=[("zt", b), ("at", b)], w=[("at", b)])
                S.act(lambda e, b=b: e.activation(out=wt[b][:], in_=at[b][:], func=AF.Exp), r=[("at", b)], w=[("wt", b)])
                S.dve(lambda e, ptot=ptot: e.tensor_tensor(out=rs[:], in0=rs[:], in1=ptot[:, 0:NH], op=ALU.add), r=[kt, "rs"], w=["rs"])
                S.pe(lambda e, b=b, pg=pg: e.matmul(poa[0:NH, 0:512], lhsT=wt[b][:], rhs=vp[b][:, 0:512], start=(pg == NP - 1), stop=(pg == 0)),
                     r=[("wt", b), ("vp", b)], w=[koa])
                S.pe(lambda e, b=b, pg=pg: e.matmul(pob[0:NH, 0:512], lhsT=wt[b][:], rhs=vp[b][:, 512:1024], start=(pg == NP - 1), stop=(pg == 0)),
                     r=[("wt", b), ("vp", b)], w=[kob])
            S.act(lambda e: e.copy(out=osb[:, 0:512], in_=poa[0:NH, 0:512]), r=[koa], w=[("osb", 0)])
            S.act(lambda e: e.copy(out=osb[:, 512:1024], in_=pob[0:NH, 0:512]), r=[kob], w=[("osb", 1)])
            for h in range(NH):
                S.dma(lambda e, s_=s_, h=h: e.dma_start(out=self.qs[s_:s_ + 1, h * 128:(h + 1) * 128], in_=osb[h:h + 1, h * 128:(h + 1) * 128]),
                      r=[("osb", 0), ("osb", 1), "qb"], w=[("qs", s_, h)])
        self.phase()
        Wo = self.sb("Wo", [128, 8, D], BF16)
        self.load_w(Wo, I["sb_w_out"], 8, "Wo")
        wok = [("Wo", k) for k in range(8)]
        self.out_proj_phase(Wo, wok, li * 3 + 1)

    def out_proj_phase(self, Wo, wok, lrow, gate=None, samples_tm=True):
        cfg, S = self.cfg, self.S
        T = cfg.T
        x = self.sb("x", [128, 8, 512], F32)
        o = self.sb("o", [128, 8, 512], F32)
        ob = self.sb("ob", [128, 8, 512], BF16)
        t2 = [self.sb("t2%d" % i, [128, 512], F32) for i in range(2)]
        rb, mean, msq, rstd = self.ln_bufs()
        ost = self.sb("ost", [NS, D], F32)
        for (c0, N) in self.tiles():
            S.dma(lambda e, c0=c0, N=N: e.dma_start(out=x[:, :, 0:N], in_=self.hT[:, :, c0:c0 + N].rearrange("c p t -> p c t")), w=["x"])
            npc = min(N, T - c0) if samples_tm else N
            S.dma(lambda e, c0=c0, npc=npc: e.dma_start(out=o[:, :, 0:npc], in_=self.oT[:, :, c0:c0 + npc].rearrange("c p t -> p c t")), w=["o"])
            okeys = ["o"]
            if samples_tm and c0 + N > T:
                S.dma(lambda e: e.dma_start(out=ost[:], in_=self.qs), w=["ost"])
                okeys += self.tm_to_fm(ost, NS, lambda c, npc=npc: o[:, c, npc:npc + NS], 8, ["ost", "o"], "osmp")
            if gate is not None:
                gate(o, okeys, c0, N)
                okeys = ["og"]
            S.pool(lambda e, N=N: e.tensor_copy(out=ob[:, :, 0:N], in_=o[:, :, 0:N]), r=okeys, w=["ob"])
            self.resid_ln_store(x, t2, N, c0, lrow, Wo, wok, ob, ["ob"], rb, mean, msq, rstd)

    def fm_to_tm(self, srcfn, n, dst, nchunks, skeys, dkey):
        S = self.S
        for c in range(nchunks):
            pb, pk = self.bank()
            S.pe(lambda e, c=c, pb=pb: e.transpose(pb[0:n, 0:128], srcfn(c), self.ident[:]), r=list(skeys) + ["ident"], w=[pk])
            if c % 2:
                S.act(lambda e, c=c, pb=pb: e.copy(out=dst[0:n, c * 128:(c + 1) * 128], in_=pb[0:n, 0:128]), r=[pk], w=[(dkey, c)])
            else:
                S.dve(lambda e, c=c, pb=pb: e.tensor_copy(out=dst[0:n, c * 128:(c + 1) * 128], in_=pb[0:n, 0:128]), r=[pk], w=[(dkey, c)])
        return [(dkey, c) for c in range(nchunks)]

    def tm_to_fm(self, src, n, dstfn, nchunks, skeys, dkey):
        S = self.S
        for c in range(nchunks):
            pb, pk = self.bank()
            S.pe(lambda e, c=c, pb=pb: e.transpose(pb[:, 0:n], src[0:n, c * 128:(c + 1) * 128], self.ident[0:n, 0:n]),
                 r=list(skeys) + ["ident"], w=[pk])
            if c % 2:
                S.act(lambda e, c=c, pb=pb: e.copy(out=dstfn(c), in_=pb[:, 0:n]), r=[pk], w=[(dkey, c)])
            else:
                S.dve(lambda e, c=c, pb=pb: e.tensor_copy(out=dstfn(c), in_=pb[:, 0:n]), r=[pk], w=[(dkey, c)])
        return [(dkey, c) for c in range(nchunks)]

    def resid_ln_store(self, x, t2, N, c0, lrow, W, wkeys, rhs, rhskeys, rb, mean, msq, rstd, biasrow=None):
        S = self.S
        for oc in range(8):
            pd, kd = self.bank()
            for kc in range(8):
                S.pe(lambda e, oc=oc, kc=kc, pd=pd: e.matmul(pd[:, 0:N], lhsT=W[:, kc, oc * 128:(oc + 1) * 128], rhs=rhs[:, kc, 0:N],
                                                            start=(kc == 0), stop=(kc == 7)), r=list(rhskeys) + list(wkeys), w=[kd])
            tt = t2[oc % 2]
            if biasrow is None:
                S.act(lambda e, pd=pd, tt=tt: e.copy(out=tt[:, 0:N], in_=pd[:, 0:N]), r=[kd], w=[("t2", oc % 2)])
            else:
                S.act(lambda e, pd=pd, tt=tt, oc=oc: e.activation(out=tt[:, 0:N], in_=pd[:, 0:N], func=AF.Identity,
                                                                  bias=self.pcol(biasrow, oc), scale=1.0), r=[kd], w=[("t2", oc % 2)])
            S.dve(lambda e, oc=oc, tt=tt: e.scalar_tensor_tensor(out=x[:, oc, 0:N], in0=x[:, oc, 0:N], scalar=float(ALPHA),
                                                                 in1=tt[:, 0:N], op0=ALU.mult, op1=ALU.add),
                  r=["x", ("t2", oc % 2)], w=[("r", oc)])
        self.ln_inplace(x, N, lrow, rb, mean, msq, rstd)
        S.dma(lambda e: e.dma_start(out=self.hT[:, :, c0:c0 + N].rearrange("c p t -> p c t"), in_=x[:, :, 0:N]),
              r=[("r", c) for c in range(8)], w=["x"])

    def ln_bufs(self):
        return (self.sb("ln_rb", [128, 8, 512], BF16), self.sb("ln_mean", [128, 512], F32),
                self.sb("ln_msq", [128, 512], F32), self.sb("ln_rstd", [128, 512], F32))

    def conformer(self, li):
        cfg, S, I, O = self.cfg, self.S, self.I, self.O
        T, TT = cfg.T, cfg.TT
        uT = self.uT
        self.phase()
        W1 = self.sb("W1", [128, 8, 2 * D], BF16)
        self.load_w(W1, I["cv_w_pw1"], 8, "W1")
        w1k = [("W1", k) for k in range(8)]
        x = self.sb("x", [128, 8, 512], F32)
        xb = self.sb("xb", [128, 8, 512], BF16)
        ga = self.sb("ga", [128, 8, 512], F32)
        gs = [self.sb("gs%d" % i, [128, 512], F32) for i in range(2)]
        for (c0, N) in self.tiles():
            self.load_x(x, xb, c0, N)
            for oc in range(8):
                pa, ka = self.bank()
                pg, kg = self.bank()
                for kc in range(8):
                    S.pe(lambda e, oc=oc, kc=kc, pa=pa, N=N: e.matmul(pa[:, 0:N], lhsT=W1[:, kc, oc * 128:(oc + 1) * 128], rhs=xb[:, kc, 0:N],
                                                                       start=(kc == 0), stop=(kc == 7)), r=["xb"] + w1k, w=[ka])
                for kc in range(8):
                    S.pe(lambda e, oc=oc, kc=kc, pg=pg, N=N: e.matmul(pg[:, 0:N], lhsT=W1[:, kc, D + oc * 128:D + (oc + 1) * 128], rhs=xb[:, kc, 0:N],
                                                                       start=(kc == 0), stop=(kc == 7)), r=["xb"] + w1k, w=[kg])
                g = gs[oc % 2]
                S.act(lambda e, oc=oc, pg=pg, g=g, N=N: e.activation(out=g[:, 0:N], in_=pg[:, 0:N], func=AF.Sigmoid,
                                                                      bias=self.pcol(PR_BP1 + 1, oc), scale=1.0), r=[kg], w=[("gs", oc % 2)])
                S.dve(lambda e, oc=oc, pa=pa, g=g, N=N: e.scalar_tensor_tensor(out=ga[:, oc, 0:N], in0=pa[:, 0:N], scalar=self.pcol(PR_BP1, oc),
                                                                               in1=g[:, 0:N], op0=ALU.add, op1=ALU.mult),
                      r=[ka, ("gs", oc % 2)], w=[("ga", oc)])
            S.dma(lambda e, c0=c0, N=N: e.dma_start(out=uT[:, :, c0:c0 + N].rearrange("c p t -> p c t"), in_=ga[:, :, 0:N]),
                  r=[("ga", oc) for oc in range(8)], w=[("uT", c0)])
        self.phase()
        W2 = self.sb("W2", [128, 8, D], BF16)
        self.load_w(W2, I["cv_w_pw2"], 8, "W2")
        w2k = [("W2", k) for k in range(8)]
        H = CW - 1
        ub = self.sb("ub", [128, 8, H + 512], F32)
        ubb = self.sb("ubb", [128, 8, H + 512], BF16)
        dg = self.sb("dg", [128, 8 * CW, 128], BF16)
        for c in range(8):
            for i in range(CW):
                if (c * CW + i) % 2:
                    S.act(lambda e, c=c, i=i: e.activation(out=dg[:, c * CW + i, :], in_=self.ident[:], func=AF.Copy, scale=self.PT[:, c, PR_DWW + i:PR_DWW + i + 1]),
                          r=["ident"], w=[("dg", c)])
                else:
                    S.dve(lambda e, c=c, i=i: e.tensor_scalar(out=dg[:, c * CW + i, :], in0=self.ident[:], scalar1=self.PT[:, c, PR_DWW + i:PR_DWW + i + 1], scalar2=None, op0=ALU.mult),
                          r=["ident"], w=[("dg", c)])
        acc = self.sb("acc", [128, 8, 512], F32)
        db = self.sb("db", [128, 8, 512], BF16)
        x = self.sb("x", [128, 8, 512], F32)
        t2 = [self.sb("t2%d" % i, [128, 512], F32) for i in range(2)]
        rb, mean, msq, rstd = self.ln_bufs()
        cst = self.sb("cst", [NS * H, D], F32)
        ctxT = self.sb("ctxT", [128, 8, NS * H], F32)
        prod = self.sb("prod", [128, H], F32)
        red = self.sb("red", [128, 8, NS], F32)
        tmo = self.sb("tmo", [H, D], F32)
        tms = self.sb("tms", [NS, D], F32)
        S.dma(lambda e: e.dma_start(out=cst[:], in_=I["state_conv"].rearrange("s r d -> (s r) d")), w=["cst"])
        ckeys = self.tm_to_fm(cst, NS * H, lambda c: ctxT[:, c, :], 8, ["cst"], "ctxT")
        wcol = lambda c, i: self.PT[:, c, PR_DWW + i:PR_DWW + i + 1]
        for (c0, N) in self.tiles():
            if c0 == 0:
                S.pool(lambda e: e.memset(ub[:, :, 0:H], 0.0), w=["ub"])
                S.dma(lambda e, N=N: e.dma_start(out=ub[:, :, H:H + N], in_=uT[:, :, 0:N].rearrange("c p t -> p c t")), r=["ub"], w=["ub"])
            else:
                S.dma(lambda e, c0=c0, N=N: e.dma_start(out=ub[:, :, 0:H + N], in_=uT[:, :, c0 - H:c0 + N].rearrange("c p t -> p c t")), w=["ub"])
            S.dma(lambda e, c0=c0, N=N: e.dma_start(out=x[:, :, 0:N], in_=self.hT[:, :, c0:c0 + N].rearrange("c p t -> p c t")), w=["x"])
            S.pool(lambda e, N=N: e.tensor_copy(out=ubb[:, :, 0:H + N], in_=ub[:, :, 0:H + N]), r=["ub"], w=["ubb"])
            for c in range(8):
                pdw, kdw = self.bank()
                for i in range(CW):
                    S.pe(lambda e, c=c, i=i, N=N, pdw=pdw: e.matmul(pdw[:, 0:N], lhsT=dg[:, c * CW + i, :], rhs=ubb[:, c, i:i + N],
                                                                     start=(i == 0), stop=(i == CW - 1)), r=["ubb", ("dg", c)], w=[kdw])
                if c % 2:
                    S.act(lambda e, c=c, N=N, pdw=pdw: e.activation(out=acc[:, c, 0:N], in_=pdw[:, 0:N], func=AF.Identity, bias=self.pcol(PR_DWB, c), scale=1.0),
                          r=[kdw], w=[("r", c)])
                else:
                    S.dve(lambda e, c=c, N=N, pdw=pdw: e.tensor_scalar(out=acc[:, c, 0:N], in0=pdw[:, 0:N], scalar1=self.pcol(PR_DWB, c), scalar2=None, op0=ALU.add),
                          r=[kdw], w=[("r", c)])
            if c0 + N > T:
                sc = T - c0
                for c in range(8):
                    for s_ in range(NS):
                        S.dve(lambda e, c=c, s_=s_: e.tensor_tensor(out=prod[:], in0=ctxT[:, c, s_ * H:(s_ + 1) * H],
                                                                    in1=self.PT[:, c, PR_DWW:PR_DWW + H], op=ALU.mult),
                              r=ckeys, w=["prod"])
                        S.dve(lambda e, c=c, s_=s_: e.tensor_reduce(out=red[:, c, s_:s_ + 1], in_=prod[:], axis=AX.X, op=ALU.add),
                              r=["prod"], w=[("red", c)])
                    S.dve(lambda e, c=c: e.scalar_tensor_tensor(out=acc[:, c, sc:sc + NS], in0=ub[:, c, H + sc:H + sc + NS], scalar=wcol(c, H),
                                                                in1=red[:, c, :], op0=ALU.mult, op1=ALU.add),
                          r=["ub", ("red", c), ("r", c)], w=[("r", c)])
                    S.dve(lambda e, c=c: e.tensor_scalar(out=acc[:, c, sc:sc + NS], in0=acc[:, c, sc:sc + NS], scalar1=self.pcol(PR_DWB, c),
                                                         scalar2=None, op0=ALU.add), r=[("r", c)], w=[("r", c)])
            self.ln_inplace(acc, N, 0, rb, mean, msq, rstd, growfn=lambda c: self.pcol(PR_CLG, c), browfn=lambda c: self.pcol(PR_CLB, c))
            S.act(lambda e, N=N: e.activation(out=db[:, :, 0:N], in_=acc[:, :, 0:N], func=AF.Silu), r=[("r", c) for c in range(8)], w=["db"])
            self.resid_ln_store(x, t2, N, c0, li * 3 + 1, W2, w2k, db, ["db"], rb, mean, msq, rstd, biasrow=PR_BP2)
        ul = self.sb("ul", [128, 8, H + NS], F32)
        S.dma(lambda e: e.dma_start(out=ul[:, :, :], in_=uT[:, :, T - H:T + NS].rearrange("c p t -> p c t")), w=["ul"])
        k1 = self.fm_to_tm(lambda c: ul[:, c, 0:H], H, tmo, 8, ["ul"], "tmo")
        S.dma(lambda e: e.dma_start(out=O["conv_prompt"], in_=tmo[:]), r=k1, w=["o_cp"])
        k2 = self.fm_to_tm(lambda c: ul[:, c, H:H + NS], NS, tms, 8, ["ul"], "tms")
        S.dma(lambda e: e.dma_start(out=O["conv_sample"][:, H - 1, :], in_=tms[:]), r=k2, w=["o_cs"])
        for s_ in range(NS):
            S.dma(lambda e, s_=s_: e.dma_start(out=O["conv_sample"][s_, 0:H - 1, :], in_=I["state_conv"][s_, 1:H, :]), w=[("o_cs2", s_)])

    def final_out(self):
        cfg, S = self.cfg, self.S
        self.phase()
        dsts = []
        for j in range(cfg.SEQ // 128):
            dsts.append((self.O["y_prompt"], j * 128, 128, NMETA + j * 128))
        dsts.append((self.O["y_sample"], 0, NS, cfg.T))
        xi = [self.sb("fx%d" % i, [128, 8, 128], F32) for i in range(2)]
        yo = [self.sb("fy%d" % i, [128, D], F32) for i in range(2)]
        for n, (dst, r0, nr, c0) in enumerate(dsts):
            b = n % 2
            S.dma(lambda e, b=b, c0=c0, nr=nr: e.dma_start(out=xi[b][:, :, 0:nr],
                                                           in_=self.hT[:, :, c0:c0 + nr].rearrange("c p t -> p c t")),
                  w=[("fx", b)])
            for c in range(8):
                pb, pk = self.bank()
                S.pe(lambda e, b=b, c=c, nr=nr, pb=pb: e.transpose(pb[0:nr, 0:128], xi[b][:, c, 0:nr], self.ident[:]),
                     r=[("fx", b), "ident"], w=[pk])
                if c % 2:
                    S.act(lambda e, b=b, c=c, nr=nr, pb=pb: e.copy(out=yo[b][0:nr, c * 128:(c + 1) * 128], in_=pb[0:nr, 0:128]),
                          r=[pk], w=[("fy", b, c)])
                else:
                    S.dve(lambda e, b=b, c=c, nr=nr, pb=pb: e.tensor_copy(out=yo[b][0:nr, c * 128:(c + 1) * 128], in_=pb[0:nr, 0:128]),
                          r=[pk], w=[("fy", b, c)])
            S.dma(lambda e, b=b, dst=dst, r0=r0, nr=nr: e.dma_start(out=dst[r0:r0 + nr, :], in_=yo[b][0:nr, :]),
                  r=[("fy", b, c) for c in range(8)], w=[("yout", n)])


def make_pvec(inp):
    rows = [np.asarray(inp["ln_g"], np.float32).reshape(12, D), np.asarray(inp["ln_b"], np.float32).reshape(12, D),
            np.asarray(inp["cv_dw_w"], np.float32).reshape(CW, D), np.asarray(inp["cv_dw_b"], np.float32).reshape(1, D),
            np.asarray(inp["cv_ln_g"], np.float32).reshape(1, D), np.asarray(inp["cv_ln_b"], np.float32).reshape(1, D),
            np.asarray(inp["cv_b_pw2"], np.float32).reshape(1, D), np.asarray(inp["cv_b_pw1"], np.float32).reshape(2, D)]
    return np.ascontiguousarray(np.concatenate(rows, 0))


def make_in_maps(inp, cfg):
    f = lambda a: np.ascontiguousarray(np.asarray(a))
    pvec = make_pvec(inp)
    shared = {
        "meta_tokens": f(inp["meta_tokens"]), "pvec": pvec,
        "ffn_w_gate": f(inp["ffn_w_gate"]), "ffn_w_up": f(inp["ffn_w_up"]), "ffn_w_down": f(inp["ffn_w_down"]),
        "gdn_w_in": f(inp["gdn_w_in"]), "gdn_conv_w": f(inp["gdn_conv_w"]).reshape(8, GQKV),
        "gdn_a_log": f(inp["gdn_a_log"]), "gdn_dt_bias": f(inp["gdn_dt_bias"]), "gdn_norm_w": f(inp["gdn_norm_w"]),
        "gdn_w_out": f(inp["gdn_w_out"]), "sb_w_qkv": f(inp["sb_w_qkv"])[0], "sb_w_out": f(inp["sb_w_out"])[0],
        "sb_logit_bias": f(inp["sb_logit_bias"]), "cv_w_pw1": f(inp["cv_w_pw1"])[0], "cv_w_pw2": f(inp["cv_w_pw2"])[0],
        "cache_k": f(inp["cache_sb_k"])[0].reshape(cfg.NPHYS, PAGE, D),
        "cache_v": f(inp["cache_sb_v"])[0].reshape(cfg.NPHYS, PAGE, D),
    }
    maps = []
    for c in range(cfg.n_cores):
        b = c // 2
        s0 = c * NS
        m = dict(shared)
        m["x_prompt"] = f(inp["x_prompt"][b])
        m["x_sample"] = f(inp["x_sample"][s0:s0 + NS, 0])
        m["state_gdn_conv"] = f(inp["state_gdn_conv"][:, s0:s0 + NS]).reshape(2, NS * 3, GQKV)
        m["state_gdn_S"] = f(inp["state_gdn_S"][:, s0:s0 + NS])
        m["state_conv"] = f(inp["state_conv"][0, s0:s0 + NS])
        m["page_table"] = f(inp["page_table"][s0:s0 + NS]).reshape(1, NS * cfg.NPAGES).astype(np.int32)
        maps.append(m)
    return maps


def assemble(results, cfg):
    nb = cfg.n_cores // 2
    even = [results[2 * b] for b in range(nb)]
    allc = results
    y_prompt = np.stack([r["y_prompt"] for r in even])
    y_sample = np.concatenate([r["y_sample"] for r in allc])[:, None, :]
    gdn_S_prompt = np.stack([r["gdn_S_prompt"] for r in even], 1)
    gdn_S_sample = np.concatenate([r["gdn_S_sample"] for r in allc], 1)
    gdn_conv_prompt = np.stack([r["gdn_conv_prompt"] for r in even], 1)
    gdn_conv_sample = np.concatenate([r["gdn_conv_sample"] for r in allc], 1)
    sb_k_prompt = np.stack([r["sb_k_prompt"].reshape(cfg.T, NH, 128) for r in even])[None]
    sb_v_prompt = np.stack([r["sb_v_prompt"].reshape(cfg.T, NH, 128) for r in even])[None]
    sb_k_sample = np.concatenate([r["sb_k_sample"].reshape(NS, 1, NH, 128) for r in allc])[None]
    sb_v_sample = np.concatenate([r["sb_v_sample"].reshape(NS, 1, NH, 128) for r in allc])[None]
    conv_prompt = np.stack([r["conv_prompt"] for r in even])[None]
    conv_sample = np.concatenate([r["conv_sample"] for r in allc])[None]
    outs = (y_prompt, y_sample, gdn_S_prompt, gdn_S_sample, gdn_conv_prompt, gdn_conv_sample,
            sb_k_prompt, sb_v_prompt, sb_k_sample, sb_v_sample, conv_prompt, conv_sample)
    return tuple(np.ascontiguousarray(o, dtype=np.float32) for o in outs)


def run(inp, n_cores=8, stop_after=99, only=None):
    seq = inp["x_prompt"].shape[1]
    npages = inp["page_table"].shape[1]
    nphys = inp["cache_sb_k"].shape[1]
    cfg = Cfg(seq, npages, nphys, n_cores)
    bld = Builder(cfg, stop_after, only)
    nc = bld.build()
    maps = make_in_maps(inp, cfg)
    res = run_bass_kernel_spmd(nc, maps, core_ids=list(range(n_cores)))
    return assemble(res.results, cfg)


def kernel(**inputs):
    return run(inputs, 8)
```
